# Optimizing a Trainium2 kernel written in Bass

```python
import math
import jax, jax.numpy as jnp
from jax import lax
import numpy as np


D_MODEL = 1024
BATCH = 8
SEQ = 2048
DEPTH = 1
DEC_BATCH = 128
DEC_SEQ = 4
PAST_LEN = 16384
PAGE_SIZE = 128

D_SSM = D_MODEL
SSM_HEAD_DIM = 64
SSM_HEADS = D_SSM // SSM_HEAD_DIM
SSM_GROUPS = 2
SSM_STATE = 128
SSM_CONV = 4
SSM_CHUNK = 128
D_XBC = D_SSM + 2 * SSM_GROUPS * SSM_STATE
D_POOL = D_MODEL
POOL_WINDOWS = (2, 4, 8, 16)
POOL_GROUPS = len(POOL_WINDOWS)
POOL_GROUP_DIM = D_POOL // POOL_GROUPS
POOL_HIST = max(POOL_WINDOWS) - 1
D_MIX = D_SSM + D_POOL
D_IN_PROJ = D_SSM + D_XBC + SSM_HEADS + D_POOL
N_MEM = 256
MEM_HEADS = 4
MEM_HEAD_DIM = D_MODEL // MEM_HEADS
D_FF = 2816
FFN_CONV = 3
EPS = 1e-6

kernel_name = 'ssd_multiscale_pool_hybrid_decoder_step'


def rmsnorm(x, g):
    xf = x.astype(jnp.float32)
    y = xf * lax.rsqrt(jnp.mean(xf * xf, axis=-1, keepdims=True) + EPS)
    return (y * g.astype(jnp.float32)).astype(x.dtype)


def gated_group_rmsnorm(y, z, g):
    shp = y.shape
    t = (y * jax.nn.silu(z)).astype(jnp.float32).reshape(shp[:-1] + (SSM_GROUPS, shp[-1] // SSM_GROUPS))
    t = t * lax.rsqrt(jnp.mean(t * t, axis=-1, keepdims=True) + EPS)
    return (t.reshape(shp) * g.astype(jnp.float32)).astype(y.dtype)


def causal_dwconv(prev, x, w, b):
    width = w.shape[0]
    seqlen = x.shape[1]
    ext = jnp.concatenate([prev, x], axis=1)
    y = b + sum(ext[:, k:k + seqlen] * w[k] for k in range(width))
    return y, ext[:, ext.shape[1] - (width - 1):]


def ssd_chunked(x, dt, a, b_in, c_in, h0):
    bsz, seqlen, n_heads, hd = x.shape
    n_groups, n_state = b_in.shape[2], b_in.shape[3]
    rep = n_heads // n_groups
    q = min(SSM_CHUNK, seqlen)
    nc = -(-seqlen // q)
    pad = nc * q - seqlen
    f32 = jnp.float32

    def chunked(t):
        t = jnp.pad(t.astype(f32), [(0, 0), (0, pad)] + [(0, 0)] * (t.ndim - 2))
        return t.reshape((bsz, nc, q) + t.shape[2:])

    xdt = chunked(x * dt[..., None]).reshape(bsz, nc, q, n_groups, rep, hd)
    da = chunked(dt * a).reshape(bsz, nc, q, n_groups, rep)
    bc = chunked(b_in)
    cc = chunked(c_in)
    acs = jnp.cumsum(da, axis=2)
    seg = acs[:, :, :, None] - acs[:, :, None, :]
    causal = jnp.tril(jnp.ones((q, q), dtype=bool))[None, None, :, :, None, None]
    decay = jnp.where(causal, jnp.exp(jnp.where(causal, seg, 0.0)), 0.0)
    cb = jnp.einsum('bclgn,bcsgn->bclsg', cc, bc)
    y_diag = jnp.einsum('bclsg,bclsgr,bcsgrp->bclgrp', cb, decay, xdt)
    decay_end = jnp.exp(acs[:, :, -1:] - acs)
    chunk_states = jnp.einsum('bcsgn,bcsgr,bcsgrp->bcgrpn', bc, decay_end, xdt)
    chunk_decay = jnp.exp(acs[:, :, -1])

    def step(h, inp):
        s, dcy = inp
        return h * dcy[..., None, None] + s, h

    h_init = h0.astype(f32).reshape(bsz, n_groups, rep, hd, n_state)
    h_final, h_prev = lax.scan(step, h_init, (jnp.moveaxis(chunk_states, 1, 0), jnp.moveaxis(chunk_decay, 1, 0)))
    h_prev = jnp.moveaxis(h_prev, 0, 1)
    y_off = jnp.einsum('bclgn,bcgrpn,bclgr->bclgrp', cc, h_prev, jnp.exp(acs))
    y = (y_diag + y_off).reshape(bsz, nc * q, n_heads, hd)[:, :seqlen]
    return y.astype(x.dtype), h_final.reshape(bsz, n_heads, hd, n_state).astype(h0.dtype)


def multiscale_pool(prev, v, start):
    seqlen = v.shape[1]
    ext = jnp.concatenate([prev, v], axis=1)
    cs = jnp.pad(jnp.cumsum(ext.astype(jnp.float32), axis=1), [(0, 0), (1, 0), (0, 0)])
    pos = start + jnp.arange(seqlen)
    e = POOL_HIST + 1
    outs = []
    for gi, w in enumerate(POOL_WINDOWS):
        lo, hi = gi * POOL_GROUP_DIM, (gi + 1) * POOL_GROUP_DIM
        win = cs[:, e:e + seqlen, lo:hi] - cs[:, e - w:e - w + seqlen, lo:hi]
        cnt = jnp.minimum(pos + 1, w).astype(jnp.float32)[None, :, None]
        outs.append(win / cnt)
    pooled = jnp.concatenate(outs, axis=-1) - v.astype(jnp.float32)
    return pooled.astype(v.dtype), ext[:, ext.shape[1] - POOL_HIST:]


def memory_kv(mem, norm_memkv, w_mk, w_mv):
    bsz = mem.shape[0]
    hm = rmsnorm(mem, norm_memkv)
    k = (hm @ w_mk).reshape(bsz, N_MEM, MEM_HEADS, MEM_HEAD_DIM)
    v = (hm @ w_mv).reshape(bsz, N_MEM, MEM_HEADS, MEM_HEAD_DIM)
    return k, v


def hybrid_layer(x, mem_k, mem_v, ssm_h, conv_buf, pool_buf, ffn_buf, start,
                 norm_mix, w_in, ssm_conv_w, ssm_conv_b, ssm_dt_bias, ssm_a_log, ssm_d, ssm_norm,
                 w_pool, pool_scale, w_out, norm_mem, w_mq, w_mo,
                 norm_ffn, w_up, ffn_conv_w, ffn_conv_b, w_down):
    bsz, seqlen, _ = x.shape
    f32 = jnp.float32
    h = rmsnorm(x, norm_mix)
    proj = h @ w_in
    z, xbc, dt_raw, v_pool = jnp.split(proj, [D_SSM, D_SSM + D_XBC, D_SSM + D_XBC + SSM_HEADS], axis=-1)
    xbc, conv_new = causal_dwconv(conv_buf, xbc, ssm_conv_w, ssm_conv_b)
    xbc = jax.nn.silu(xbc)
    xs, b_ssm, c_ssm = jnp.split(xbc, [D_SSM, D_SSM + SSM_GROUPS * SSM_STATE], axis=-1)
    xs = xs.reshape(bsz, seqlen, SSM_HEADS, SSM_HEAD_DIM)
    dt = jax.nn.softplus((dt_raw + ssm_dt_bias).astype(f32))
    a = -jnp.exp(ssm_a_log.astype(f32))
    y, ssm_new = ssd_chunked(xs, dt, a,
                             b_ssm.reshape(bsz, seqlen, SSM_GROUPS, SSM_STATE),
                             c_ssm.reshape(bsz, seqlen, SSM_GROUPS, SSM_STATE), ssm_h)
    y = y + xs * ssm_d[:, None]
    y = gated_group_rmsnorm(y.reshape(bsz, seqlen, D_SSM), z, ssm_norm)
    pooled, pool_new = multiscale_pool(pool_buf, v_pool, start)
    pooled = jnp.einsum('blgc,gcd->blgd', pooled.reshape(bsz, seqlen, POOL_GROUPS, POOL_GROUP_DIM), w_pool)
    pooled = pooled.reshape(bsz, seqlen, D_POOL) * pool_scale
    x = x + jnp.concatenate([y, pooled], axis=-1) @ w_out
    h = rmsnorm(x, norm_mem)
    qm = (h @ w_mq).reshape(bsz, seqlen, MEM_HEADS, MEM_HEAD_DIM)
    s = jnp.einsum('blhd,bmhd->bhlm', qm, mem_k).astype(f32) * (MEM_HEAD_DIM ** -0.5)
    pr = jax.nn.softmax(s, axis=-1).astype(x.dtype)
    o = jnp.einsum('bhlm,bmhd->blhd', pr, mem_v).reshape(bsz, seqlen, D_MODEL)
    x = x + o @ w_mo
    h = rmsnorm(x, norm_ffn)
    u = h @ w_up
    u, ffn_new = causal_dwconv(ffn_buf, u, ffn_conv_w, ffn_conv_b)
    g, val = jnp.split(u, [D_FF], axis=-1)
    x = x + (jax.nn.silu(g) * val) @ w_down
    return x, ssm_new, conv_new, pool_new, ffn_new


def setup_inputs(seed: int = 0) -> dict:
    key = jax.random.key(seed)
    ks = iter(jax.random.split(key, 48))
    f32 = jnp.float32

    def nrm(shape, scale):
        return jax.random.normal(next(ks), shape, f32) * scale

    def gain(shape):
        return 1.0 + nrm(shape, 0.02)

    dt0 = jnp.exp(jax.random.uniform(next(ks), (DEPTH, SSM_HEADS), f32)
                  * (math.log(0.1) - math.log(0.001)) + math.log(0.001))
    dt_bias = dt0 + jnp.log(-jnp.expm1(-dt0))
    a_log = jnp.log(jax.random.uniform(next(ks), (DEPTH, SSM_HEADS), f32, 1.0, 16.0))
    return {
        'x_prompt': nrm((BATCH, SEQ, D_MODEL), 1.0),
        'x_sample': nrm((DEC_BATCH, DEC_SEQ, D_MODEL), 1.0),
        'mem_prompt': nrm((BATCH, N_MEM, D_MODEL), 1.0),
        'state_ssm': nrm((DEPTH, DEC_BATCH, SSM_HEADS, SSM_HEAD_DIM, SSM_STATE), 0.1),
        'state_ssm_conv': nrm((DEPTH, DEC_BATCH, SSM_CONV - 1, D_XBC), 1.0),
        'state_pool': nrm((DEPTH, DEC_BATCH, POOL_HIST, D_POOL), 1.0),
        'state_ffn_conv': nrm((DEPTH, DEC_BATCH, FFN_CONV - 1, 2 * D_FF), 1.0),
        'cache_mem_k': nrm((DEPTH, DEC_BATCH, N_MEM, MEM_HEADS, MEM_HEAD_DIM), 1.0),
        'cache_mem_v': nrm((DEPTH, DEC_BATCH, N_MEM, MEM_HEADS, MEM_HEAD_DIM), 1.0),
        'norm_mix': gain((DEPTH, D_MODEL)),
        'w_in': nrm((DEPTH, D_MODEL, D_IN_PROJ), D_MODEL ** -0.5),
        'ssm_conv_w': nrm((DEPTH, SSM_CONV, D_XBC), 0.5),
        'ssm_conv_b': nrm((DEPTH, D_XBC), 0.02),
        'ssm_dt_bias': dt_bias,
        'ssm_a_log': a_log,
        'ssm_d': 1.0 + nrm((DEPTH, SSM_HEADS), 0.1),
        'ssm_norm': gain((DEPTH, D_SSM)),
        'w_pool': nrm((DEPTH, POOL_GROUPS, POOL_GROUP_DIM, POOL_GROUP_DIM), POOL_GROUP_DIM ** -0.5),
        'pool_scale': 1.0 + nrm((DEPTH, D_POOL), 0.1),
        'w_out': nrm((DEPTH, D_MIX, D_MODEL), D_MIX ** -0.5),
        'norm_mem': gain((DEPTH, D_MODEL)),
        'norm_memkv': gain((DEPTH, D_MODEL)),
        'w_mq': nrm((DEPTH, D_MODEL, D_MODEL), D_MODEL ** -0.5),
        'w_mk': nrm((DEPTH, D_MODEL, D_MODEL), D_MODEL ** -0.5),
        'w_mv': nrm((DEPTH, D_MODEL, D_MODEL), D_MODEL ** -0.5),
        'w_mo': nrm((DEPTH, D_MODEL, D_MODEL), D_MODEL ** -0.5),
        'norm_ffn': gain((DEPTH, D_MODEL)),
        'w_up': nrm((DEPTH, D_MODEL, 2 * D_FF), D_MODEL ** -0.5),
        'ffn_conv_w': nrm((DEPTH, FFN_CONV, 2 * D_FF), 0.5),
        'ffn_conv_b': nrm((DEPTH, 2 * D_FF), 0.02),
        'w_down': nrm((DEPTH, D_FF, D_MODEL), D_FF ** -0.5),
        'final_norm': gain((D_MODEL,)),
    }


def reference(x_prompt, x_sample, mem_prompt, state_ssm, state_ssm_conv, state_pool, state_ffn_conv,
              cache_mem_k, cache_mem_v,
              norm_mix, w_in, ssm_conv_w, ssm_conv_b, ssm_dt_bias, ssm_a_log, ssm_d, ssm_norm,
              w_pool, pool_scale, w_out, norm_mem, norm_memkv, w_mq, w_mk, w_mv, w_mo,
              norm_ffn, w_up, ffn_conv_w, ffn_conv_b, w_down, final_norm):
    layer_params = [norm_mix, w_in, ssm_conv_w, ssm_conv_b, ssm_dt_bias, ssm_a_log, ssm_d, ssm_norm,
                    w_pool, pool_scale, w_out, norm_mem, w_mq, w_mo,
                    norm_ffn, w_up, ffn_conv_w, ffn_conv_b, w_down]
    dtp = x_prompt.dtype
    hp, hs = x_prompt, x_sample
    ssm_p, ssm_s, conv_p, conv_s, pool_p, pool_s, ffn_p, ffn_s, mk_p, mv_p = ([] for _ in range(10))
    for i in range(DEPTH):
        lp = [p[i] for p in layer_params]
        mem_k, mem_v = memory_kv(mem_prompt, norm_memkv[i], w_mk[i], w_mv[i])
        hp, s1, s2, s3, s4 = hybrid_layer(
            hp, mem_k, mem_v,
            jnp.zeros((BATCH, SSM_HEADS, SSM_HEAD_DIM, SSM_STATE), dtp),
            jnp.zeros((BATCH, SSM_CONV - 1, D_XBC), dtp),
            jnp.zeros((BATCH, POOL_HIST, D_POOL), dtp),
            jnp.zeros((BATCH, FFN_CONV - 1, 2 * D_FF), dtp),
            0, *lp)
        ssm_p.append(s1); conv_p.append(s2); pool_p.append(s3); ffn_p.append(s4)
        mk_p.append(mem_k); mv_p.append(mem_v)
        hs, t1, t2, t3, t4 = hybrid_layer(
            hs, cache_mem_k[i], cache_mem_v[i], state_ssm[i], state_ssm_conv[i], state_pool[i],
            state_ffn_conv[i], PAST_LEN, *lp)
        ssm_s.append(t1); conv_s.append(t2); pool_s.append(t3); ffn_s.append(t4)
    y_prompt = rmsnorm(hp, final_norm)
    y_sample = rmsnorm(hs, final_norm)
    return (y_prompt, y_sample,
            jnp.stack(ssm_p), jnp.stack(ssm_s),
            jnp.stack(conv_p), jnp.stack(conv_s),
            jnp.stack(pool_p), jnp.stack(pool_s),
            jnp.stack(ffn_p), jnp.stack(ffn_s),
            jnp.stack(mk_p), jnp.stack(mv_p))
```

```python
import numpy as np
import concourse.bass as bass
import concourse.mybir as mybir
from concourse.bass_utils import run_bass_kernel_spmd
from contextlib import ExitStack

F32, BF16 = mybir.dt.float32, mybir.dt.bfloat16
AF = mybir.ActivationFunctionType
ALU = mybir.AluOpType
AX = mybir.AxisListType

SAME_ENGINE_SYNC = True
CHECK_CLOBBER = False
D = 1024
T = 2048
NS = 64
DFF = 2816
EPS = 1e-6


class Prog:
    ENG = ('tensor', 'vector', 'scalar', 'gpsimd', 'sync')

    def __init__(self, nc, es):
        self.nc, self.es = nc, es
        self.ops = {e: [] for e in self.ENG}
        self.sem, self.cnt = {}, {}
        self.phase, self.free, self.retired, self.nsem, self.semcls = 0, {'sw': [], 'hw': []}, set(), 0, {}
        for e in self.ENG:
            self._mksem('E_' + e)
        self.seen = {e: {} for e in self.ENG}
        self.lastw, self.readers = {}, {}
        self.nbuf = 0

    def _mksem(self, name, q=None):
        if name.startswith('D_'):
            name = name + '@%d' % self.phase
        if name not in self.sem:
            cls = 'sw' if q == 'gpsimd' else 'hw'
            if name.startswith('D_'):
                self.semcls[name] = cls
            if name.startswith('D_') and self.free[cls]:
                h, c = self.free[cls].pop()
                self.sem[name] = h
                self.cnt[name] = c
            else:
                self.nsem += 1
                self.sem[name] = self.es.enter_context(self.nc.semaphore('s%d' % self.nsem))
                self.cnt[name] = 0
        return name

    def end_phase(self):
        self.barrier()
        for name in list(self.sem):
            if name.startswith('D_') and name not in self.retired:
                self.retired.add(name)
                self.free[self.semcls[name]].append((self.sem[name], self.cnt[name]))
        self.phase += 1

    def dma_multi(self, q, pairs, keys, sem, **kw):
        s = self._mksem('D_' + str(sem), q)
        for (o, i) in pairs:
            self.cnt[s] += 16
            self.ops[q].append(([], (lambda e, o=o, i=i: e.dma_start(out=o, in_=i, **kw)), (s, 16)))
        tok = (s, self.cnt[s])
        for k in keys:
            self.lastw[k] = tok
            self.readers[k] = []

    def sb(self, shape, dt, name=None, es=None):
        self.nbuf += 1
        return (es or self.es).enter_context(self.nc.sbuf_tensor(name or f"b{self.nbuf}", list(shape), dt))

    def op(self, eng, fn, reads=(), writes=(), dma=None):
        need = {}

        def want(tok, kind):
            if tok is None:
                return
            s, v = tok
            if s == 'E_' + eng:
                if eng == 'tensor' or not SAME_ENGINE_SYNC:
                    return
            if self.seen[eng].get(s, 0) >= v:
                return
            if need.get(s, 0) < v:
                need[s] = v
        for k in reads:
            want(self.lastw.get(k), 'raw')
            if isinstance(k, tuple) and k[0] == 'ps':
                for r in self.readers.get(k, ()):
                    want(r, 'war')
        for k in writes:
            if CHECK_CLOBBER and isinstance(k, tuple) and k[0] == 'ps' and k in self.lastw and not self.readers.get(k):
                import traceback
                print("CLOBBER? unread PSUM", k, [f.lineno for f in traceback.extract_stack()[-6:-1]])
            want(self.lastw.get(k), 'waw')
            for r in self.readers.get(k, ()):
                want(r, 'war')
        for s, v in need.items():
            self.seen[eng][s] = v
        if dma is not None:
            s = self._mksem('D_' + str(dma), eng)
            self.cnt[s] += 16
            tok = (s, self.cnt[s])
            inc = (s, 16)
        else:
            s = 'E_' + eng
            self.cnt[s] += 1
            tok = (s, self.cnt[s])
            inc = (s, 1)
        for k in writes:
            self.lastw[k] = tok
            self.readers[k] = []
        for k in reads:
            self.readers.setdefault(k, []).append(tok)
        self.ops[eng].append((list(need.items()), fn, inc))
        return tok

    def V(self, fn, reads=(), writes=()):
        return self.op('vector', fn, reads, writes)

    def A(self, fn, reads=(), writes=()):
        return self.op('scalar', fn, reads, writes)

    def G(self, fn, reads=(), writes=()):
        return self.op('gpsimd', fn, reads, writes)

    def PE(self, fn, reads=(), writes=()):
        return self.op('tensor', fn, reads, writes)

    def dma(self, q, out, in_, reads=(), writes=(), sem=None, **kw):
        return self.op(q, lambda e: e.dma_start(out=out, in_=in_, **kw), reads, writes, dma=sem)

    def barrier(self):
        allc = [(s_, c_) for s_, c_ in self.cnt.items() if c_ > 0]
        for e in self.ENG:
            w = [(s_, c_) for s_, c_ in allc if self.seen[e].get(s_, 0) < c_]
            for s_, c_ in w:
                self.seen[e][s_] = c_
            self.ops[e].append((w, None, None))

    def emit(self):
        fin = [(s, c) for s, c in self.cnt.items() if c > 0 and s != 'E_sync']
        self.ops['sync'].append((fin, None, None))
        with self.nc.Block() as block:
            for e in self.ENG:
                def body(eng, e=e):
                    for waits, fn, inc in self.ops[e]:
                        for s, v in waits:
                            eng.wait_ge(self.sem[s], v)
                        if fn is None:
                            continue
                        ins = fn(eng)
                        ins.then_inc(self.sem[inc[0]], inc[1])
                getattr(block, e)(body)


def bcast(ap, axis, n):
    u = ap.unsqueeze(axis)
    shp = list(u.shape)
    shp[axis] = n
    return u.broadcast_to(shp)


class K:
    pass


CST_LAYOUT = [('ident', 128), ('mhalf', 1), ('ones', 128), ('rsp', 256), ('rss', 64), ('csc', 64), ('blk', 16),
              ('negc', 128), ('negs', 128), ('e2', 2048), ('expd', 1024)]
CST_SMALL = 128 + 1 + 128 + 256 + 64 + 64 + 16
CST_OFF = {}
_o = 0
for _n, _w in CST_LAYOUT:
    CST_OFF[_n] = (_o, _w)
    _o += _w
CST_W = _o


def make_consts():
    c = np.zeros((128, CST_W), np.float32)

    def put(name, arr):
        o, w = CST_OFF[name]
        c[:arr.shape[0], o:o + arr.shape[1]] = arr
    put('ident', np.eye(128, dtype=np.float32))
    put('mhalf', np.full((128, 1), -0.5, np.float32))
    put('ones', np.ones((128, 128), np.float32))
    s_ = np.arange(128)[:, None]
    l_ = np.arange(128)[None, :]
    put('negc', np.where(l_ >= s_, 0.0, -30000.0).astype(np.float32))
    put('negs', np.where((l_ >= s_) & (l_ // 4 == s_ // 4), 0.0, -30000.0).astype(np.float32))
    e2 = np.zeros((128, 16, 128), np.float32)
    for h in range(16):
        e2[h, h, :] = 1.0
        e2[32 + h, h, :] = 1.0
    put('e2', e2.reshape(128, 2048))
    rsp = np.ones((128, 256), np.float32)
    rsp[:, 0] = 0.0
    rsp[:, 128] = 0.0
    put('rsp', rsp)
    rss = np.ones((128, 64), np.float32)
    rss[:, 0::4] = 0.0
    put('rss', rss)
    csc = np.zeros((128, 4, 16), np.float32)
    for gi, w in enumerate((2, 4, 8, 16)):
        for t in range(16):
            csc[:, gi, t] = 1.0 / min(t + 1, w)
    put('csc', csc.reshape(128, 64))
    blk = np.zeros((128, 16), np.float32)
    for r in range(64):
        blk[r, r // 4] = 1.0
    put('blk', blk)
    expd = np.zeros((128, 8, 128), np.float32)
    for j in range(8):
        for m in range(128):
            expd[2 * j + m // 64, j, m] = 1.0
            expd[32 + 2 * j + m // 64, j, m] = 1.0
    put('expd', expd.reshape(128, 1024))
    return c


def setup_common(P, k, consts):
    nc = P.nc
    k.ps = P.es.enter_context(nc.psum_tensor("ps", [128, 4096], F32))
    k.psb = k.ps[:].bitcast(BF16)
    k.cst = P.sb([128, CST_SMALL], F32, "cst")
    P.dma('sync', k.cst[:], consts[:, 0:CST_SMALL], writes=['cst'], sem='cst')

    def cs(name, rows=128):
        o, w = CST_OFF[name]
        return k.cst[0:rows, o:o + w]
    k.cs = cs
    k.identf = cs('ident')
    k.mhalf = cs('mhalf')
    k.cstb = P.sb([128, 384], BF16, "cstb")
    k.junk = P.sb([128, 1024], BF16, 'junk')
    CP(P, 'vector', k.cstb[:, 0:128], cs('ident'), ['cst'], ['identb'])
    P.dma('gpsimd', k.cstb[:, 128:384], consts[:, CST_SMALL:CST_SMALL + 256], writes=['cstb'], sem='cstbn')
    k.identb = k.cstb[:, 0:128]
    k.negcb = k.cstb[:, 128:256]
    k.negsb = k.cstb[:, 256:384]


def bank(k, b, n=512, bf=False, nb=1):
    if bf:
        return k.psb[:, 1024 * b: 1024 * b + n]
    return k.ps[:, 512 * b: 512 * b + n]


def load_fm(P, k, rows_ap, R, out_ap, key, tag, es=None):
    t = P.sb([128, 128], F32, "lfm_" + tag, es)
    P.dma('sync', t[0:R, :], rows_ap, writes=['lfm_' + tag], sem='lfm_' + tag)
    pb = bank(k, 7)
    P.PE(lambda e: e.transpose(out=pb[:, 0:R], in_=t[0:R, :], identity=k.identf[0:R, 0:R]),
         reads=['lfm_' + tag, 'cst'], writes=[('ps', 7)])
    P.V(lambda e: e.tensor_copy(out=out_ap, in_=pb[:, 0:R]), reads=[('ps', 7)], writes=[key])


def TT(P, eng, out, in0, in1, op, reads, writes):
    return P.op(eng, lambda e: e.tensor_tensor(out=out, in0=in0, in1=in1, op=op), reads, writes)


def TS(P, eng, out, in0, s1, s2, op0, op1, reads, writes):
    if op1 is None:
        return P.op(eng, lambda e: e.tensor_scalar(out=out, in0=in0, scalar1=s1, scalar2=None, op0=op0), reads, writes)
    return P.op(eng, lambda e: e.tensor_scalar(out=out, in0=in0, scalar1=s1, scalar2=s2, op0=op0, op1=op1), reads, writes)


def STT(P, out, in0, scalar, in1, op0, op1, reads, writes):
    return P.op('vector', lambda e: e.scalar_tensor_tensor(out=out, in0=in0, scalar=scalar, in1=in1, op0=op0, op1=op1), reads, writes)


def ACTF(P, out, in_, func, reads, writes, bias=None, scale=None, accum=None):
    kw = {}
    if bias is not None:
        kw['bias'] = bias
    if scale is not None:
        kw['scale'] = scale
    if accum is not None:
        kw['accum_out'] = accum
    return P.op('scalar', lambda e: e.activation(out=out, in_=in_, func=func, **kw), reads, writes)


def CP(P, eng, out, in_, reads, writes):
    if eng == 'scalar':
        return P.op(eng, lambda e: e.activation(out=out, in_=in_, func=AF.Identity), reads, writes)
    return P.op(eng, lambda e: e.tensor_copy(out=out, in_=in_), reads, writes)


def MM(P, items, reads, writes):
    items = list(items)

    def f(e):
        for it in items:
            (o, l, r, st, sp) = it[:5]
            if len(it) > 5:
                ins = e.matmul(o, lhsT=l, rhs=r, start=st, stop=sp, tile_position=it[5])
            else:
                ins = e.matmul(o, lhsT=l, rhs=r, start=st, stop=sp)
        return ins
    return P.op('tensor', f, reads, writes)


def TR(P, items, reads, writes):
    items = list(items)

    def f(e):
        for (o, i, idn) in items:
            ins = e.transpose(out=o, in_=i, identity=idn)
        return ins
    return P.op('tensor', f, reads, writes)


def MEMSET(P, eng, ap, val, writes):
    return P.op(eng, lambda e: e.memset(ap, val), (), writes)


def rstd_op(P, k, x_ap, n, xkeys, st, col, scale=1.0 / D, extra=None):
    junk = k.junk
    ks = ('st', id(st), col)
    P.A(lambda e: e.activation(out=junk[0:n, 0:x_ap.shape[1]], in_=x_ap, func=AF.Square, accum_out=st[0:n, col:col + 1]),
        reads=xkeys, writes=['junk', ks])
    P.V(lambda e: e.tensor_scalar(out=st[0:n, col + 1:col + 2], in0=st[0:n, col:col + 1], scalar1=scale, scalar2=EPS,
                                  op0=ALU.mult, op1=ALU.add), reads=[ks], writes=[ks])
    P.G(lambda e: e.tensor_tensor(out=st[0:n, col + 2:col + 3], in0=st[0:n, col + 1:col + 2], in1=k.mhalf[0:n, :], op=ALU.pow),
        reads=[ks, 'cst'], writes=[ks])
    if extra is not None:
        P.V(lambda e: e.tensor_scalar(out=st[0:n, col + 2:col + 3], in0=st[0:n, col + 2:col + 3], scalar1=extra, scalar2=None,
                                      op0=ALU.mult), reads=[ks], writes=[ks])
    return st[0:n, col + 2:col + 3], ks


def norm_T(P, k, x_ap, n, xkeys, gain_fm, gkey, hT_ap, hTkey, st, col, hb, hbkey, tb):
    r, ks = rstd_op(P, k, x_ap, n, xkeys, st, col)
    P.V(lambda e: e.tensor_scalar(out=hb[0:n, :], in0=x_ap, scalar1=r, scalar2=None, op0=ALU.mult),
        reads=list(xkeys) + [ks], writes=[hbkey])
    pb = k.psb[:, 1024 * tb: 1024 * tb + 1024].rearrange("p (c t) -> p c t", c=8)

    def tr(e):
        for c in range(8):
            ins = e.transpose(out=pb[:, c, 0:n], in_=hb[0:n, c * 128:(c + 1) * 128], identity=k.identb[0:n, 0:n])
        return ins
    P.PE(tr, reads=[hbkey, 'identb'], writes=[('ps', tb)])
    P.V(lambda e: e.tensor_tensor(out=hT_ap, in0=pb[:, :, 0:n], in1=bcast(gain_fm, 2, n), op=ALU.mult),
        reads=[('ps', tb), gkey], writes=[hTkey])


def phase_C(P, k, es, io, x_src, do_prompt=True, do_sample=True):
    wup = P.sb([128, 8, 2 * DFF], BF16, "wup", es)
    wdn = P.sb([128, 22, D], BF16, "wdn", es)
    wup_d = io['w_up'].rearrange("(c p) n -> p c n", p=128)
    wdn_d = io['w_down'].rearrange("(c p) n -> p c n", p=128)
    NCB = 4
    cbw = DFF // NCB
    def load_wup_block(q):
        prs = []
        for br in range(2):
            c0 = br * DFF + q * cbw
            prs.append((wup[:, :, c0:c0 + cbw], wup_d[:, :, c0:c0 + cbw]))
        P.dma_multi('gpsimd', prs, [('wup', q)], ('wup', q), max_dma_last_dim=4096)

    def load_rest_weights():
        for q in range(1, NCB):
            load_wup_block(q)
        for gq in range(2):
            P.dma_multi('gpsimd', [(wdn[:, 11 * gq:11 * gq + 11, :], wdn_d[:, 11 * gq:11 * gq + 11, :])], [('wdn', c) for c in range(11 * gq, 11 * gq + 11)], ('wdn', gq), max_dma_last_dim=4096)
    load_wup_block(0)
    wupk = None
    gC = P.sb([128, 8], F32, "gC", es)
    load_fm(P, k, io['norm_ffn'].rearrange("(c p) -> c p", p=128), 8, gC[:, :], 'gC', 'gC', es)
    cw = P.sb([128, 3, 44], F32, "cwC", es)
    cb = P.sb([128, 44], F32, "cbC", es)
    cwd = io['ffn_conv_w'].rearrange("k (c p) -> k c p", p=128)
    for kk in range(3):
        load_fm(P, k, cwd[kk], 44, cw[:, kk, :], ('cwC', kk), 'cwC%d' % kk, es)
    load_fm(P, k, io['ffn_conv_b'].rearrange("(c p) -> c p", p=128), 44, cb[:, :], 'cbC', 'cbC', es)
    cwk = [('cwC', kk) for kk in range(3)] + ['cbC']
    P.V(lambda e: e.tensor_scalar(out=cw[:, :, 0:22], in0=cw[:, :, 0:22], scalar1=0.5, scalar2=None, op0=ALU.mult),
        reads=cwk, writes=cwk[:3])
    P.V(lambda e: e.tensor_scalar(out=cb[:, 0:22], in0=cb[:, 0:22], scalar1=0.5, scalar2=None, op0=ALU.mult),
        reads=['cbC'], writes=['cbC'])
    fgb = P.sb([128, D], F32, "fgb", es)
    P.dma('sync', fgb[:], io['final_norm'].partition_broadcast(128), writes=['fgb'], sem='fgb')

    xt = [P.sb([128, D], F32, "xtC%d" % i, es) for i in range(2)]
    hb = [k.junk] * 2
    st = P.sb([128, 64], F32, "stC", es)
    aT = P.sb([128, 22, 512], BF16, "aTC", es)
    NR = 3
    B_ = {}

    def alloc_group(stack, ntok, nseq, tag):
        B_['hT'] = P.sb([128, 8, ntok], BF16, "hTC" + tag, stack)
        B_['accg'] = [P.sb([128, ntok], F32, "accg%d%s" % (i, tag), stack) for i in range(NR)]
        B_['accv'] = [P.sb([128, ntok], F32, "accv%d%s" % (i, tag), stack) for i in range(NR)]
        B_['th'] = [P.sb([128, ntok], F32, "th%d%s" % (i, tag), stack) for i in range(2)]
        halo = P.sb([128, 44, nseq, 2], F32, "halo" + tag, stack)
        fix = P.sb([128, 44, nseq, 2], F32, "fix" + tag, stack)
        return halo, fix
    x3 = [P.sb([128, D], F32, "x3C%d" % i, es) for i in range(1)] * 2
    stg = [P.sb([32, 512], F32, "stgC%d" % i, es) for i in range(2)]
    cnt = {'x': 0, 'st': 0, 'p': 0, 'o': 0}

    def run_group(tok0, nseq, L, halo, fix, y_dst, gname, do=('norm', 'loop', 'down'), js=None):
        ntok = nseq * L
        nsub = (ntok + 127) // 128
        hT, accg, accv, th = B_['hT'], B_['accg'], B_['accv'], B_['th']
        hk = [('halo' + gname, c) for c in range(44)]
        fk = 'fix' + gname
        if 'norm' in do:
            for j in (range(nsub) if js is None else js):
                n = min(128, ntok - j * 128)
                i = cnt['x'] % 2
                cnt['x'] += 1
                P.dma('sync', xt[i][0:n, :], x_src[tok0 + j * 128: tok0 + j * 128 + n, :], writes=[('xtC', i)], sem=('xtC', i))
                col = (cnt['st'] % 8) * 4
                cnt['st'] += 1
                norm_T(P, k, xt[i][0:n, :], n, [('xtC', i)], gC[:, :], 'gC', hT[:, :, j * 128: j * 128 + n], ('hTC', j),
                       st, col, hb[i], 'junk', 4)
        hTk = [('hTC', j) for j in range(nsub)]
        if 'loop' in do:
            w0 = bcast(cw[:, 0, :], 2, nseq)
            w1 = bcast(cw[:, 1, :], 2, nseq)
            P.G(lambda e: e.tensor_tensor(out=fix[:, :, :, 0], in0=halo[:, :, :, 0], in1=w0, op=ALU.mult), reads=hk + cwk, writes=[fk])
            P.G(lambda e: e.tensor_tensor(out=fix[:, :, :, 1], in0=halo[:, :, :, 1], in1=w1, op=ALU.mult), reads=hk + cwk, writes=[fk + 'b'])
            P.G(lambda e: e.tensor_tensor(out=fix[:, :, :, 0], in0=fix[:, :, :, 0], in1=fix[:, :, :, 1], op=ALU.add), reads=[fk, fk + 'b'], writes=[fk])
            P.G(lambda e: e.tensor_tensor(out=fix[:, :, :, 1], in0=halo[:, :, :, 1], in1=w0, op=ALU.mult), reads=hk + cwk + [fk], writes=[fk + 'b'])
            fks = [fk, fk + 'b']

            def v3(ap, a, b):
                return ap.rearrange("p (s l) -> p s l", s=nseq)[:, :, a:b]

            def stage1(jj):
                for br, acc, nm in ((0, accg, 'accg'), (1, accv, 'accv')):
                    c = br * 22 + jj
                    b = (cnt['p'] % 4)
                    cnt['p'] += 1
                    pb = bank(k, b, ntok)

                    def mm(e, c=c, pb=pb):
                        for kc in range(8):
                            ins = e.matmul(pb, lhsT=wup[:, kc, c * 128:(c + 1) * 128], rhs=hT[:, kc, 0:ntok], start=(kc == 0), stop=(kc == 7))
                        return ins
                    P.PE(mm, reads=[('wup', min(NCB - 1, (jj * 128) // cbw)), ('wup', min(NCB - 1, (jj * 128 + 127) // cbw))] + hTk, writes=[('ps', b)])
                    r = jj % NR
                    a_ap = acc[r][:, 0:ntok]
                    P.A(lambda e, c=c, pb=pb, a_ap=a_ap: e.activation(out=a_ap, in_=pb, func=AF.Identity, bias=cb[:, c:c + 1], scale=cw[:, 2, c:c + 1]),
                        reads=[('ps', b)] + cwk, writes=[(nm, r)])
                    P.V(lambda e, c=c, pb=pb, a_ap=a_ap: e.scalar_tensor_tensor(out=v3(a_ap, 1, L), in0=v3(pb, 0, L - 1), scalar=cw[:, 1, c:c + 1], in1=v3(a_ap, 1, L), op0=ALU.mult, op1=ALU.add),
                        reads=[('ps', b), (nm, r)] + cwk, writes=[(nm, r)])
                    P.V(lambda e, c=c, pb=pb, a_ap=a_ap: e.scalar_tensor_tensor(out=v3(a_ap, 2, L), in0=v3(pb, 0, L - 2), scalar=cw[:, 0, c:c + 1], in1=v3(a_ap, 2, L), op0=ALU.mult, op1=ALU.add),
                        reads=[('ps', b), (nm, r)] + cwk, writes=[(nm, r)])
                    P.V(lambda e, c=c, a_ap=a_ap: e.tensor_tensor(out=v3(a_ap, 0, 2), in0=v3(a_ap, 0, 2), in1=fix[:, c, :, :], op=ALU.add),
                        reads=[(nm, r)] + fks, writes=[(nm, r)])
                    P.V(lambda e, c=c, pb=pb: e.tensor_copy(out=halo[:, c, :, :], in_=v3(pb, L - 2, L)),
                        reads=[('ps', b)] + fks, writes=[hk[c]])

            def stage2(jj):
                r = jj % NR
                P.A(lambda e: e.activation(out=th[jj % 2][:, 0:ntok], in_=accg[r][:, 0:ntok], func=AF.Tanh), reads=[('accg', r)], writes=[('th', jj % 2)])
                P.G(lambda e: e.tensor_scalar(out=th[jj % 2][:, 0:ntok], in0=th[jj % 2][:, 0:ntok], scalar1=1.0, scalar2=1.0, op0=ALU.add, op1=ALU.mult),
                    reads=[('th', jj % 2)], writes=[('th', jj % 2)])
                P.G(lambda e: e.tensor_tensor(out=th[jj % 2][:, 0:ntok], in0=th[jj % 2][:, 0:ntok], in1=accg[r][:, 0:ntok], op=ALU.mult),
                    reads=[('th', jj % 2), ('accg', r)], writes=[('th', jj % 2)])
                P.G(lambda e: e.tensor_tensor(out=aT[:, jj, 0:ntok], in0=th[jj % 2][:, 0:ntok], in1=accv[r][:, 0:ntok], op=ALU.mult),
                    reads=[('th', jj % 2), ('accv', r)], writes=[('aT', jj)])
            SK = 1
            for step in range(22 + SK):
                if step < 22:
                    stage1(step)
                if step >= SK:
                    stage2(step - SK)
        aTk = [('aT', jj) for jj in range(22)]
        wdk = [('wdn', c) for c in range(22)]
        if 'down' in do:
            for j in (range(nsub) if js is None else js):
                n = min(128, ntok - j * 128)
                i = cnt['x'] % 2
                cnt['x'] += 1
                P.dma('sync', xt[i][0:n, :], x_src[tok0 + j * 128: tok0 + j * 128 + n, :], writes=[('xtC', i)], sem=('xtC', i))
                o = cnt['o'] % 2
                cnt['o'] += 1
                for hf in range(2):
                    b = 5 + hf
                    pb = bank(k, b)[0:n, :]

                    def mm(e, pb=pb, hf=hf, n=n, j=j):
                        for jj in range(22):
                            ins = e.matmul(pb, lhsT=aT[:, jj, j * 128: j * 128 + n], rhs=wdn[:, jj, hf * 512:(hf + 1) * 512], start=(jj == 0), stop=(jj == 21))
                        return ins
                    P.PE(mm, reads=aTk + wdk, writes=[('ps', b)])
                    P.V(lambda e, pb=pb, hf=hf, n=n, i=i, o=o: e.tensor_tensor(out=x3[o][0:n, hf * 512:(hf + 1) * 512], in0=pb, in1=xt[i][0:n, hf * 512:(hf + 1) * 512], op=ALU.add),
                        reads=[('ps', b), ('xtC', i)], writes=[('x3', 0, hf)])
                col = (cnt['st'] % 8) * 4
                cnt['st'] += 1
                r, ks = rstd_op(P, k, x3[o][0:n, :], n, [('x3', 0, 0), ('x3', 0, 1)], st, col)
                P.V(lambda e, n=n, o=o, r=r: e.scalar_tensor_tensor(out=x3[o][0:n, :], in0=x3[o][0:n, :], scalar=r, in1=fgb[0:n, :], op0=ALU.mult, op1=ALU.mult),
                    reads=[('x3', 0, 0), ('x3', 0, 1), ks, 'fgb'], writes=[('x3', 0, 0), ('x3', 0, 1)])
                P.dma('scalar', y_dst[j * 128: j * 128 + n, :], x3[o][0:n, :], reads=[('x3', 0, 0), ('x3', 0, 1)], sem=('yout', o))
        return hk

    def state_out(halo, hk, nseq, dst):
        R = nseq * 2
        for q in range(11):
            pb = bank(k, 7)

            def tr(e, q=q, pb=pb):
                for u in range(4):
                    c = q * 4 + u
                    ins = e.transpose(out=pb[0:R, u * 128:(u + 1) * 128], in_=halo[:, c, :, :].rearrange("p s r -> p (s r)"), identity=k.identf)
                return ins
            P.PE(tr, reads=hk[q * 4:q * 4 + 4] + ['cst'], writes=[('ps', 7)])
            P.V(lambda e, q=q, pb=pb: e.tensor_copy(out=stg[q % 2][0:R, :], in_=pb[0:R, :]), reads=[('ps', 7)], writes=[('stgC', q % 2)])
            P.dma('sync', dst[:, q * 512:(q + 1) * 512], stg[q % 2][0:R, :], reads=[('stgC', q % 2)], sem=('stgC', q % 2))

    if do_prompt:
        with ExitStack() as esp:
            haloP, fixP = alloc_group(esp, 512, 1, 'P')
            MEMSET(P, 'vector', haloP[:], 0.0, [('haloP', c) for c in range(44)])
            NSUP = T // 512
            ga = lambda S: (S * 512, 1, 512, haloP, fixP, io['y_prompt'][S * 512:(S + 1) * 512, :], 'P')
            run_group(*ga(0), do=('norm',))
            load_rest_weights()
            for S in range(NSUP):
                hk = run_group(*ga(S), do=('loop',))
                if S + 1 < NSUP:
                    run_group(*ga(S + 1), do=('norm',))
                run_group(*ga(S), do=('down',))
            state_out(haloP, hk, 1, io['ffn_prompt'])
            P.barrier()
    if not do_prompt:
        load_rest_weights()
    if do_sample:
        with ExitStack() as ess:
            haloS, fixS = alloc_group(ess, NS, 16, 'S')
            hkS = [('haloS', c) for c in range(44)]
            sfv = io['state_ffn_conv'].rearrange("b r c -> (b r) c")
            for q in range(11):
                P.dma('sync', stg[q % 2][0:32, :], sfv[:, q * 512:(q + 1) * 512], writes=[('stgC', q % 2)], sem=('stgC', q % 2))
                pb = bank(k, 7)[:, 0:128].rearrange("p (u r) -> p u r", u=4)
                TR(P, [(pb[:, u, :], stg[q % 2][0:32, u * 128:(u + 1) * 128], k.identf[0:32, 0:32]) for u in range(4)], [('stgC', q % 2), 'cst'], [('ps', 7)])
                CP(P, 'vector', haloS[:, 4 * q:4 * q + 4, :, :].rearrange("p u s r -> p u (s r)"), pb, [('ps', 7)], hkS[4 * q:4 * q + 4])
            hk = run_group(T, 16, 4, haloS, fixS, io['y_sample'], 'S')
            state_out(haloS, hk, 16, io['ffn_sample'].rearrange("b r c -> (b r) c"))
            P.barrier()


def phase_B(P, k, es, io, x_src, x_dst, do_prompt=True, do_sample=True):
    wq = P.sb([128, 8, D], BF16, "wmq", es)
    wo = P.sb([128, 8, D], BF16, "wmo", es)

    def load_w(nm, w):
        wd = io[nm].rearrange("(c p) n -> p c n", p=128)
        for gq in range(2):
            P.dma_multi('gpsimd', [(w[:, 4 * gq:4 * gq + 4, :], wd[:, 4 * gq:4 * gq + 4, :])], [(nm, c) for c in range(4 * gq, 4 * gq + 4)], (nm, gq), max_dma_last_dim=4096)
    kk_ = lambda nm: [(nm, c) for c in range(8)]
    gM = P.sb([128, 8], F32, "gM", es)
    load_fm(P, k, io['norm_mem'].rearrange("(c p) -> c p", p=128), 8, gM[:, :], 'gM', 'gM', es)
    xt = [P.sb([128, D], F32, "xtB%d" % i, es) for i in range(2)]
    hb = k.junk
    st = P.sb([128, 64], F32, "stB", es)
    hT = P.sb([128, 8, 512], BF16, "hTB", es)
    qT = P.sb([128, 8, 512], BF16, "qTB", es)
    PT = P.sb([128, 8, 512], BF16, "PTB", es)
    oT = P.sb([128, 8, 512], BF16, "oTB", es)
    pe = P.sb([128, 4, 256], F32, "peB", es)
    pn = P.sb([128, 4, 256], BF16, "pnB", es)
    KT, Vb = [None], [None]
    sm = P.sb([128, 32], F32, "smB", es)
    ob = P.sb([128, D], BF16, "obB", es)
    x2 = P.sb([128, D], F32, "x2B", es)
    cnt = {'x': 0, 'st': 0}

    def load_norm(tok0, ntok, gain, gkey, src, tb=4):
        nsub = (ntok + 127) // 128
        for j in range(nsub):
            n = min(128, ntok - j * 128)
            i = cnt['x'] % 2
            cnt['x'] += 1
            P.dma('sync', xt[i][0:n, :], src[tok0 + j * 128: tok0 + j * 128 + n, :], writes=[('xtB', i)], sem=('xtB', i))
            col = (cnt['st'] % 8) * 4
            cnt['st'] += 1
            norm_T(P, k, xt[i][0:n, :], n, [('xtB', i)], gain, gkey, hT[:, :, j * 128: j * 128 + n], ('hTB', j), st, col, hb, 'junk', tb)
        return [('hTB', j) for j in range(nsub)]

    def softmax_rows(S_ap, n, skeys, pn_out=None, pn_key='pnB'):
        pn_ = pn if pn_out is None else pn_out
        P.V(lambda e: e.tensor_reduce(out=sm[0:n, 0:4], in_=S_ap, axis=AX.X, op=ALU.max), reads=skeys, writes=['smB'])
        P.V(lambda e: e.tensor_scalar(out=sm[0:n, 4:8], in0=sm[0:n, 0:4], scalar1=-1.0, scalar2=None, op0=ALU.mult), reads=['smB'], writes=['smB'])
        for h in range(4):
            P.A(lambda e, h=h: e.activation(out=pe[0:n, h, :], in_=S_ap[:, h, :], func=AF.Exp, bias=sm[0:n, 4 + h:5 + h], accum_out=sm[0:n, 8 + h:9 + h]),
                reads=skeys + ['smB'], writes=[('peB', h), ('smB', h)])
        smk = [('smB', h) for h in range(4)]
        P.V(lambda e: e.reciprocal(out=sm[0:n, 12:16], in_=sm[0:n, 8:12]), reads=smk, writes=['smB2'])
        P.V(lambda e: e.tensor_tensor(out=pn_[0:n, :, :], in0=pe[0:n, :, :], in1=bcast(sm[0:n, 12:16], 2, 256), op=ALU.mult),
            reads=[('peB', h) for h in range(4)] + ['smB2'], writes=[pn_key])

    def qproj(ntok, hTk):
        for c in range(8):
            b = c % 2
            pb = bank(k, b, ntok)

            def mm(e, c=c, pb=pb):
                for kc in range(8):
                    ins = e.matmul(pb, lhsT=wq[:, kc, c * 128:(c + 1) * 128], rhs=hT[:, kc, 0:ntok], start=(kc == 0), stop=(kc == 7))
                return ins
            P.PE(mm, reads=kk_('w_mq') + hTk, writes=[('ps', b)])
            P.A(lambda e, c=c, pb=pb: e.activation(out=qT[:, c, 0:ntok], in_=pb, func=AF.Identity, scale=1.0 / 16.0), reads=[('ps', b)], writes=[('qTB', c)])
        return [('qTB', c) for c in range(8)]

    def oproj(tok0, ntok, src, dst, oTk):
        nsub = (ntok + 127) // 128
        for j in range(nsub):
            n = min(128, ntok - j * 128)
            i = cnt['x'] % 2
            cnt['x'] += 1
            P.dma('sync', xt[i][0:n, :], src[tok0 + j * 128: tok0 + j * 128 + n, :], writes=[('xtB', i)], sem=('xtB', i))
            for hf in range(2):
                b = 2 + hf
                pb = bank(k, b)[0:n, :]

                def mm(e, pb=pb, hf=hf, n=n, j=j):
                    for c in range(8):
                        ins = e.matmul(pb, lhsT=oT[:, c, j * 128: j * 128 + n], rhs=wo[:, c, hf * 512:(hf + 1) * 512], start=(c == 0), stop=(c == 7))
                    return ins
                P.PE(mm, reads=oTk + kk_('w_mo'), writes=[('ps', b)])
                P.V(lambda e, pb=pb, hf=hf, n=n, i=i: e.tensor_tensor(out=x2[0:n, hf * 512:(hf + 1) * 512], in0=pb, in1=xt[i][0:n, hf * 512:(hf + 1) * 512], op=ALU.add),
                    reads=[('ps', b), ('xtB', i)], writes=[('x2B', hf)])
            P.dma('sync', dst[tok0 + j * 128: tok0 + j * 128 + n, :], x2[0:n, :], reads=[('x2B', 0), ('x2B', 1)], sem='x2B')

    BSTEP = 99
    if do_prompt and BSTEP >= 1:
        esp = ExitStack()
        esp.__enter__()
        wk = P.sb([128, 8, D], BF16, "wmk", esp)
        wv = P.sb([128, 8, D], BF16, "wmv", esp)
        gKV = P.sb([128, 8], F32, "gKV", esp)
        KT[0] = P.sb([128, 8, 256], BF16, "KTB0", esp)
        Vb[0] = P.sb([128, 2, D], BF16, "VbB0", esp)
        kvf = P.sb([128, D], F32, "kvf", esp)
        load_w('w_mk', wk)
        load_w('w_mv', wv)
        load_w('w_mq', wq)
        load_w('w_mo', wo)
        load_fm(P, k, io['norm_memkv'].rearrange("(c p) -> c p", p=128), 8, gKV[:, :], 'gKV', 'gKV', esp)
        if True:
            hmk = load_norm(0, 256, gKV[:, :], 'gKV', io['mem_prompt'])
            for nm, w, dst in (('w_mk', wk, io['mem_k_prompt']), ('w_mv', wv, io['mem_v_prompt'])):
                if BSTEP < 2:
                    break
                for mt in range(2):
                    for hf in range(2):
                        b = hf
                        pb = bank(k, b)

                        def mm(e, pb=pb, hf=hf, mt=mt, w=w):
                            for kc in range(8):
                                ins = e.matmul(pb, lhsT=hT[:, kc, mt * 128:(mt + 1) * 128], rhs=w[:, kc, hf * 512:(hf + 1) * 512], start=(kc == 0), stop=(kc == 7))
                            return ins
                        P.PE(mm, reads=kk_(nm) + hmk, writes=[('ps', b)])
                        P.A(lambda e, pb=pb, hf=hf: e.activation(out=kvf[:, hf * 512:(hf + 1) * 512], in_=pb, func=AF.Identity), reads=[('ps', b)], writes=[('kvf', hf)])
                        if nm == 'w_mv':
                            P.V(lambda e, pb=pb, hf=hf, mt=mt: e.tensor_copy(out=Vb[0][:, mt, hf * 512:(hf + 1) * 512], in_=pb), reads=[('ps', b)], writes=[('VbB', 0, mt, hf)])
                    P.dma('sync', dst[mt * 128:(mt + 1) * 128, :], kvf[:, :], reads=[('kvf', 0), ('kvf', 1)], sem='kvf')
            for c in range(8 if BSTEP >= 3 else 0):
                b = c % 2
                pb = bank(k, b, 256)

                def mm(e, c=c, pb=pb):
                    for kc in range(8):
                        ins = e.matmul(pb, lhsT=wk[:, kc, c * 128:(c + 1) * 128], rhs=hT[:, kc, 0:256], start=(kc == 0), stop=(kc == 7))
                    return ins
                P.PE(mm, reads=kk_('w_mk') + hmk, writes=[('ps', b)])
                P.V(lambda e, c=c, pb=pb: e.tensor_copy(out=KT[0][:, c, :], in_=pb), reads=[('ps', b)], writes=[('KTB', 0, c)])
            KTk = [('KTB', 0, c) for c in range(8)]
            Vk = [('VbB', 0, mt, hf) for mt in range(2) for hf in range(2)]
            pn2 = P.sb([128, 4, 256], BF16, "pnB2", esp)
            qT1 = P.sb([128, 8, 512], BF16, "qTB1", esp)
            PT1 = P.sb([128, 8, 512], BF16, "PTB1", esp)
            qTs, PTs = [qT, qT1], [PT, PT1]
            NSUP = T // 512

            def g_qproj(S):
                hTk = [('hTB', j) for j in range(4)]
                q_ = qTs[S % 2]
                for c in range(8):
                    b_ = c % 2
                    pb = bank(k, b_, 512)
                    MM(P, [(pb, wq[:, kc, c * 128:(c + 1) * 128], hT[:, kc, 0:512], kc == 0, kc == 7) for kc in range(8)], kk_('w_mq') + hTk, [('ps', b_)])
                    ACTF(P, q_[:, c, :], pb, AF.Identity, [('ps', b_)], [('qTB', S % 2, c)], scale=1.0 / 16.0)
                    yield

            def g_soft(S):
                q_, pt_ = qTs[S % 2], PTs[S % 2]
                qk = [('qTB', S % 2, c) for c in range(8)]
                pend = None

                def ptrans(j, pnj, pk):
                    tb = k.psb[:, 1024 * 2: 1024 * 3].rearrange("p (c t) -> p c t", c=8)
                    TR(P, [(tb[:, 2 * h + mc, :], pnj[:, h, mc * 128:(mc + 1) * 128], k.identb) for h in range(4) for mc in range(2)], [pk, 'identb'], [('ps', 2)])
                    CP(P, 'vector', pt_[:, :, j * 128:(j + 1) * 128], tb, [('ps', 2)], [('PTB', S % 2, j)])
                for j in range(4):
                    Sps = k.ps[:, 512 * 4: 512 * 6].rearrange("p (h m) -> p h m", h=4)
                    for h in range(4):
                        MM(P, [(Sps[:, h, :], q_[:, 2 * h + dc, j * 128:(j + 1) * 128], KT[0][:, 2 * h + dc, :], dc == 0, dc == 1) for dc in range(2)],
                           qk + KTk, [('ps', 4 + h // 2)])
                    yield
                    pnj = pn if j % 2 == 0 else pn2
                    pk = 'pnB' if j % 2 == 0 else 'pnB2'
                    softmax_rows(Sps, 128, [('ps', 4), ('ps', 5)], pn_out=pnj, pn_key=pk)
                    yield
                    if pend is not None:
                        ptrans(*pend)
                        yield
                    pend = (j, pnj, pk)
                ptrans(*pend)
                yield

            def g_out(S):
                pt_ = PTs[S % 2]
                PTk = [('PTB', S % 2, j) for j in range(4)]
                for c in range(8):
                    h, dc = c // 2, c % 2
                    pb = bank(k, 3)
                    MM(P, [(pb, Vb[0][:, mc, h * 256 + dc * 128: h * 256 + dc * 128 + 128], pt_[:, 2 * h + mc, :], mc == 0, mc == 1) for mc in range(2)], PTk + Vk, [('ps', 3)])
                    CP(P, 'scalar', oT[:, c, :], pb, [('ps', 3)], [('oTB', c)])
                    if c % 2 == 1:
                        yield
                oTk = [('oTB', c) for c in range(8)]
                for j in range(4):
                    i = cnt['x'] % 2
                    cnt['x'] += 1
                    P.dma('sync', xt[i][:, :], x_src[S * 512 + j * 128: S * 512 + (j + 1) * 128, :], writes=[('xtB', i)], sem=('xtB', i))
                    for hf in range(2):
                        pb = bank(k, 6 + hf)
                        MM(P, [(pb, oT[:, c, j * 128:(j + 1) * 128], wo[:, c, hf * 512:(hf + 1) * 512], c == 0, c == 7) for c in range(8)], oTk + kk_('w_mo'), [('ps', 6 + hf)])
                        TT(P, 'vector', x2[:, hf * 512:(hf + 1) * 512], pb, xt[i][:, hf * 512:(hf + 1) * 512], ALU.add, [('ps', 6 + hf), ('xtB', i)], [('x2B', hf)])
                    P.dma('scalar', x_dst[S * 512 + j * 128: S * 512 + (j + 1) * 128, :], x2[:, :], reads=[('x2B', 0), ('x2B', 1)], sem='x2B')
                    yield

            def run_all(gs):
                gs = [g for g in gs if g is not None]
                while gs:
                    for g in list(gs):
                        try:
                            next(g)
                        except StopIteration:
                            gs.remove(g)
            load_norm(0, 512, gM[:, :], 'gM', x_src, tb=2)
            run_all([g_qproj(0)])
            for S in range(NSUP):
                nxt = None
                if S + 1 < NSUP:
                    load_norm((S + 1) * 512, 512, gM[:, :], 'gM', x_src, tb=2)
                    nxt = g_qproj(S + 1)
                run_all([g_soft(S), g_out(S - 1) if S > 0 else None, nxt])
            run_all([g_out(NSUP - 1)])
        P.barrier()
        esp.__exit__(None, None, None)
    else:
        load_w('w_mq', wq)
        load_w('w_mo', wo)
    if do_sample:
        with ExitStack() as ess:
            NG = 4
            Kb4 = [P.sb([128, NG, 2, D], BF16, "Kb4_%d" % i, ess) for i in range(2)]
            Vb4 = [P.sb([128, NG, 2, D], BF16, "Vb4_%d" % i, ess) for i in range(2)]
            KT4 = P.sb([128, NG, 8, 256], BF16, "KT4", ess)
            hTk = load_norm(T, NS, gM[:, :], 'gM', x_src)
            qTk = qproj(NS, hTk)
            ck = io['cache_mem_k'].rearrange("b (mt p) h d -> b p mt (h d)", p=128)
            cv = io['cache_mem_v'].rearrange("b (mt p) h d -> b p mt (h d)", p=128)
            MEMSET(P, 'vector', k.ps[:, 512 * 2: 512 * 6], 0.0, [('ps', 2), ('ps', 3), ('ps', 4), ('ps', 5)])
            CP(P, 'vector', pn[:, :, :], k.ps[:, 512 * 4: 512 * 6].rearrange("p (h m) -> p h m", h=4), [('ps', 4), ('ps', 5)], ['pnB'])
            CP(P, 'vector', ob[:, :], k.ps[:, 512 * 2: 512 * 4], [('ps', 2), ('ps', 3)], ['obB'])

            def loads(g):
                r = g % 2
                for q in range(NG):
                    bq = NG * g + q
                    for mt in range(2):
                        P.dma('gpsimd', Kb4[r][:, q, mt, :], ck[bq, :, mt, :], writes=[('Kb4', r, q, mt)], sem=('Kb4', r, q, mt), max_dma_last_dim=4096)
                        P.dma('gpsimd', Vb4[r][:, q, mt, :], cv[bq, :, mt, :], writes=[('Vb4', r, q, mt)], sem=('Vb4', r, q, mt), max_dma_last_dim=4096)
            loads(0)
            NR_ = 32 * (NG - 1) + 4
            for g in range(16 // NG):
                r = g % 2
                if g + 1 < 16 // NG:
                    loads(g + 1)
                for q in range(NG):
                    for half in range(2):
                        tb = k.psb[:, 1024 * (6 + half): 1024 * (7 + half)].rearrange("p (c m) -> p c m", c=4)
                        TR(P, [(tb[:, cc, mt * 128:(mt + 1) * 128], Kb4[r][:, q, mt, (half * 4 + cc) * 128:(half * 4 + cc + 1) * 128], k.identb) for cc in range(4) for mt in range(2)],
                           [('Kb4', r, q, 0), ('Kb4', r, q, 1), 'identb'], [('ps', 6 + half)])
                        CP(P, 'vector' if half == 0 else 'scalar', KT4[:, q, half * 4:(half + 1) * 4, :], tb, [('ps', 6 + half)], [('KT4', q, half)])
                for q in range(NG):
                    bq = NG * g + q
                    Sq = k.ps[32 * q:32 * q + 4, 512 * 4: 512 * 6].rearrange("p (h m) -> p h m", h=4)
                    for h in range(4):
                        MM(P, [(Sq[:, h, :], qT[:, 2 * h + dc, bq * 4:(bq + 1) * 4], KT4[:, q, 2 * h + dc, :], dc == 0, dc == 1, (0, 32 * q)) for dc in range(2)],
                           qTk + [('KT4', q, 0), ('KT4', q, 1)], [('ps', 4 + h // 2)])
                Sps = k.ps[0:NR_, 512 * 4: 512 * 6].rearrange("p (h m) -> p h m", h=4)
                softmax_rows(Sps, NR_, [('ps', 4), ('ps', 5)])
                tb = k.psb[:, 1024 * 0: 1024 * 1].rearrange("p (c t) -> p c t", c=8)
                TR(P, [(tb[:, 2 * h + mc, 0:NR_], pn[0:NR_, h, mc * 128:(mc + 1) * 128], k.identb[0:NR_, 0:NR_]) for h in range(4) for mc in range(2)], ['pnB', 'identb'], [('ps', 0)])
                CP(P, 'vector', PT[:, :, 0:NR_], tb[:, :, 0:NR_], [('ps', 0)], [('PTB', 0)])
                for q in range(NG):
                    oq = k.ps[32 * q:32 * q + 4, 512 * 2: 512 * 4].rearrange("p (h d) -> p h d", h=4)
                    for h in range(4):
                        MM(P, [(oq[:, h, :], PT[:, 2 * h + mc, 32 * q:32 * q + 4], Vb4[r][:, q, mc, h * 256:(h + 1) * 256], mc == 0, mc == 1, (0, 32 * q)) for mc in range(2)],
                           [('PTB', 0), ('Vb4', r, q, 0), ('Vb4', r, q, 1)], [('ps', 2 + h // 2)])
                CP(P, 'scalar', ob[0:NR_, :], k.ps[0:NR_, 512 * 2: 512 * 4], [('ps', 2), ('ps', 3)], ['obB'])
                tb2 = k.psb[:, 1024 * 1: 1024 * 2].rearrange("p (c t) -> p c t", c=8)
                TR(P, [(tb2[:, c, 0:NR_], ob[0:NR_, c * 128:(c + 1) * 128], k.identb[0:NR_, 0:NR_]) for c in range(8)], ['obB', 'identb'], [('ps', 1)])
                CP(P, 'vector', oT[:, :, 4 * NG * g:4 * NG * (g + 1)].rearrange("p c (q t) -> p c q t", t=4),
                   tb2[:, :, :].rearrange("p c (q u) -> p c q u", u=32)[:, :, 0:NG, 0:4], [('ps', 1)], [('oTB', 'c')])
            oproj(T, NS, x_src, x_dst, [('oTB', 'c')])


XBC0 = 1024
DT0 = 2560
VP0 = 2576
DIN = 3600
ST_A = 256


def phase_A(P, k, es, io, x_src, x_dst, do_prompt=True, do_sample=True):
    win = P.sb([128, 8, DIN], BF16, "win", es)
    wdt3 = P.sb([128, 8, 96], BF16, "wdt3", es)
    wout = P.sb([128, 16, D], BF16, "wout", es)
    wpool = P.sb([128, 4, 2, 256], BF16, "wpool", es)
    cb2 = P.sb([128, 2048], BF16, "cstb2", es)
    P.dma('gpsimd', cb2[:], io['consts'][:, CST_SMALL + 256:CST_SMALL + 256 + 2048], writes=['cstb'], sem='cstb2', max_dma_last_dim=4096)
    k.e2b = cb2[:, 0:2048]
    win_d = io['w_in'].rearrange("(c p) n -> p c n", p=128)
    MEMSET(P, 'vector', wdt3[:], 0.0, ['wdt3'])
    for nm_, c0, c1 in (('xbc', XBC0, XBC0 + 768), ('xbc2', XBC0 + 768, DT0 + 16), ('z', 0, 1024), ('vp', VP0, DIN)):
        P.dma_multi('gpsimd', [(win[:, :, c0:c1], win_d[:, :, c0:c1])], [('win', nm_)], ('win', nm_), max_dma_last_dim=4096)
    for r in range(3):
        P.dma('gpsimd', wdt3[:, :, 32 * r:32 * r + 16], win_d[:, :, DT0:DT0 + 16], reads=['wdt3'], writes=[('wdt3', r)], sem=('wdt3', r))
    wout_d = io['w_out'].rearrange("(c p) n -> p c n", p=128)
    for gq in range(2):
        P.dma_multi('gpsimd', [(wout[:, 8 * gq:8 * gq + 8, :], wout_d[:, 8 * gq:8 * gq + 8, :])], [('wout', c) for c in range(8 * gq, 8 * gq + 8)], ('wout', gq), max_dma_last_dim=4096)
    wp_d = io['w_pool'].rearrange("g (cc p) d -> p g cc d", p=128)
    P.dma_multi('gpsimd', [(wpool[:, 0:2, :, :], wp_d[:, 0:2, :, :]), (wpool[:, 2:4, :, :], wp_d[:, 2:4, :, :])], [('wpool', g) for g in range(4)], 'wpool')
    wink = [('win', nm_) for nm_ in ('xbc', 'xbc2', 'z', 'vp')]
    wdtk = [('wdt3', r) for r in range(3)]
    woutk = [('wout', c) for c in range(16)]
    wpk = [('wpool', g) for g in range(4)]

    gA = P.sb([128, 8], F32, "gA", es)
    gY = P.sb([128, 8], F32, "gY", es)
    psc = P.sb([128, 8], F32, "psc", es)
    load_fm(P, k, io['norm_mix'].rearrange("(c p) -> c p", p=128), 8, gA[:, :], 'gA', 'gA', es)
    load_fm(P, k, io['ssm_norm'].rearrange("(c p) -> c p", p=128), 8, gY[:, :], 'gY', 'gY', es)
    load_fm(P, k, io['pool_scale'].rearrange("(c p) -> c p", p=128), 8, psc[:, :], 'psc', 'psc', es)
    cw = P.sb([128, 4, 12], F32, "cwA", es)
    cb = P.sb([128, 12], F32, "cbA", es)
    cwd = io['ssm_conv_w'].rearrange("k (c p) -> k c p", p=128)
    for kk in range(4):
        load_fm(P, k, cwd[kk], 12, cw[:, kk, :], ('cwA', kk), 'cwA%d' % kk, es)
    load_fm(P, k, io['ssm_conv_b'].rearrange("(c p) -> c p", p=128), 12, cb[:, :], 'cbA', 'cbA', es)
    cwk = [('cwA', kk) for kk in range(4)] + ['cbA']
    TS(P, 'vector', cw[:], cw[:], 0.5, None, ALU.mult, None, cwk, cwk[:4])
    TS(P, 'vector', cb[:], cb[:], 0.5, None, ALU.mult, None, ['cbA'], ['cbA'])
    hp = P.sb([128, 4], F32, "hpA", es)
    MEMSET(P, 'vector', hp[:], 0.0, ['hpA'])
    for r in range(3):
        P.dma('sync', hp[32 * r:32 * r + 16, 0:1], io['ssm_dt_bias'].rearrange("(h o) -> h o", o=1), reads=['hpA'], writes=[('hpA', r)], sem=('hpA', r))
        P.dma('sync', hp[32 * r:32 * r + 16, 1:2], io['ssm_a_log'].rearrange("(h o) -> h o", o=1), reads=['hpA'], writes=[('hpA', r)], sem=('hpA', r))
    hpk = [('hpA', r) for r in range(3)]
    ACTF(P, hp[0:96, 2:3], hp[0:96, 1:2], AF.Exp, hpk, ['hpA2'])
    TS(P, 'vector', hp[0:96, 2:3], hp[0:96, 2:3], -1.0, None, ALU.mult, None, ['hpA2'], ['hpA2'])
    hb16 = P.sb([128, 48], F32, "hb16", es)
    P.dma('sync', hb16[:, 0:16], io['ssm_d'].partition_broadcast(128), writes=['hb16d'], sem='hb16d')
    P.dma('sync', hb16[:, 16:32], io['ssm_a_log'].partition_broadcast(128), writes=['hb16a'], sem='hb16a')
    ACTF(P, hb16[:, 32:48], hb16[:, 16:32], AF.Exp, ['hb16a'], ['hb16A'])
    TS(P, 'vector', hb16[:, 32:48], hb16[:, 32:48], -1.0, None, ALU.mult, None, ['hb16A'], ['hb16A'])
    Dbc = hb16[:, 0:16]
    Abc = hb16[:, 32:48]
    onesf = k.cs('ones')

    NSUB = ST_A // 128
    xt = [P.sb([128, D], F32, "xtA%d" % i, es) for i in range(2)]
    hb = k.junk
    st = P.sb([128, 64], F32, "stA", es)
    hT = P.sb([128, 8, ST_A], BF16, "hTA", es)
    xcT2 = [P.sb([128, 12, ST_A], BF16, "xcT0", es), None]
    acc = [P.sb([128, ST_A], F32, "accA%d" % i, es) for i in range(2)]
    th = [P.sb([128, ST_A], F32, "thA0", es)] * 2
    sz2 = [P.sb([128, NSUB, D], BF16, "szA0", es), None]
    pl = P.sb([128, 4, ST_A], BF16, "plA", es)
    pmT2 = [P.sb([128, 8, ST_A], BF16, "pmT0", es), None]
    ynT = P.sb([128, 8, ST_A], BF16, "ynT", es)
    dt3 = P.sb([128, ST_A], F32, "dt3A", es)
    a3 = P.sb([128, ST_A], F32, "a3A", es)
    d1 = a3
    acs3 = P.sb([128, ST_A], F32, "acs3A", es)
    stk2 = [P.sb([128, ST_A], F32, "stkA0", es), None]
    hl2 = [P.sb([128, ST_A], BF16, "hlA0", es), None]
    ptmp = P.sb([128, 2, 320], F32, "ptmpA", es)
    zt = P.sb([128, 512], F32, "ztA", es)
    tk = P.sb([128, 128], F32, "tkA", es)
    sml = P.sb([128, 64], F32, "smlA", es)
    xtok = P.sb([128, D], BF16, "xtokA", es)
    xdt = P.sb([128, D], BF16, "xdtA", es)
    xdtE = P.sb([128, D], BF16, "xdtEA", es)
    Btok = P.sb([128, 256], BF16, "BtokA", es)
    dcy = [P.sb([128, 4, 128], F32, "dcyA%d" % i, es) for i in range(2)]
    MT = P.sb([128, 16, 128], BF16, "MTA", es)
    y1 = P.sb([128, D], F32, "y1A", es)
    y2 = P.sb([128, D], F32, "y2A", es)
    yn = P.sb([128, D], BF16, "ynA", es)
    hst = P.sb([128, D], F32, "hstA", es)
    hbf = P.sb([128, D], BF16, "hbfA", es)
    cnt = {'x': 0, 'st': 0, 'p': 0, 'a': 0, 'd': 0}
    MEMSET(P, 'vector', y1[:], 0.0, [('y1A', 0), ('y1A', 1)])
    MEMSET(P, 'vector', ptmp[:], 0.0, [('ptmpA', 0), ('ptmpA', 1)])

    def v3(ap, nseq, a, b):
        return ap.rearrange("p (s l) -> p s l", s=nseq)[:, :, a:b]

    def run_group(tok0, nseq, L, extx, extv, first, negb, rsp, h0_bf_key, gname, sample_fn=None, par=0):
        xcT, sz, pmT, stk, hl = xcT2[par], sz2[par], pmT2[par], stk2[par], hl2[par]
        KP = 'p%d' % par
        ntok = nseq * L
        nsub = (ntok + 127) // 128
        exk = [('extx' + gname, c) for c in range(12)]
        evk = [('extv' + gname, c) for c in range(8)]
        for j in range(nsub):
            n = min(128, ntok - j * 128)
            i = cnt['x'] % 2
            cnt['x'] += 1
            P.dma('sync', xt[i][0:n, :], x_src[tok0 + j * 128: tok0 + j * 128 + n, :], writes=[('xtA', i)], sem=('xtA', i))
            col = (cnt['st'] % 8) * 4
            cnt['st'] += 1
            norm_T(P, k, xt[i][0:n, :], n, [('xtA', i)], gA[:, :], 'gA', hT[:, :, j * 128: j * 128 + n], ('hTA', j), st, col, hb, 'junk', 2)
        hTk = [('hTA', j) for j in range(nsub)]

        def proj_fm(col0, width, lhs_w, wkeys):
            b = cnt['p'] % 2
            cnt['p'] += 1
            pb = k.ps[0:width, 512 * b: 512 * b + ntok]
            MM(P, [(pb, lhs_w[:, kc, col0:col0 + width], hT[:, kc, 0:ntok], kc == 0, kc == 7) for kc in range(8)], wkeys + hTk, [('ps', b)])
            return pb, ('ps', b)

        xck = [('xcT' + KP, c) for c in range(12)]

        def sec_xbc():
            halo, fixx = extx
            fk = ['fixx0' + gname, 'fixx1' + gname, 'fixx2' + gname, 'fixt' + gname]
            wv = [bcast(cw[:, kk, :], 2, nseq) for kk in range(4)]
            hh = [halo[:, :, :, r_] for r_ in range(3)]
            tmpf = fixx[:, :, :, 3]
            TT(P, 'gpsimd', fixx[:, :, :, 0], hh[0], wv[0], ALU.mult, exk + cwk, [fk[0]])
            TT(P, 'gpsimd', tmpf, hh[1], wv[1], ALU.mult, exk + cwk, [fk[3]])
            TT(P, 'gpsimd', fixx[:, :, :, 0], fixx[:, :, :, 0], tmpf, ALU.add, [fk[0], fk[3]], [fk[0]])
            TT(P, 'gpsimd', tmpf, hh[2], wv[2], ALU.mult, exk + cwk + [fk[0]], [fk[3]])
            TT(P, 'gpsimd', fixx[:, :, :, 0], fixx[:, :, :, 0], tmpf, ALU.add, [fk[0], fk[3]], [fk[0]])
            TT(P, 'gpsimd', fixx[:, :, :, 1], hh[1], wv[0], ALU.mult, exk + cwk, [fk[1]])
            TT(P, 'gpsimd', tmpf, hh[2], wv[1], ALU.mult, exk + cwk + [fk[0]], [fk[3]])
            TT(P, 'gpsimd', fixx[:, :, :, 1], fixx[:, :, :, 1], tmpf, ALU.add, [fk[1], fk[3]], [fk[1]])
            TT(P, 'gpsimd', fixx[:, :, :, 2], hh[2], wv[0], ALU.mult, exk + cwk + [fk[1]], [fk[2]])
            fks = fk[:3]

            def xbc_stage1(c):
                pb, pk = proj_fm(XBC0 + c * 128, 128, win, [('win', 'xbc' if c < 6 else 'xbc2')])
                r = c % 2
                a_ap = v3(acc[r][:, 0:ntok], nseq, 0, L)
                p3 = v3(pb, nseq, 0, L)
                ACTF(P, a_ap, p3, AF.Identity, [pk] + cwk, [('accA', r)], bias=cb[:, c:c + 1], scale=cw[:, 3, c:c + 1])
                for sh in (1, 2, 3):
                    STT(P, a_ap[:, :, sh:L], p3[:, :, 0:L - sh], cw[:, 3 - sh, c:c + 1], a_ap[:, :, sh:L], ALU.mult, ALU.add, [pk, ('accA', r)] + cwk, [('accA', r)])
                TT(P, 'vector', a_ap[:, :, 0:3], a_ap[:, :, 0:3], fixx[:, c, :, 0:3], ALU.add, [('accA', r)] + fks, [('accA', r)])
                CP(P, 'vector', halo[:, c, :, :], p3[:, :, L - 3:L], [pk] + fk, [exk[c]])

            def xbc_stage2(c):
                r = c % 2
                a_ap = v3(acc[r][:, 0:ntok], nseq, 0, L)
                t_ap = v3(th[c % 2][:, 0:ntok], nseq, 0, L)
                ACTF(P, t_ap, a_ap, AF.Tanh, [('accA', r)], [('thA', 0)])
                STT(P, v3(xcT[:, c, 0:ntok], nseq, 0, L), t_ap, 1.0, a_ap, ALU.add, ALU.mult, [('thA', 0), ('accA', r)], [('xcT' + KP, c)])
            for c in range(13):
                if c < 12:
                    xbc_stage1(c)
                if c >= 1:
                    xbc_stage2(c - 1)
                yield
            xck = [('xcT' + KP, c) for c in range(12)]

            pb, pk = proj_fm(0, 96, wdt3, wdtk)
            ACTF(P, d1[0:96, 0:ntok], pb, AF.Exp, [pk] + hpk, ['a3A'], bias=hp[0:96, 0:1])
            ACTF(P, dt3[0:96, 0:ntok], d1[0:96, 0:ntok], AF.Ln, ['a3A'], ['dt3A'], bias=1.0)
            TS(P, 'vector', a3[0:96, 0:ntok], dt3[0:96, 0:ntok], hp[0:96, 2:3], None, ALU.mult, None, ['dt3A', 'hpA2'], ['a3A'])
            P.V(lambda e: e.tensor_tensor_scan(out=acs3[0:96, 0:ntok], data0=rsp[0:96, 0:ntok], data1=a3[0:96, 0:ntok], initial=0.0, op0=ALU.mult, op1=ALU.add),
                reads=['a3A', 'cst'], writes=['acs3A'])
            CP(P, 'vector', stk[0:96, 0:ntok], dt3[0:96, 0:ntok], ['dt3A'], ['stkA' + KP])
            CP(P, 'vector', stk[32:48, 0:ntok], acs3[32:48, 0:ntok], ['acs3A', 'stkA' + KP], ['stkA' + KP])
            nch = ntok // min(L, 128)
            Lc = min(L, 128)
            a3v = acs3[64:80, 0:ntok].rearrange("p (s l) -> p s l", l=Lc)
            TT(P, 'vector', stk[64:80, 0:ntok].rearrange("p (s l) -> p s l", l=Lc), a3v, bcast(a3v[:, :, Lc - 1], 2, Lc), ALU.subtract, ['acs3A', 'stkA' + KP], ['stkA' + KP])
            CP(P, 'vector', hl[0:96, 0:ntok], acs3[0:96, 0:ntok], ['acs3A'], ['hlA' + KP])
            TT(P, 'vector', hl[32:48, 0:ntok], acs3[32:48, 0:ntok], hl[32:48, 0:ntok], ALU.subtract, ['acs3A', 'hlA' + KP], ['hlA' + KP])

            yield
            yield

        def sec_z():
            for j in range(nsub):
                n = min(128, ntok - j * 128)
                for hf in range(2):
                    b = cnt['p'] % 2
                    cnt['p'] += 1
                    pb = k.ps[0:n, 512 * b: 512 * b + 512]
                    MM(P, [(pb, hT[:, kc, j * 128: j * 128 + n], win[:, kc, hf * 512:(hf + 1) * 512], kc == 0, kc == 7) for kc in range(8)], [('win', 'z')] + hTk, [('ps', b)])
                    r = cnt['a'] % 2
                    cnt['a'] += 1
                    ACTF(P, zt[0:n, :], pb, AF.Tanh, [('ps', b)], ['ztA'], scale=0.5)
                    STT(P, sz[0:n, j, hf * 512:(hf + 1) * 512], zt[0:n, :], 1.0, pb, ALU.add, ALU.mult, ['ztA', ('ps', b)], [('szA' + KP, j, hf)])
                    yield

            yield

        def sec_pool_a():
            if nseq == 1 and not first:
                CP(P, 'vector', extv[:, :, 0, 0:15], extv[:, :, 0, L:L + 15], evk, evk)
            for c in range(8):
                pb, pk = proj_fm(VP0 + c * 128, 128, win, [('win', 'vp')])
                CP(P, 'scalar', extv[:, c, :, 15:L + 15], v3(pb, nseq, 0, L), [pk], [evk[c]])
                if c % 2 == 1:
                    yield
            yield

        def sec_pool():
            def wpool_chunk(co):
                g, dh = co // 2, co % 2
                b = cnt['p'] % 2
                cnt['p'] += 1
                pb = k.ps[:, 512 * b: 512 * b + ntok]
                MM(P, [(pb, wpool[:, g, cc, dh * 128:(dh + 1) * 128], pl[:, (2 * g + cc) % 4, 0:ntok], cc == 0, cc == 1) for cc in range(2)],
                   wpk + [('plA', (2 * g) % 4), ('plA', (2 * g + 1) % 4)], [('ps', b)])
                ACTF(P, pmT[:, co, 0:ntok], pb, AF.Identity, [('ps', b), 'psc'], [('pmT' + KP, co)], scale=psc[:, co:co + 1])

            for c in range(8):
                gi = c // 2
                w = 2 << gi
                cur = extv[:, c, :, :]
                tot = L + 15
                step = 1
                bufs = [ptmp[:, 0, 0:nseq * tot].rearrange("p (s l) -> p s l", s=nseq), ptmp[:, 1, 0:nseq * tot].rearrange("p (s l) -> p s l", s=nseq)]
                bkeys = [('ptmpA', 0), ('ptmpA', 1)]
                bi = 0
                ckeys = [evk[c]]
                while step < w:
                    o = bufs[bi]
                    TT(P, 'gpsimd', o[:, :, step:tot], cur[:, :, step:tot], cur[:, :, 0:tot - step], ALU.add, ckeys, [bkeys[bi]])
                    cur = o
                    ckeys = [bkeys[bi]]
                    bi ^= 1
                    step *= 2
                o = bufs[bi]
                TS(P, 'gpsimd', o[:, :, 15:tot], cur[:, :, 15:tot], 1.0 / w, 0.0, ALU.mult, ALU.add, ckeys, [bkeys[bi]])
                if first and nseq == 1:
                    TT(P, 'gpsimd', o[:, :, 15:31], cur[:, :, 15:31], k.cs('csc')[:, gi * 16:(gi + 1) * 16].unsqueeze(1), ALU.mult, ckeys + ['cst'], [bkeys[bi]])
                TT(P, 'gpsimd', v3(pl[:, c % 4, 0:ntok], nseq, 0, L), o[:, :, 15:tot], extv[:, c, :, 15:tot], ALU.subtract, [bkeys[bi], evk[c]], [('plA', c % 4)])
                yield
                if c % 2 == 1:
                    for co in (c - 1, c):
                        wpool_chunk(co)
                    yield
            yield

        yield from sec_xbc()
        yield from sec_z()
        yield 'POOLA'
        yield from sec_pool_a()
        pmk = [('pmT' + KP, c) for c in range(8)]
        gpb = sec_pool()

        def adv(nsteps=1):
            for _ in range(nsteps):
                try:
                    next(gpb)
                except StopIteration:
                    return

        yield 'SPLIT'
        for j in range(nsub):
            n = min(128, ntok - j * 128)
            js = slice(j * 128, j * 128 + n)
            xb = k.psb[0:n, 1024 * 2: 1024 * 2 + 1024].rearrange("p (c t) -> p c t", c=8)
            TR(P, [(xb[:, c, :], xcT[:, c, js], k.identb) for c in range(8)], xck + ['identb'], [('ps', 2)])
            sp = k.ps[0:n, 512 * 3: 512 * 3 + 96]
            TR(P, [(sp, stk[0:96, js], k.identf[0:96, 0:96])], ['stkA' + KP, 'cst'], [('ps', 3)])
            CP(P, 'vector', tk[0:n, 0:96], sp, [('ps', 3)], ['tkA'])
            bb = k.psb[0:n, 1024 * 3 + 256: 1024 * 3 + 512].rearrange("p (c t) -> p c t", c=2)
            TR(P, [(bb[:, g, :], xcT[:, 8 + g, js], k.identb) for g in range(2)], xck + ['identb'], [('ps', 3)])
            CP(P, 'vector', Btok[0:n, :].rearrange("p (c t) -> p c t", c=2), bb, [('ps', 3)], ['BtokA'])
            TS(P, 'vector', sml[0:n, 0:16], tk[0:n, 32:48], -1.0, None, ALU.mult, None, ['tkA'], ['nacs'])
            ACTF(P, sml[0:n, 16:32], tk[0:n, 32:48], AF.Exp, ['tkA'], ['eacs'])
            ACTF(P, sml[0:n, 32:48], tk[0:n, 64:80], AF.Exp, ['tkA'], ['dend'], scale=-1.0)
            TT(P, 'vector', tk[0:n, 96:112], tk[0:n, 0:16], Abc[0:n, :], ALU.mult, ['tkA', 'hb16A'], ['atok'])
            CP(P, 'scalar', xtok[0:n, :].rearrange("p (c t) -> p c t", c=8), xb, [('ps', 2)], ['xtokA'])
            TT(P, 'vector', xdt[0:n, :].rearrange("p (h q) -> p h q", h=16), k.psb[0:n, 1024 * 2: 1024 * 2 + 1024].rearrange("p (h q) -> p h q", h=16),
               bcast(tk[0:n, 0:16], 2, 64), ALU.mult, [('ps', 2), 'tkA'], ['xdtA'])
            TT(P, 'gpsimd', xdtE[0:n, :].rearrange("p (h q) -> p h q", h=16), xdt[0:n, :].rearrange("p (h q) -> p h q", h=16),
               bcast(sml[0:n, 32:48], 2, 64), ALU.mult, ['xdtA', 'dend'], ['xdtEA'])
            adv(2)
            yield
            cbp = k.ps[0:n, 512 * 3: 512 * 3 + 2 * n].rearrange("p (g l) -> p g l", g=2)
            MM(P, [(cbp[:, g, :], xcT[:, 8 + g, js], xcT[:, 10 + g, js], True, True) for g in range(2)], xck, [('ps', 3)])
            for q in range(4):
                b = 4 + q % 2
                dp = k.ps[0:n, 512 * b: 512 * b + 4 * n].rearrange("p (h l) -> p h l", h=4)
                items = []
                for hh in range(4):
                    h = 4 * q + hh
                    items.append((dp[:, hh, :], k.e2b[0:64, h * 128: h * 128 + n], hl[0:64, js], True, False))
                    items.append((dp[:, hh, :], k.identb[0:n, 0:n], negb[0:n, 0:n], False, True))
                MM(P, items, ['hlA' + KP, 'cstb', 'identb'], [('ps', b)])
                r = cnt['d'] % 2
                cnt['d'] += 1
                for hh in range(4):
                    h = 4 * q + hh
                    ACTF(P, dcy[r][0:n, hh, 0:n], dp[:, hh, :], AF.Exp, [('ps', b), 'nacs'], [('dcyA', r, hh)], bias=sml[0:n, h:h + 1])
                g = q // 2
                TT(P, 'vector', MT[0:n, 4 * q:4 * q + 4, 0:n], dcy[r][0:n, :, 0:n], bcast(cbp[:, g, :], 1, 4), ALU.mult,
                   [('dcyA', r, hh) for hh in range(4)] + [('ps', 3)], [('MTA', q)])
                adv(2)
                yield
            MTk = [('MTA', q) for q in range(4)]
            yd = k.ps[0:n, 512 * 6: 512 * 8].rearrange("p (h q) -> p h q", h=16)
            for half in range(2):
                MM(P, [(yd[:, h, :], MT[0:n, h, 0:n], xdt[0:n, h * 64:(h + 1) * 64], True, True) for h in range(8 * half, 8 * half + 8)],
                   MTk + ['xdtA'], [('ps', 6 + half)])
            adv(2)
            yield
            yo = k.ps[0:n, 512 * 4: 512 * 6]
            if sample_fn is not None:
                sample_fn(MTk, xck, xcT, hl, 'hlA' + KP)
                h0_bf_key = 'sample'
                TT(P, 'vector', y1[0:n, :].rearrange("p (h q) -> p h q", h=16), yo.rearrange("p (h q) -> p h q", h=16), bcast(sml[0:n, 16:32], 2, 64), ALU.mult,
                   [('ps', 4), ('ps', 5), 'eacs'], [('y1A', 0), ('y1A', 1)])
            elif h0_bf_key is not None:
                for g in range(2):
                    MM(P, [(yo[:, g * 512:(g + 1) * 512], xcT[:, 10 + g, js], hbf[:, g * 512:(g + 1) * 512], True, True)], xck + [h0_bf_key], [('ps', 4 + g)])
                TT(P, 'vector', y1[0:n, :].rearrange("p (h q) -> p h q", h=16), yo.rearrange("p (h q) -> p h q", h=16), bcast(sml[0:n, 16:32], 2, 64), ALU.mult,
                   [('ps', 4), ('ps', 5), 'eacs'], [('y1A', 0), ('y1A', 1)])
            TT(P, 'gpsimd', y2[0:n, :].rearrange("p (h q) -> p h q", h=16), xtok[0:n, :].rearrange("p (h q) -> p h q", h=16), bcast(Dbc[0:n, :], 2, 64), ALU.mult,
               ['xtokA', 'hb16d'], [('y2A', 0), ('y2A', 1)])
            if h0_bf_key is not None:
                TT(P, 'gpsimd', y1[0:n, :], y1[0:n, :], y2[0:n, :], ALU.add, [('y1A', 0), ('y1A', 1), ('y2A', 0), ('y2A', 1)], [('y1A', 0), ('y1A', 1)])
                ysrc, ysk = y1, [('y1A', 0), ('y1A', 1)]
            else:
                ysrc, ysk = y2, [('y2A', 0), ('y2A', 1)]
            TT(P, 'vector', y1[0:n, :], k.ps[0:n, 512 * 6: 512 * 8], ysrc[0:n, :], ALU.add, [('ps', 6), ('ps', 7)] + ysk, [('y1A', 0), ('y1A', 1)])
            TT(P, 'vector', y1[0:n, :], y1[0:n, :], sz[0:n, j, :], ALU.mult, [('y1A', 0), ('y1A', 1), ('szA' + KP, j, 0), ('szA' + KP, j, 1)], [('y1A', 0), ('y1A', 1)])
            adv(2)
            yield
            for g in range(2):
                col = (cnt['st'] % 8) * 4
                cnt['st'] += 1
                r_ap, ks = rstd_op(P, k, y1[0:n, g * 512:(g + 1) * 512], n, [('y1A', 0), ('y1A', 1)], st, col, scale=0.25 / 512, extra=0.5)
                TS(P, 'vector', yn[0:n, g * 512:(g + 1) * 512], y1[0:n, g * 512:(g + 1) * 512], r_ap, None, ALU.mult, None, [('y1A', 0), ('y1A', 1), ks], [('ynA', g)])
            yb = k.psb[:, 1024 * 2: 1024 * 2 + 1024].rearrange("p (c t) -> p c t", c=8)
            TR(P, [(yb[:, c, 0:n], yn[0:n, c * 128:(c + 1) * 128], k.identb[0:n, 0:n]) for c in range(8)], [('ynA', 0), ('ynA', 1), 'identb'], [('ps', 2)])
            TT(P, 'vector', ynT[:, :, js], yb[:, :, 0:n], bcast(gY[:, :], 2, n), ALU.mult, [('ps', 2), 'gY'], [('ynT', j)])
            adv(2)
            yield
            if nseq == 1:
                sp2 = k.ps[:, 512 * 6: 512 * 8]
                for g in range(2):
                    MM(P, [(sp2[:, g * 512:(g + 1) * 512], Btok[0:n, g * 128:(g + 1) * 128], xdtE[0:n, g * 512:(g + 1) * 512], True, True)], ['BtokA', 'xdtEA'], [('ps', 6 + g)])
                cdp = k.ps[:, 512 * 3 + 256: 512 * 3 + 272]
                MM(P, [(cdp, onesf[0:n, :], tk[0:n, 96:112], True, True)], ['cst', 'atok'], [('ps', 3)])
                ACTF(P, tk[:, 112:128], cdp, AF.Exp, [('ps', 3)], ['cdA'])
                if h0_bf_key is not None:
                    TT(P, 'gpsimd', y2[:, :].rearrange("p (h q) -> p h q", h=16), hst[:, :].rearrange("p (h q) -> p h q", h=16), bcast(tk[:, 112:128], 2, 64), ALU.mult,
                       ['hstA', 'cdA'], [('y2A', 0), ('y2A', 1)])
                    TT(P, 'vector', hst[:, :], sp2, y2[:, :], ALU.add, [('ps', 6), ('ps', 7), ('y2A', 0), ('y2A', 1)], ['hstA'])
                else:
                    CP(P, 'vector', hst[:, :], sp2, [('ps', 6), ('ps', 7)], ['hstA'])
                CP(P, 'scalar', hbf[:, :], hst[:, :], ['hstA'], ['hbfA'])
                h0_bf_key = 'hbfA'
            adv(2)
            yield
        ynk = [('ynT', j) for j in range(nsub)]
        adv(100)
        for j in range(nsub):
            n = min(128, ntok - j * 128)
            js = slice(j * 128, j * 128 + n)
            i = cnt['x'] % 2
            cnt['x'] += 1
            P.dma('sync', xt[i][0:n, :], x_src[tok0 + j * 128: tok0 + j * 128 + n, :], writes=[('xtA', i)], sem=('xtA', i))
            for hf in range(2):
                b = cnt['p'] % 2
                cnt['p'] += 1
                pb = k.ps[0:n, 512 * b: 512 * b + 512]
                items = [(pb, ynT[:, c, js], wout[:, c, hf * 512:(hf + 1) * 512], c == 0, False) for c in range(8)]
                items += [(pb, pmT[:, c, js], wout[:, 8 + c, hf * 512:(hf + 1) * 512], False, c == 7) for c in range(8)]
                MM(P, items, ynk + pmk + woutk, [('ps', b)])
                TT(P, 'vector', xt[i][0:n, hf * 512:(hf + 1) * 512], pb, xt[i][0:n, hf * 512:(hf + 1) * 512], ALU.add, [('ps', b), ('xtA', i)], [('xtA', i)])
            P.dma('sync', x_dst[tok0 + j * 128: tok0 + j * 128 + n, :], xt[i][0:n, :], reads=[('xtA', i)], sem=('x1o', i))
            adv(2)
            yield

    def rows_out(src_fm, nrows_per, nchunks, skeys, dst_rows, tag):
        R = nrows_per
        for q in range(0, nchunks, 4):
            m = min(4, nchunks - q)
            pb = k.ps[0:R, 512 * 3: 512 * 3 + m * 128]
            for u in range(m):
                sap = src_fm(q + u)
                if len(sap.shape) > 2:
                    CP(P, 'vector', tk[:, 0:R].rearrange("p (a b) -> p a b", a=sap.shape[1]), sap, skeys, ['tkA'])
                    sap, sk2 = tk[:, 0:R], ['tkA']
                else:
                    sk2 = skeys
                TR(P, [(pb[:, u * 128:(u + 1) * 128], sap, k.identf)], sk2 + ['cst'], [('ps', 3)])
            CP(P, 'vector', y2[0:R, 0:m * 128], pb, [('ps', 3)], [('y2A', 0), ('y2A', 1)])
            P.dma('sync', dst_rows[:, q * 128:(q + m) * 128], y2[0:R, 0:m * 128], reads=[('y2A', 0), ('y2A', 1)], sem='rows' + tag)

    if do_prompt:
        with ExitStack() as esp:
            xcT2[1] = P.sb([128, 12, ST_A], BF16, "xcT1", esp)
            sz2[1] = P.sb([128, NSUB, D], BF16, "szA1", esp)
            pmT2[1] = P.sb([128, 8, ST_A], BF16, "pmT1", esp)
            stk2[1] = P.sb([128, ST_A], F32, "stkA1", esp)
            hl2[1] = P.sb([128, ST_A], BF16, "hlA1", esp)
            haloP = P.sb([128, 12, 1, 3], F32, "haloxP", esp)
            fixP = P.sb([128, 12, 1, 4], F32, "fixxP", esp)
            extx = (haloP, fixP)
            extv = P.sb([128, 8, 1, ST_A + 15], F32, "extvP", esp)
            MEMSET(P, 'vector', haloP[:], 0.0, [('extxP', c) for c in range(12)])
            MEMSET(P, 'vector', extv[:], 0.0, [('extvP', c) for c in range(8)])
            NSUP = T // ST_A
            RB, RF = 1, 1
            gens = [run_group(S * ST_A, 1, ST_A, extx, extv, S == 0, k.negcb, k.cs('rsp'), (None if S == 0 else 'hbfA'), 'P', par=S % 2) for S in range(NSUP)]

            hold = {}

            def step(g, front, back_alive=False):
                if front and hold.get(id(g)) and back_alive:
                    return True
                hold.pop(id(g), None)
                try:
                    v = next(g)
                except StopIteration:
                    return False
                if front and v == 'POOLA' and back_alive:
                    hold[id(g)] = True
                return not (front and v == 'SPLIT')
            while step(gens[0], True):
                pass
            for S in range(NSUP):
                gb = gens[S]
                gf = gens[S + 1] if S + 1 < NSUP else None
                ab, af = True, gf is not None
                while ab or af:
                    for _ in range(RB):
                        if ab:
                            ab = step(gb, False)
                    for _ in range(RF):
                        if af:
                            af = step(gf, True, ab)
            rows_out(lambda c: haloP[:, c, 0, :], 3, 12, [('extxP', c) for c in range(12)], io['conv_prompt'], 'cp')
            rows_out(lambda c: extv[:, c, 0, ST_A:ST_A + 15], 15, 8, [('extvP', c) for c in range(8)], io['pool_prompt'], 'pp')
            for half in range(2):
                pb = k.ps[:, 512 * (4 + half): 512 * (5 + half)]
                TR(P, [(pb[:, u * 128:(u + 1) * 128], hst[:, (4 * half + u) * 128:(4 * half + u + 1) * 128], k.identf) for u in range(4)], ['hstA', 'cst'], [('ps', 4 + half)])
                CP(P, 'vector', y1[:, half * 512:(half + 1) * 512], pb, [('ps', 4 + half)], [('y1A', half)])
            P.dma('sync', io['ssm_prompt'].rearrange("(c p) n -> p c n", p=128), y1[:, :].rearrange("p (c n) -> p c n", c=8), reads=[('y1A', 0), ('y1A', 1)], sem='ssmP')
            P.barrier()

    if do_sample:
        with ExitStack() as ess:
            haloS = P.sb([128, 12, 16, 3], F32, "haloxS", ess)
            fixS = P.sb([128, 12, 16, 4], F32, "fixxS", ess)
            extx = (haloS, fixS)
            extv = P.sb([128, 8, 16, 19], F32, "extvS", ess)
            h0a = P.sb([128, 8, 128], F32, "h0S", ess)
            cdT = P.sb([128, 8, 16], F32, "cdTS", ess)
            cb3 = P.sb([128, 1024], BF16, "cstb3", ess)
            P.dma('gpsimd', cb3[:], io['consts'][:, CST_SMALL + 256 + 2048:CST_SMALL + 256 + 3072], writes=['cstb3'], sem='cstb3', max_dma_last_dim=4096)
            k.expdb = cb3[:, :]
            exk = [('extxS', c) for c in range(12)]
            evk = [('extvS', c) for c in range(8)]
            y1k = [('y1A', 0), ('y1A', 1)]
            y2k = [('y2A', 0), ('y2A', 1)]
            scv = io['state_ssm_conv'].rearrange("b r c -> (b r) c")
            P.dma('sync', y1[0:48, :], scv[:, 0:1024], writes=y1k, sem='stS1')
            P.dma('sync', y2[0:48, 0:512], scv[:, 1024:1536], writes=y2k, sem='stS2')
            for c in range(12):
                src = y1[0:48, c * 128:(c + 1) * 128] if c < 8 else y2[0:48, (c - 8) * 128:(c - 7) * 128]
                bq_ = 3 - c % 2
                pb = k.ps[:, 512 * bq_: 512 * bq_ + 48]
                TR(P, [(pb, src, k.identf[0:48, 0:48])], y1k + y2k + ['cst'], [('ps', bq_)])
                CP(P, 'vector' if c % 2 == 0 else 'scalar', haloS[:, c, :, :], pb.rearrange("p (b r) -> p b r", r=3), [('ps', bq_)], [exk[c]])
            spv = io['state_pool'].rearrange("b r c -> (b r) c")
            for half in range(2):
                P.dma('sync', y1[0:120, :], spv[half * 120:(half + 1) * 120, :], reads=y1k, writes=y1k, sem=('stS3', half))
                for c in range(8):
                    bq_ = 3 - c % 2
                    pb = k.ps[:, 512 * bq_: 512 * bq_ + 120]
                    TR(P, [(pb, y1[0:120, c * 128:(c + 1) * 128], k.identf[0:120, 0:120])], y1k + ['cst'], [('ps', bq_)])
                    CP(P, 'vector' if c % 2 == 0 else 'scalar', extv[:, c, 8 * half:8 * half + 8, 0:15], pb.rearrange("p (b r) -> p b r", r=15), [('ps', bq_)], [evk[c]])
            ssd = io['state_ssm'].rearrange("b (c q) n -> b q c n", q=128)
            sso = io['ssm_sample'].rearrange("b (c q) n -> b q c n", q=128)

            def sample_fn(MTk, xck, xcT, hl, hlk):
                n = NS
                MEMSET(P, 'vector', MT[:], 0.0, MTk)
                ctm_diag = bass.AP(MT.tensor if hasattr(MT, 'tensor') else MT, 0, [[2048, 128], [1024, 2], [68, 16], [1, 4]])
                CP(P, 'vector', ctm_diag, xcT[:, 10:12, 0:64].rearrange("p g (b t) -> p g b t", t=4), xck + MTk, MTk)
                CTm = MT[:].rearrange("p h l -> p (h l)").rearrange("p (g b t) -> p g b t", g=2, b=16)
                cdp = k.ps[:, 512 * 3: 512 * 3 + 128].rearrange("p (c b) -> p c b", c=8)
                hl_last = hl[0:64, 0:64].rearrange("p (b t) -> p b t", t=4)[:, :, 3]
                MM(P, [(cdp[:, jc, :], k.expdb[0:64, jc * 128:(jc + 1) * 128], hl_last, True, True) for jc in range(8)], [hlk, 'cstb3'], [('ps', 3)])
                ACTF(P, cdT[:, :, :], cdp, AF.Exp, [('ps', 3)], ['cdTS'])
                for b in range(16):
                    h0 = h0a[:] if b % 2 == 0 else hst[:, :].rearrange("p (c n) -> p c n", c=8)
                    h0k = 'h0S' if b % 2 == 0 else 'hstA'
                    hn = (y1 if b % 2 == 0 else y2)[:, :].rearrange("p (c n) -> p c n", c=8)
                    hnk = y1k if b % 2 == 0 else y2k
                    if b == 0:
                        P.dma('sync', h0, ssd[0], writes=[h0k], sem=('h0S', 0))
                    if b + 1 < 16:
                        h0n = h0a[:] if (b + 1) % 2 == 0 else hst[:, :].rearrange("p (c n) -> p c n", c=8)
                        P.dma('sync', h0n, ssd[b + 1], writes=['h0S' if (b + 1) % 2 == 0 else 'hstA'], sem=('h0S', (b + 1) % 2))
                    tp = k.ps[:, 512 * 2: 512 * 4]
                    for half in range(2):
                        TR(P, [(tp[:, (4 * half + u) * 128:(4 * half + u + 1) * 128], h0[:, 4 * half + u, :], k.identf) for u in range(4)], [h0k, 'cst'], [('ps', 2 + half)])
                    CP(P, 'scalar', hbf[:, 0:512], tp[:, 0:512], [('ps', 2)], [('hbfA', 0)])
                    CP(P, 'vector', hbf[:, 512:1024], tp[:, 512:1024], [('ps', 3)], [('hbfA', 1)])
                    for g in range(2):
                        MM(P, [(k.ps[0:n, 512 * (4 + g): 512 * (5 + g)], CTm[:, g, b, :], hbf[:, g * 512:(g + 1) * 512], b == 0, b == 15)], MTk + [('hbfA', g)], [('ps', 4 + g)])
                    ACTF(P, xdt[0:n, :], xdtE[0:n, :], AF.Identity, ['xdtEA', 'cst'], ['xdtA'], scale=k.cs('blk')[0:n, b:b + 1])
                    sp = k.ps[:, 0:1024].rearrange("p (c n) -> p c n", c=8)
                    for half in range(2):
                        MM(P, [(sp[:, jc, :], xdt[0:n, jc * 128:(jc + 1) * 128], Btok[0:n, (jc // 4) * 128:(jc // 4 + 1) * 128], True, True) for jc in range(4 * half, 4 * half + 4)],
                           ['xdtA', 'BtokA'], [('ps', half)])
                    TT(P, 'gpsimd', hn, h0, bcast(cdT[:, :, b], 2, 128), ALU.mult, [h0k, 'cdTS'], hnk)
                    TT(P, 'vector', hn.rearrange("p c n -> p (c n)"), hn.rearrange("p c n -> p (c n)"), k.ps[:, 0:1024], ALU.add, hnk + [('ps', 0), ('ps', 1)], hnk)
                    P.dma('sync', sso[b], hn, reads=hnk, sem=('hnS', b % 2))

            for _ in run_group(T, 16, 4, extx, extv, False, k.negsb, k.cs('rss'), None, 'S', sample_fn=sample_fn, par=0):
                pass
            cso = io['conv_sample'].rearrange("b r c -> (b r) c")
            rows_out(lambda c: haloS[:, c, :, :], 48, 12, exk, cso, 'cs')
            pso = io['pool_sample'].rearrange("b r c -> (b r) c")
            for half in range(2):
                rows_out(lambda c, half=half: extv[:, c, 8 * half:8 * half + 8, 4:19], 120, 8, evk, pso[half * 120:(half + 1) * 120, :], 'ps%d' % half)
            P.barrier()


N_CORES = 8
_IN_SPECS = [
    ('x_src', [T + NS, D]), ('consts', [128, CST_W]), ('mem_prompt', [256, D]),
    ('state_ssm', [16, 1024, 128]), ('state_ssm_conv', [16, 3, 1536]), ('state_pool', [16, 15, 1024]),
    ('state_ffn_conv', [16, 2, 2 * DFF]), ('cache_mem_k', [16, 256, 4, 256]), ('cache_mem_v', [16, 256, 4, 256]),
    ('norm_mix', [D]), ('w_in', [D, DIN]), ('ssm_conv_w', [4, 1536]), ('ssm_conv_b', [1536]),
    ('ssm_dt_bias', [16]), ('ssm_a_log', [16]), ('ssm_d', [16]), ('ssm_norm', [D]),
    ('w_pool', [4, 256, 256]), ('pool_scale', [D]), ('w_out', [2 * D, D]), ('norm_mem', [D]), ('norm_memkv', [D]),
    ('w_mq', [D, D]), ('w_mk', [D, D]), ('w_mv', [D, D]), ('w_mo', [D, D]), ('norm_ffn', [D]),
    ('w_up', [D, 2 * DFF]), ('ffn_conv_w', [3, 2 * DFF]), ('ffn_conv_b', [2 * DFF]), ('w_down', [DFF, D]), ('final_norm', [D]),
]
_OUT_SPECS = [
    ('y_prompt', [T, D]), ('y_sample', [NS, D]), ('ssm_prompt', [1024, 128]), ('ssm_sample', [16, 1024, 128]),
    ('conv_prompt', [3, 1536]), ('conv_sample', [16, 3, 1536]), ('pool_prompt', [15, 1024]), ('pool_sample', [16, 15, 1024]),
    ('ffn_prompt', [2, 2 * DFF]), ('ffn_sample', [16, 2, 2 * DFF]), ('mem_k_prompt', [256, D]), ('mem_v_prompt', [256, D]),
]


def build_program():
    nc = bass.Bass("TRN2", target_bir_lowering=False)
    io = {}
    for name, shape in _IN_SPECS:
        io[name] = nc.dram_tensor(name, list(shape), F32, kind="ExternalInput").ap()
    for name, shape in _OUT_SPECS:
        io[name] = nc.dram_tensor(name, list(shape), F32, kind="ExternalOutput").ap()
    x1 = nc.dram_tensor("x1_scratch", [T + NS, D], F32, kind="Internal").ap()
    x2 = nc.dram_tensor("x2_scratch", [T + NS, D], F32, kind="Internal").ap()
    with ExitStack() as es:
        P = Prog(nc, es)
        k = K()
        setup_common(P, k, io['consts'])
        with ExitStack() as es2:
            phase_A(P, k, es2, io, io['x_src'], x1)
            P.end_phase()
        with ExitStack() as es2:
            phase_B(P, k, es2, io, x1, x2)
            P.end_phase()
        with ExitStack() as es2:
            phase_C(P, k, es2, io, x2)
            P.end_phase()
        P.emit()
    return nc


_PROG = {}


def kernel(**inputs):
    f = lambda a: np.ascontiguousarray(np.asarray(a, dtype=np.float32))
    if 'nc' not in _PROG:
        _PROG['nc'] = build_program()
    nc = _PROG['nc']
    consts = make_consts()
    xp, xs = f(inputs['x_prompt']), f(inputs['x_sample'])
    shared = {'consts': consts}
    for name in ['norm_mix', 'w_in', 'ssm_conv_w', 'ssm_conv_b', 'ssm_dt_bias', 'ssm_a_log', 'ssm_d', 'ssm_norm', 'w_pool',
                 'pool_scale', 'w_out', 'norm_mem', 'norm_memkv', 'w_mq', 'w_mk', 'w_mv', 'w_mo', 'norm_ffn', 'w_up',
                 'ffn_conv_w', 'ffn_conv_b', 'w_down']:
        shared[name] = f(inputs[name])[0]
    shared['final_norm'] = f(inputs['final_norm'])
    st_ssm, st_conv = f(inputs['state_ssm'])[0], f(inputs['state_ssm_conv'])[0]
    st_pool, st_ffn = f(inputs['state_pool'])[0], f(inputs['state_ffn_conv'])[0]
    ck, cv, mem = f(inputs['cache_mem_k'])[0], f(inputs['cache_mem_v'])[0], f(inputs['mem_prompt'])
    in_maps = []
    for i in range(N_CORES):
        sl = slice(16 * i, 16 * i + 16)
        m = dict(shared)
        m['x_src'] = np.concatenate([xp[i], xs[sl].reshape(NS, D)], axis=0)
        m['mem_prompt'] = mem[i]
        m['state_ssm'] = st_ssm[sl].reshape(16, 1024, 128)
        m['state_ssm_conv'] = st_conv[sl]
        m['state_pool'] = st_pool[sl]
        m['state_ffn_conv'] = st_ffn[sl]
        m['cache_mem_k'] = ck[sl]
        m['cache_mem_v'] = cv[sl]
        in_maps.append(m)
    res = run_bass_kernel_spmd(nc, in_maps, core_ids=list(range(N_CORES)))
    R = res.results
    g = lambda name: np.stack([np.asarray(R[i][name], dtype=np.float32) for i in range(N_CORES)], axis=0)
    y_prompt = g('y_prompt')
    y_sample = g('y_sample').reshape(128, 4, D)
    ssm_p = g('ssm_prompt').reshape(1, 8, 16, 64, 128)
    ssm_s = g('ssm_sample').reshape(1, 128, 16, 64, 128)
    conv_p = g('conv_prompt').reshape(1, 8, 3, 1536)
    conv_s = g('conv_sample').reshape(1, 128, 3, 1536)
    pool_p = g('pool_prompt').reshape(1, 8, 15, 1024)
    pool_s = g('pool_sample').reshape(1, 128, 15, 1024)
    ffn_p = g('ffn_prompt').reshape(1, 8, 2, 2 * DFF)
    ffn_s = g('ffn_sample').reshape(1, 128, 2, 2 * DFF)
    mk_p = g('mem_k_prompt').reshape(1, 8, 256, 4, 256)
    mv_p = g('mem_v_prompt').reshape(1, 8, 256, 4, 256)
    return (y_prompt, y_sample, ssm_p, ssm_s, conv_p, conv_s, pool_p, pool_s, ffn_p, ffn_s, mk_p, mv_p)
```

```python
import numpy as np
import concourse.bass as bass
import concourse.mybir as mybir
from concourse.bass_utils import run_bass_kernel_spmd
from contextlib import ExitStack

F32, BF16 = mybir.dt.float32, mybir.dt.bfloat16
AF = mybir.ActivationFunctionType
ALU = mybir.AluOpType
AX = mybir.AxisListType

SAME_ENGINE_SYNC = True
CHECK_CLOBBER = False
D = 1024
T = 2048
NS = 64
DFF = 2816
EPS = 1e-6


class Prog:
    ENG = ('tensor', 'vector', 'scalar', 'gpsimd', 'sync')

    def __init__(self, nc, es):
        self.nc, self.es = nc, es
        self.ops = {e: [] for e in self.ENG}
        self.sem, self.cnt = {}, {}
        self.phase, self.free, self.retired, self.nsem, self.semcls = 0, {'sw': [], 'hw': []}, set(), 0, {}
        for e in self.ENG:
            self._mksem('E_' + e)
        self.seen = {e: {} for e in self.ENG}
        self.lastw, self.readers = {}, {}
        self.nbuf = 0

    def _mksem(self, name, q=None):
        if name.startswith('D_'):
            name = name + '@%d' % self.phase
        if name not in self.sem:
            cls = 'sw' if q == 'gpsimd' else 'hw'
            if name.startswith('D_'):
                self.semcls[name] = cls
            if name.startswith('D_') and self.free[cls]:
                h, c = self.free[cls].pop()
                self.sem[name] = h
                self.cnt[name] = c
            else:
                self.nsem += 1
                self.sem[name] = self.es.enter_context(self.nc.semaphore('s%d' % self.nsem))
                self.cnt[name] = 0
        return name

    def end_phase(self):
        self.barrier()
        for name in list(self.sem):
            if name.startswith('D_') and name not in self.retired:
                self.retired.add(name)
                self.free[self.semcls[name]].append((self.sem[name], self.cnt[name]))
        self.phase += 1

    def dma_multi(self, q, pairs, keys, sem, **kw):
        s = self._mksem('D_' + str(sem), q)
        for (o, i) in pairs:
            self.cnt[s] += 16
            self.ops[q].append(([], (lambda e, o=o, i=i: e.dma_start(out=o, in_=i, **kw)), (s, 16)))
        tok = (s, self.cnt[s])
        for k in keys:
            self.lastw[k] = tok
            self.readers[k] = []

    def sb(self, shape, dt, name=None, es=None):
        self.nbuf += 1
        return (es or self.es).enter_context(self.nc.sbuf_tensor(name or f"b{self.nbuf}", list(shape), dt))

    def op(self, eng, fn, reads=(), writes=(), dma=None):
        need = {}

        def want(tok, kind):
            if tok is None:
                return
            s, v = tok
            if s == 'E_' + eng:
                if eng == 'tensor' or not SAME_ENGINE_SYNC:
                    return
            if self.seen[eng].get(s, 0) >= v:
                return
            if need.get(s, 0) < v:
                need[s] = v
        for k in reads:
            want(self.lastw.get(k), 'raw')
            if isinstance(k, tuple) and k[0] == 'ps':
                for r in self.readers.get(k, ()):
                    want(r, 'war')
        for k in writes:
            if CHECK_CLOBBER and isinstance(k, tuple) and k[0] == 'ps' and k in self.lastw and not self.readers.get(k):
                import traceback
                print("CLOBBER? unread PSUM", k, [f.lineno for f in traceback.extract_stack()[-6:-1]])
            want(self.lastw.get(k), 'waw')
            for r in self.readers.get(k, ()):
                want(r, 'war')
        for s, v in need.items():
            self.seen[eng][s] = v
        if dma is not None:
            s = self._mksem('D_' + str(dma), eng)
            self.cnt[s] += 16
            tok = (s, self.cnt[s])
            inc = (s, 16)
        else:
            s = 'E_' + eng
            self.cnt[s] += 1
            tok = (s, self.cnt[s])
            inc = (s, 1)
        for k in writes:
            self.lastw[k] = tok
            self.readers[k] = []
        for k in reads:
            self.readers.setdefault(k, []).append(tok)
        self.ops[eng].append((list(need.items()), fn, inc))
        return tok

    def V(self, fn, reads=(), writes=()):
        return self.op('vector', fn, reads, writes)

    def A(self, fn, reads=(), writes=()):
        return self.op('scalar', fn, reads, writes)

    def G(self, fn, reads=(), writes=()):
        return self.op('gpsimd', fn, reads, writes)

    def PE(self, fn, reads=(), writes=()):
        return self.op('tensor', fn, reads, writes)

    def dma(self, q, out, in_, reads=(), writes=(), sem=None, **kw):
        return self.op(q, lambda e: e.dma_start(out=out, in_=in_, **kw), reads, writes, dma=sem)

    def barrier(self):
        allc = [(s_, c_) for s_, c_ in self.cnt.items() if c_ > 0]
        for e in self.ENG:
            w = [(s_, c_) for s_, c_ in allc if self.seen[e].get(s_, 0) < c_]
            for s_, c_ in w:
                self.seen[e][s_] = c_
            self.ops[e].append((w, None, None))

    def emit(self):
        fin = [(s, c) for s, c in self.cnt.items() if c > 0 and s != 'E_sync']
        self.ops['sync'].append((fin, None, None))
        with self.nc.Block() as block:
            for e in self.ENG:
                def body(eng, e=e):
                    for waits, fn, inc in self.ops[e]:
                        for s, v in waits:
                            eng.wait_ge(self.sem[s], v)
                        if fn is None:
                            continue
                        ins = fn(eng)
                        ins.then_inc(self.sem[inc[0]], inc[1])
                getattr(block, e)(body)


def bcast(ap, axis, n):
    u = ap.unsqueeze(axis)
    shp = list(u.shape)
    shp[axis] = n
    return u.broadcast_to(shp)


class K:
    pass


CST_LAYOUT = [('ident', 128), ('mhalf', 1), ('ones', 128), ('rsp', 256), ('rss', 64), ('csc', 64), ('blk', 16),
              ('negc', 128), ('negs', 128), ('e2', 2048), ('expd', 1024)]
CST_SMALL = 128 + 1 + 128 + 256 + 64 + 64 + 16
CST_OFF = {}
_o = 0
for _n, _w in CST_LAYOUT:
    CST_OFF[_n] = (_o, _w)
    _o += _w
CST_W = _o


def make_consts():
    c = np.zeros((128, CST_W), np.float32)

    def put(name, arr):
        o, w = CST_OFF[name]
        c[:arr.shape[0], o:o + arr.shape[1]] = arr
    put('ident', np.eye(128, dtype=np.float32))
    put('mhalf', np.full((128, 1), -0.5, np.float32))
    put('ones', np.ones((128, 128), np.float32))
    s_ = np.arange(128)[:, None]
    l_ = np.arange(128)[None, :]
    put('negc', np.where(l_ >= s_, 0.0, -30000.0).astype(np.float32))
    put('negs', np.where((l_ >= s_) & (l_ // 4 == s_ // 4), 0.0, -30000.0).astype(np.float32))
    e2 = np.zeros((128, 16, 128), np.float32)
    for h in range(16):
        e2[h, h, :] = 1.0
        e2[32 + h, h, :] = 1.0
    put('e2', e2.reshape(128, 2048))
    rsp = np.ones((128, 256), np.float32)
    rsp[:, 0] = 0.0
    rsp[:, 128] = 0.0
    put('rsp', rsp)
    rss = np.ones((128, 64), np.float32)
    rss[:, 0::4] = 0.0
    put('rss', rss)
    csc = np.zeros((128, 4, 16), np.float32)
    for gi, w in enumerate((2, 4, 8, 16)):
        for t in range(16):
            csc[:, gi, t] = 1.0 / min(t + 1, w)
    put('csc', csc.reshape(128, 64))
    blk = np.zeros((128, 16), np.float32)
    for r in range(64):
        blk[r, r // 4] = 1.0
    put('blk', blk)
    expd = np.zeros((128, 8, 128), np.float32)
    for j in range(8):
        for m in range(128):
            expd[2 * j + m // 64, j, m] = 1.0
            expd[32 + 2 * j + m // 64, j, m] = 1.0
    put('expd', expd.reshape(128, 1024))
    return c


def setup_common(P, k, consts):
    nc = P.nc
    k.ps = P.es.enter_context(nc.psum_tensor("ps", [128, 4096], F32))
    k.psb = k.ps[:].bitcast(BF16)
    k.cst = P.sb([128, CST_SMALL], F32, "cst")
    P.dma('sync', k.cst[:], consts[:, 0:CST_SMALL], writes=['cst'], sem='cst')

    def cs(name, rows=128):
        o, w = CST_OFF[name]
        return k.cst[0:rows, o:o + w]
    k.cs = cs
    k.identf = cs('ident')
    k.mhalf = cs('mhalf')
    k.cstb = P.sb([128, 384], BF16, "cstb")
    k.junk = P.sb([128, 1024], BF16, 'junk')
    CP(P, 'vector', k.cstb[:, 0:128], cs('ident'), ['cst'], ['identb'])
    P.dma('gpsimd', k.cstb[:, 128:384], consts[:, CST_SMALL:CST_SMALL + 256], writes=['cstb'], sem='cstbn')
    k.identb = k.cstb[:, 0:128]
    k.negcb = k.cstb[:, 128:256]
    k.negsb = k.cstb[:, 256:384]


def bank(k, b, n=512, bf=False, nb=1):
    if bf:
        return k.psb[:, 1024 * b: 1024 * b + n]
    return k.ps[:, 512 * b: 512 * b + n]


def load_fm(P, k, rows_ap, R, out_ap, key, tag, es=None):
    t = P.sb([128, 128], F32, "lfm_" + tag, es)
    P.dma('sync', t[0:R, :], rows_ap, writes=['lfm_' + tag], sem='lfm_' + tag)
    pb = bank(k, 7)
    P.PE(lambda e: e.transpose(out=pb[:, 0:R], in_=t[0:R, :], identity=k.identf[0:R, 0:R]),
         reads=['lfm_' + tag, 'cst'], writes=[('ps', 7)])
    P.V(lambda e: e.tensor_copy(out=out_ap, in_=pb[:, 0:R]), reads=[('ps', 7)], writes=[key])


def TT(P, eng, out, in0, in1, op, reads, writes):
    return P.op(eng, lambda e: e.tensor_tensor(out=out, in0=in0, in1=in1, op=op), reads, writes)


def TS(P, eng, out, in0, s1, s2, op0, op1, reads, writes):
    if op1 is None:
        return P.op(eng, lambda e: e.tensor_scalar(out=out, in0=in0, scalar1=s1, scalar2=None, op0=op0), reads, writes)
    return P.op(eng, lambda e: e.tensor_scalar(out=out, in0=in0, scalar1=s1, scalar2=s2, op0=op0, op1=op1), reads, writes)


def STT(P, out, in0, scalar, in1, op0, op1, reads, writes):
    return P.op('vector', lambda e: e.scalar_tensor_tensor(out=out, in0=in0, scalar=scalar, in1=in1, op0=op0, op1=op1), reads, writes)


def ACTF(P, out, in_, func, reads, writes, bias=None, scale=None, accum=None):
    kw = {}
    if bias is not None:
        kw['bias'] = bias
    if scale is not None:
        kw['scale'] = scale
    if accum is not None:
        kw['accum_out'] = accum
    return P.op('scalar', lambda e: e.activation(out=out, in_=in_, func=func, **kw), reads, writes)


def CP(P, eng, out, in_, reads, writes):
    if eng == 'scalar':
        return P.op(eng, lambda e: e.activation(out=out, in_=in_, func=AF.Identity), reads, writes)
    return P.op(eng, lambda e: e.tensor_copy(out=out, in_=in_), reads, writes)


def MM(P, items, reads, writes):
    items = list(items)

    def f(e):
        for it in items:
            (o, l, r, st, sp) = it[:5]
            if len(it) > 5:
                ins = e.matmul(o, lhsT=l, rhs=r, start=st, stop=sp, tile_position=it[5])
            else:
                ins = e.matmul(o, lhsT=l, rhs=r, start=st, stop=sp)
        return ins
    return P.op('tensor', f, reads, writes)


def TR(P, items, reads, writes):
    items = list(items)

    def f(e):
        for (o, i, idn) in items:
            ins = e.transpose(out=o, in_=i, identity=idn)
        return ins
    return P.op('tensor', f, reads, writes)


def MEMSET(P, eng, ap, val, writes):
    return P.op(eng, lambda e: e.memset(ap, val), (), writes)


def rstd_op(P, k, x_ap, n, xkeys, st, col, scale=1.0 / D, extra=None):
    junk = k.junk
    ks = ('st', id(st), col)
    P.A(lambda e: e.activation(out=junk[0:n, 0:x_ap.shape[1]], in_=x_ap, func=AF.Square, accum_out=st[0:n, col:col + 1]),
        reads=xkeys, writes=['junk', ks])
    P.V(lambda e: e.tensor_scalar(out=st[0:n, col + 1:col + 2], in0=st[0:n, col:col + 1], scalar1=scale, scalar2=EPS,
                                  op0=ALU.mult, op1=ALU.add), reads=[ks], writes=[ks])
    P.G(lambda e: e.tensor_tensor(out=st[0:n, col + 2:col + 3], in0=st[0:n, col + 1:col + 2], in1=k.mhalf[0:n, :], op=ALU.pow),
        reads=[ks, 'cst'], writes=[ks])
    if extra is not None:
        P.V(lambda e: e.tensor_scalar(out=st[0:n, col + 2:col + 3], in0=st[0:n, col + 2:col + 3], scalar1=extra, scalar2=None,
                                      op0=ALU.mult), reads=[ks], writes=[ks])
    return st[0:n, col + 2:col + 3], ks


def norm_T(P, k, x_ap, n, xkeys, gain_fm, gkey, hT_ap, hTkey, st, col, hb, hbkey, tb):
    r, ks = rstd_op(P, k, x_ap, n, xkeys, st, col)
    P.V(lambda e: e.tensor_scalar(out=hb[0:n, :], in0=x_ap, scalar1=r, scalar2=None, op0=ALU.mult),
        reads=list(xkeys) + [ks], writes=[hbkey])
    pb = k.psb[:, 1024 * tb: 1024 * tb + 1024].rearrange("p (c t) -> p c t", c=8)

    def tr(e):
        for c in range(8):
            ins = e.transpose(out=pb[:, c, 0:n], in_=hb[0:n, c * 128:(c + 1) * 128], identity=k.identb[0:n, 0:n])
        return ins
    P.PE(tr, reads=[hbkey, 'identb'], writes=[('ps', tb)])
    P.V(lambda e: e.tensor_tensor(out=hT_ap, in0=pb[:, :, 0:n], in1=bcast(gain_fm, 2, n), op=ALU.mult),
        reads=[('ps', tb), gkey], writes=[hTkey])


def phase_C(P, k, es, io, x_src, do_prompt=True, do_sample=True):
    wup = P.sb([128, 8, 2 * DFF], BF16, "wup", es)
    wdn = P.sb([128, 22, D], BF16, "wdn", es)
    wup_d = io['w_up'].rearrange("(c p) n -> p c n", p=128)
    wdn_d = io['w_down'].rearrange("(c p) n -> p c n", p=128)
    NCB = 4
    cbw = DFF // NCB
    def load_wup_block(q):
        prs = []
        for br in range(2):
            c0 = br * DFF + q * cbw
            prs.append((wup[:, :, c0:c0 + cbw], wup_d[:, :, c0:c0 + cbw]))
        P.dma_multi('gpsimd', prs, [('wup', q)], ('wup', q), max_dma_last_dim=4096)

    def load_rest_weights():
        for q in range(1, NCB):
            load_wup_block(q)
        for gq in range(2):
            P.dma_multi('gpsimd', [(wdn[:, 11 * gq:11 * gq + 11, :], wdn_d[:, 11 * gq:11 * gq + 11, :])], [('wdn', c) for c in range(11 * gq, 11 * gq + 11)], ('wdn', gq), max_dma_last_dim=4096)
    load_wup_block(0)
    wupk = None
    gC = P.sb([128, 8], F32, "gC", es)
    load_fm(P, k, io['norm_ffn'].rearrange("(c p) -> c p", p=128), 8, gC[:, :], 'gC', 'gC', es)
    cw = P.sb([128, 3, 44], F32, "cwC", es)
    cb = P.sb([128, 44], F32, "cbC", es)
    cwd = io['ffn_conv_w'].rearrange("k (c p) -> k c p", p=128)
    for kk in range(3):
        load_fm(P, k, cwd[kk], 44, cw[:, kk, :], ('cwC', kk), 'cwC%d' % kk, es)
    load_fm(P, k, io['ffn_conv_b'].rearrange("(c p) -> c p", p=128), 44, cb[:, :], 'cbC', 'cbC', es)
    cwk = [('cwC', kk) for kk in range(3)] + ['cbC']
    P.V(lambda e: e.tensor_scalar(out=cw[:, :, 0:22], in0=cw[:, :, 0:22], scalar1=0.5, scalar2=None, op0=ALU.mult),
        reads=cwk, writes=cwk[:3])
    P.V(lambda e: e.tensor_scalar(out=cb[:, 0:22], in0=cb[:, 0:22], scalar1=0.5, scalar2=None, op0=ALU.mult),
        reads=['cbC'], writes=['cbC'])
    fgb = P.sb([128, D], F32, "fgb", es)
    P.dma('sync', fgb[:], io['final_norm'].partition_broadcast(128), writes=['fgb'], sem='fgb')

    xt = [P.sb([128, D], F32, "xtC%d" % i, es) for i in range(2)]
    hb = [k.junk] * 2
    st = P.sb([128, 64], F32, "stC", es)
    aT = P.sb([128, 22, 512], BF16, "aTC", es)
    NR = 3
    B_ = {}

    def alloc_group(stack, ntok, nseq, tag):
        B_['hT'] = P.sb([128, 8, ntok], BF16, "hTC" + tag, stack)
        B_['accg'] = [P.sb([128, ntok], F32, "accg%d%s" % (i, tag), stack) for i in range(NR)]
        B_['accv'] = [P.sb([128, ntok], F32, "accv%d%s" % (i, tag), stack) for i in range(NR)]
        B_['th'] = [P.sb([128, ntok], F32, "th%d%s" % (i, tag), stack) for i in range(2)]
        halo = P.sb([128, 44, nseq, 2], F32, "halo" + tag, stack)
        fix = P.sb([128, 44, nseq, 2], F32, "fix" + tag, stack)
        return halo, fix
    x3 = [P.sb([128, D], F32, "x3C%d" % i, es) for i in range(1)] * 2
    stg = [P.sb([32, 512], F32, "stgC%d" % i, es) for i in range(2)]
    cnt = {'x': 0, 'st': 0, 'p': 0, 'o': 0}

    def run_group(tok0, nseq, L, halo, fix, y_dst, gname, do=('norm', 'loop', 'down'), js=None):
        ntok = nseq * L
        nsub = (ntok + 127) // 128
        hT, accg, accv, th = B_['hT'], B_['accg'], B_['accv'], B_['th']
        hk = [('halo' + gname, c) for c in range(44)]
        fk = 'fix' + gname
        if 'norm' in do:
            for j in (range(nsub) if js is None else js):
                n = min(128, ntok - j * 128)
                i = cnt['x'] % 2
                cnt['x'] += 1
                P.dma('sync', xt[i][0:n, :], x_src[tok0 + j * 128: tok0 + j * 128 + n, :], writes=[('xtC', i)], sem=('xtC', i))
                col = (cnt['st'] % 8) * 4
                cnt['st'] += 1
                norm_T(P, k, xt[i][0:n, :], n, [('xtC', i)], gC[:, :], 'gC', hT[:, :, j * 128: j * 128 + n], ('hTC', j),
                       st, col, hb[i], 'junk', 4)
        hTk = [('hTC', j) for j in range(nsub)]
        if 'loop' in do:
            w0 = bcast(cw[:, 0, :], 2, nseq)
            w1 = bcast(cw[:, 1, :], 2, nseq)
            P.G(lambda e: e.tensor_tensor(out=fix[:, :, :, 0], in0=halo[:, :, :, 0], in1=w0, op=ALU.mult), reads=hk + cwk, writes=[fk])
            P.G(lambda e: e.tensor_tensor(out=fix[:, :, :, 1], in0=halo[:, :, :, 1], in1=w1, op=ALU.mult), reads=hk + cwk, writes=[fk + 'b'])
            P.G(lambda e: e.tensor_tensor(out=fix[:, :, :, 0], in0=fix[:, :, :, 0], in1=fix[:, :, :, 1], op=ALU.add), reads=[fk, fk + 'b'], writes=[fk])
            P.G(lambda e: e.tensor_tensor(out=fix[:, :, :, 1], in0=halo[:, :, :, 1], in1=w0, op=ALU.mult), reads=hk + cwk + [fk], writes=[fk + 'b'])
            fks = [fk, fk + 'b']

            def v3(ap, a, b):
                return ap.rearrange("p (s l) -> p s l", s=nseq)[:, :, a:b]

            def stage1(jj):
                for br, acc, nm in ((0, accg, 'accg'), (1, accv, 'accv')):
                    c = br * 22 + jj
                    b = (cnt['p'] % 4)
                    cnt['p'] += 1
                    pb = bank(k, b, ntok)

                    def mm(e, c=c, pb=pb):
                        for kc in range(8):
                            ins = e.matmul(pb, lhsT=wup[:, kc, c * 128:(c + 1) * 128], rhs=hT[:, kc, 0:ntok], start=(kc == 0), stop=(kc == 7))
                        return ins
                    P.PE(mm, reads=[('wup', min(NCB - 1, (jj * 128) // cbw)), ('wup', min(NCB - 1, (jj * 128 + 127) // cbw))] + hTk, writes=[('ps', b)])
                    r = jj % NR
                    a_ap = acc[r][:, 0:ntok]
                    P.A(lambda e, c=c, pb=pb, a_ap=a_ap: e.activation(out=a_ap, in_=pb, func=AF.Identity, bias=cb[:, c:c + 1], scale=cw[:, 2, c:c + 1]),
                        reads=[('ps', b)] + cwk, writes=[(nm, r)])
                    P.V(lambda e, c=c, pb=pb, a_ap=a_ap: e.scalar_tensor_tensor(out=v3(a_ap, 1, L), in0=v3(pb, 0, L - 1), scalar=cw[:, 1, c:c + 1], in1=v3(a_ap, 1, L), op0=ALU.mult, op1=ALU.add),
                        reads=[('ps', b), (nm, r)] + cwk, writes=[(nm, r)])
                    P.V(lambda e, c=c, pb=pb, a_ap=a_ap: e.scalar_tensor_tensor(out=v3(a_ap, 2, L), in0=v3(pb, 0, L - 2), scalar=cw[:, 0, c:c + 1], in1=v3(a_ap, 2, L), op0=ALU.mult, op1=ALU.add),
                        reads=[('ps', b), (nm, r)] + cwk, writes=[(nm, r)])
                    P.V(lambda e, c=c, a_ap=a_ap: e.tensor_tensor(out=v3(a_ap, 0, 2), in0=v3(a_ap, 0, 2), in1=fix[:, c, :, :], op=ALU.add),
                        reads=[(nm, r)] + fks, writes=[(nm, r)])
                    P.V(lambda e, c=c, pb=pb: e.tensor_copy(out=halo[:, c, :, :], in_=v3(pb, L - 2, L)),
                        reads=[('ps', b)] + fks, writes=[hk[c]])

            def stage2(jj):
                r = jj % NR
                P.A(lambda e: e.activation(out=th[jj % 2][:, 0:ntok], in_=accg[r][:, 0:ntok], func=AF.Tanh), reads=[('accg', r)], writes=[('th', jj % 2)])
                P.G(lambda e: e.tensor_scalar(out=th[jj % 2][:, 0:ntok], in0=th[jj % 2][:, 0:ntok], scalar1=1.0, scalar2=1.0, op0=ALU.add, op1=ALU.mult),
                    reads=[('th', jj % 2)], writes=[('th', jj % 2)])
                P.G(lambda e: e.tensor_tensor(out=th[jj % 2][:, 0:ntok], in0=th[jj % 2][:, 0:ntok], in1=accg[r][:, 0:ntok], op=ALU.mult),
                    reads=[('th', jj % 2), ('accg', r)], writes=[('th', jj % 2)])
                P.G(lambda e: e.tensor_tensor(out=aT[:, jj, 0:ntok], in0=th[jj % 2][:, 0:ntok], in1=accv[r][:, 0:ntok], op=ALU.mult),
                    reads=[('th', jj % 2), ('accv', r)], writes=[('aT', jj)])
            SK = 1
            for step in range(22 + SK):
                if step < 22:
                    stage1(step)
                if step >= SK:
                    stage2(step - SK)
        aTk = [('aT', jj) for jj in range(22)]
        wdk = [('wdn', c) for c in range(22)]
        if 'down' in do:
            for j in (range(nsub) if js is None else js):
                n = min(128, ntok - j * 128)
                i = cnt['x'] % 2
                cnt['x'] += 1
                P.dma('sync', xt[i][0:n, :], x_src[tok0 + j * 128: tok0 + j * 128 + n, :], writes=[('xtC', i)], sem=('xtC', i))
                o = cnt['o'] % 2
                cnt['o'] += 1
                for hf in range(2):
                    b = 5 + hf
                    pb = bank(k, b)[0:n, :]

                    def mm(e, pb=pb, hf=hf, n=n, j=j):
                        for jj in range(22):
                            ins = e.matmul(pb, lhsT=aT[:, jj, j * 128: j * 128 + n], rhs=wdn[:, jj, hf * 512:(hf + 1) * 512], start=(jj == 0), stop=(jj == 21))
                        return ins
                    P.PE(mm, reads=aTk + wdk, writes=[('ps', b)])
                    P.V(lambda e, pb=pb, hf=hf, n=n, i=i, o=o: e.tensor_tensor(out=x3[o][0:n, hf * 512:(hf + 1) * 512], in0=pb, in1=xt[i][0:n, hf * 512:(hf + 1) * 512], op=ALU.add),
                        reads=[('ps', b), ('xtC', i)], writes=[('x3', 0, hf)])
                col = (cnt['st'] % 8) * 4
                cnt['st'] += 1
                r, ks = rstd_op(P, k, x3[o][0:n, :], n, [('x3', 0, 0), ('x3', 0, 1)], st, col)
                P.V(lambda e, n=n, o=o, r=r: e.scalar_tensor_tensor(out=x3[o][0:n, :], in0=x3[o][0:n, :], scalar=r, in1=fgb[0:n, :], op0=ALU.mult, op1=ALU.mult),
                    reads=[('x3', 0, 0), ('x3', 0, 1), ks, 'fgb'], writes=[('x3', 0, 0), ('x3', 0, 1)])
                P.dma('scalar', y_dst[j * 128: j * 128 + n, :], x3[o][0:n, :], reads=[('x3', 0, 0), ('x3', 0, 1)], sem=('yout', o))
        return hk

    def state_out(halo, hk, nseq, dst):
        R = nseq * 2
        for q in range(11):
            pb = bank(k, 7)

            def tr(e, q=q, pb=pb):
                for u in range(4):
                    c = q * 4 + u
                    ins = e.transpose(out=pb[0:R, u * 128:(u + 1) * 128], in_=halo[:, c, :, :].rearrange("p s r -> p (s r)"), identity=k.identf)
                return ins
            P.PE(tr, reads=hk[q * 4:q * 4 + 4] + ['cst'], writes=[('ps', 7)])
            P.V(lambda e, q=q, pb=pb: e.tensor_copy(out=stg[q % 2][0:R, :], in_=pb[0:R, :]), reads=[('ps', 7)], writes=[('stgC', q % 2)])
            P.dma('sync', dst[:, q * 512:(q + 1) * 512], stg[q % 2][0:R, :], reads=[('stgC', q % 2)], sem=('stgC', q % 2))

    if do_prompt:
        with ExitStack() as esp:
            haloP, fixP = alloc_group(esp, 512, 1, 'P')
            MEMSET(P, 'vector', haloP[:], 0.0, [('haloP', c) for c in range(44)])
            NSUP = T // 512
            ga = lambda S: (S * 512, 1, 512, haloP, fixP, io['y_prompt'][S * 512:(S + 1) * 512, :], 'P')
            run_group(*ga(0), do=('norm',))
            load_rest_weights()
            for S in range(NSUP):
                hk = run_group(*ga(S), do=('loop',))
                if S + 1 < NSUP:
                    run_group(*ga(S + 1), do=('norm',))
                run_group(*ga(S), do=('down',))
            state_out(haloP, hk, 1, io['ffn_prompt'])
            P.barrier()
    if not do_prompt:
        load_rest_weights()
    if do_sample:
        with ExitStack() as ess:
            haloS, fixS = alloc_group(ess, NS, 16, 'S')
            hkS = [('haloS', c) for c in range(44)]
            sfv = io['state_ffn_conv'].rearrange("b r c -> (b r) c")
            for q in range(11):
                P.dma('sync', stg[q % 2][0:32, :], sfv[:, q * 512:(q + 1) * 512], writes=[('stgC', q % 2)], sem=('stgC', q % 2))
                pb = bank(k, 7)[:, 0:128].rearrange("p (u r) -> p u r", u=4)
                TR(P, [(pb[:, u, :], stg[q % 2][0:32, u * 128:(u + 1) * 128], k.identf[0:32, 0:32]) for u in range(4)], [('stgC', q % 2), 'cst'], [('ps', 7)])
                CP(P, 'vector', haloS[:, 4 * q:4 * q + 4, :, :].rearrange("p u s r -> p u (s r)"), pb, [('ps', 7)], hkS[4 * q:4 * q + 4])
            hk = run_group(T, 16, 4, haloS, fixS, io['y_sample'], 'S')
            state_out(haloS, hk, 16, io['ffn_sample'].rearrange("b r c -> (b r) c"))
            P.barrier()


def phase_B(P, k, es, io, x_src, x_dst, do_prompt=True, do_sample=True):
    wq = P.sb([128, 8, D], BF16, "wmq", es)
    wo = P.sb([128, 8, D], BF16, "wmo", es)

    def load_w(nm, w):
        wd = io[nm].rearrange("(c p) n -> p c n", p=128)
        for gq in range(2):
            P.dma_multi('gpsimd', [(w[:, 4 * gq:4 * gq + 4, :], wd[:, 4 * gq:4 * gq + 4, :])], [(nm, c) for c in range(4 * gq, 4 * gq + 4)], (nm, gq), max_dma_last_dim=4096)
    kk_ = lambda nm: [(nm, c) for c in range(8)]
    gM = P.sb([128, 8], F32, "gM", es)
    load_fm(P, k, io['norm_mem'].rearrange("(c p) -> c p", p=128), 8, gM[:, :], 'gM', 'gM', es)
    xt = [P.sb([128, D], F32, "xtB%d" % i, es) for i in range(2)]
    hb = k.junk
    st = P.sb([128, 64], F32, "stB", es)
    hT = P.sb([128, 8, 512], BF16, "hTB", es)
    qT = P.sb([128, 8, 512], BF16, "qTB", es)
    PT = P.sb([128, 8, 512], BF16, "PTB", es)
    oT = P.sb([128, 8, 512], BF16, "oTB", es)
    pe = P.sb([128, 4, 256], F32, "peB", es)
    pn = P.sb([128, 4, 256], BF16, "pnB", es)
    KT, Vb = [None], [None]
    sm = P.sb([128, 32], F32, "smB", es)
    ob = P.sb([128, D], BF16, "obB", es)
    x2 = P.sb([128, D], F32, "x2B", es)
    cnt = {'x': 0, 'st': 0}

    def load_norm(tok0, ntok, gain, gkey, src, tb=4):
        nsub = (ntok + 127) // 128
        for j in range(nsub):
            n = min(128, ntok - j * 128)
            i = cnt['x'] % 2
            cnt['x'] += 1
            P.dma('sync', xt[i][0:n, :], src[tok0 + j * 128: tok0 + j * 128 + n, :], writes=[('xtB', i)], sem=('xtB', i))
            col = (cnt['st'] % 8) * 4
            cnt['st'] += 1
            norm_T(P, k, xt[i][0:n, :], n, [('xtB', i)], gain, gkey, hT[:, :, j * 128: j * 128 + n], ('hTB', j), st, col, hb, 'junk', tb)
        return [('hTB', j) for j in range(nsub)]

    def softmax_rows(S_ap, n, skeys, pn_out=None, pn_key='pnB'):
        pn_ = pn if pn_out is None else pn_out
        P.V(lambda e: e.tensor_reduce(out=sm[0:n, 0:4], in_=S_ap, axis=AX.X, op=ALU.max), reads=skeys, writes=['smB'])
        P.V(lambda e: e.tensor_scalar(out=sm[0:n, 4:8], in0=sm[0:n, 0:4], scalar1=-1.0, scalar2=None, op0=ALU.mult), reads=['smB'], writes=['smB'])
        for h in range(4):
            P.A(lambda e, h=h: e.activation(out=pe[0:n, h, :], in_=S_ap[:, h, :], func=AF.Exp, bias=sm[0:n, 4 + h:5 + h], accum_out=sm[0:n, 8 + h:9 + h]),
                reads=skeys + ['smB'], writes=[('peB', h), ('smB', h)])
        smk = [('smB', h) for h in range(4)]
        P.V(lambda e: e.reciprocal(out=sm[0:n, 12:16], in_=sm[0:n, 8:12]), reads=smk, writes=['smB2'])
        P.V(lambda e: e.tensor_tensor(out=pn_[0:n, :, :], in0=pe[0:n, :, :], in1=bcast(sm[0:n, 12:16], 2, 256), op=ALU.mult),
            reads=[('peB', h) for h in range(4)] + ['smB2'], writes=[pn_key])

    def qproj(ntok, hTk):
        for c in range(8):
            b = c % 2
            pb = bank(k, b, ntok)

            def mm(e, c=c, pb=pb):
                for kc in range(8):
                    ins = e.matmul(pb, lhsT=wq[:, kc, c * 128:(c + 1) * 128], rhs=hT[:, kc, 0:ntok], start=(kc == 0), stop=(kc == 7))
                return ins
            P.PE(mm, reads=kk_('w_mq') + hTk, writes=[('ps', b)])
            P.A(lambda e, c=c, pb=pb: e.activation(out=qT[:, c, 0:ntok], in_=pb, func=AF.Identity, scale=1.0 / 16.0), reads=[('ps', b)], writes=[('qTB', c)])
        return [('qTB', c) for c in range(8)]

    def oproj(tok0, ntok, src, dst, oTk):
        nsub = (ntok + 127) // 128
        for j in range(nsub):
            n = min(128, ntok - j * 128)
            i = cnt['x'] % 2
            cnt['x'] += 1
            P.dma('sync', xt[i][0:n, :], src[tok0 + j * 128: tok0 + j * 128 + n, :], writes=[('xtB', i)], sem=('xtB', i))
            for hf in range(2):
                b = 2 + hf
                pb = bank(k, b)[0:n, :]

                def mm(e, pb=pb, hf=hf, n=n, j=j):
                    for c in range(8):
                        ins = e.matmul(pb, lhsT=oT[:, c, j * 128: j * 128 + n], rhs=wo[:, c, hf * 512:(hf + 1) * 512], start=(c == 0), stop=(c == 7))
                    return ins
                P.PE(mm, reads=oTk + kk_('w_mo'), writes=[('ps', b)])
                P.V(lambda e, pb=pb, hf=hf, n=n, i=i: e.tensor_tensor(out=x2[0:n, hf * 512:(hf + 1) * 512], in0=pb, in1=xt[i][0:n, hf * 512:(hf + 1) * 512], op=ALU.add),
                    reads=[('ps', b), ('xtB', i)], writes=[('x2B', hf)])
            P.dma('sync', dst[tok0 + j * 128: tok0 + j * 128 + n, :], x2[0:n, :], reads=[('x2B', 0), ('x2B', 1)], sem='x2B')

    BSTEP = 99
    if do_prompt and BSTEP >= 1:
        esp = ExitStack()
        esp.__enter__()
        wk = P.sb([128, 8, D], BF16, "wmk", esp)
        wv = P.sb([128, 8, D], BF16, "wmv", esp)
        gKV = P.sb([128, 8], F32, "gKV", esp)
        KT[0] = P.sb([128, 8, 256], BF16, "KTB0", esp)
        Vb[0] = P.sb([128, 2, D], BF16, "VbB0", esp)
        kvf = P.sb([128, D], F32, "kvf", esp)
        load_w('w_mk', wk)
        load_w('w_mv', wv)
        load_w('w_mq', wq)
        load_w('w_mo', wo)
        load_fm(P, k, io['norm_memkv'].rearrange("(c p) -> c p", p=128), 8, gKV[:, :], 'gKV', 'gKV', esp)
        if True:
            hmk = load_norm(0, 256, gKV[:, :], 'gKV', io['mem_prompt'])
            for nm, w, dst in (('w_mk', wk, io['mem_k_prompt']), ('w_mv', wv, io['mem_v_prompt'])):
                if BSTEP < 2:
                    break
                for mt in range(2):
                    for hf in range(2):
                        b = hf
                        pb = bank(k, b)

                        def mm(e, pb=pb, hf=hf, mt=mt, w=w):
                            for kc in range(8):
                                ins = e.matmul(pb, lhsT=hT[:, kc, mt * 128:(mt + 1) * 128], rhs=w[:, kc, hf * 512:(hf + 1) * 512], start=(kc == 0), stop=(kc == 7))
                            return ins
                        P.PE(mm, reads=kk_(nm) + hmk, writes=[('ps', b)])
                        P.A(lambda e, pb=pb, hf=hf: e.activation(out=kvf[:, hf * 512:(hf + 1) * 512], in_=pb, func=AF.Identity), reads=[('ps', b)], writes=[('kvf', hf)])
                        if nm == 'w_mv':
                            P.V(lambda e, pb=pb, hf=hf, mt=mt: e.tensor_copy(out=Vb[0][:, mt, hf * 512:(hf + 1) * 512], in_=pb), reads=[('ps', b)], writes=[('VbB', 0, mt, hf)])
                    P.dma('sync', dst[mt * 128:(mt + 1) * 128, :], kvf[:, :], reads=[('kvf', 0), ('kvf', 1)], sem='kvf')
            for c in range(8 if BSTEP >= 3 else 0):
                b = c % 2
                pb = bank(k, b, 256)

                def mm(e, c=c, pb=pb):
                    for kc in range(8):
                        ins = e.matmul(pb, lhsT=wk[:, kc, c * 128:(c + 1) * 128], rhs=hT[:, kc, 0:256], start=(kc == 0), stop=(kc == 7))
                    return ins
                P.PE(mm, reads=kk_('w_mk') + hmk, writes=[('ps', b)])
                P.V(lambda e, c=c, pb=pb: e.tensor_copy(out=KT[0][:, c, :], in_=pb), reads=[('ps', b)], writes=[('KTB', 0, c)])
            KTk = [('KTB', 0, c) for c in range(8)]
            Vk = [('VbB', 0, mt, hf) for mt in range(2) for hf in range(2)]
            pn2 = P.sb([128, 4, 256], BF16, "pnB2", esp)
            qT1 = P.sb([128, 8, 512], BF16, "qTB1", esp)
            PT1 = P.sb([128, 8, 512], BF16, "PTB1", esp)
            qTs, PTs = [qT, qT1], [PT, PT1]
            NSUP = T // 512

            def g_qproj(S):
                hTk = [('hTB', j) for j in range(4)]
                q_ = qTs[S % 2]
                for c in range(8):
                    b_ = c % 2
                    pb = bank(k, b_, 512)
                    MM(P, [(pb, wq[:, kc, c * 128:(c + 1) * 128], hT[:, kc, 0:512], kc == 0, kc == 7) for kc in range(8)], kk_('w_mq') + hTk, [('ps', b_)])
                    ACTF(P, q_[:, c, :], pb, AF.Identity, [('ps', b_)], [('qTB', S % 2, c)], scale=1.0 / 16.0)
                    yield

            def g_soft(S):
                q_, pt_ = qTs[S % 2], PTs[S % 2]
                qk = [('qTB', S % 2, c) for c in range(8)]
                pend = None

                def ptrans(j, pnj, pk):
                    tb = k.psb[:, 1024 * 2: 1024 * 3].rearrange("p (c t) -> p c t", c=8)
                    TR(P, [(tb[:, 2 * h + mc, :], pnj[:, h, mc * 128:(mc + 1) * 128], k.identb) for h in range(4) for mc in range(2)], [pk, 'identb'], [('ps', 2)])
                    CP(P, 'vector', pt_[:, :, j * 128:(j + 1) * 128], tb, [('ps', 2)], [('PTB', S % 2, j)])
                for j in range(4):
                    Sps = k.ps[:, 512 * 4: 512 * 6].rearrange("p (h m) -> p h m", h=4)
                    for h in range(4):
                        MM(P, [(Sps[:, h, :], q_[:, 2 * h + dc, j * 128:(j + 1) * 128], KT[0][:, 2 * h + dc, :], dc == 0, dc == 1) for dc in range(2)],
                           qk + KTk, [('ps', 4 + h // 2)])
                    yield
                    pnj = pn if j % 2 == 0 else pn2
                    pk = 'pnB' if j % 2 == 0 else 'pnB2'
                    softmax_rows(Sps, 128, [('ps', 4), ('ps', 5)], pn_out=pnj, pn_key=pk)
                    yield
                    if pend is not None:
                        ptrans(*pend)
                        yield
                    pend = (j, pnj, pk)
                ptrans(*pend)
                yield

            def g_out(S):
                pt_ = PTs[S % 2]
                PTk = [('PTB', S % 2, j) for j in range(4)]
                for c in range(8):
                    h, dc = c // 2, c % 2
                    pb = bank(k, 3)
                    MM(P, [(pb, Vb[0][:, mc, h * 256 + dc * 128: h * 256 + dc * 128 + 128], pt_[:, 2 * h + mc, :], mc == 0, mc == 1) for mc in range(2)], PTk + Vk, [('ps', 3)])
                    CP(P, 'scalar', oT[:, c, :], pb, [('ps', 3)], [('oTB', c)])
                    if c % 2 == 1:
                        yield
                oTk = [('oTB', c) for c in range(8)]
                for j in range(4):
                    i = cnt['x'] % 2
                    cnt['x'] += 1
                    P.dma('sync', xt[i][:, :], x_src[S * 512 + j * 128: S * 512 + (j + 1) * 128, :], writes=[('xtB', i)], sem=('xtB', i))
                    for hf in range(2):
                        pb = bank(k, 6 + hf)
                        MM(P, [(pb, oT[:, c, j * 128:(j + 1) * 128], wo[:, c, hf * 512:(hf + 1) * 512], c == 0, c == 7) for c in range(8)], oTk + kk_('w_mo'), [('ps', 6 + hf)])
                        TT(P, 'vector', x2[:, hf * 512:(hf + 1) * 512], pb, xt[i][:, hf * 512:(hf + 1) * 512], ALU.add, [('ps', 6 + hf), ('xtB', i)], [('x2B', hf)])
                    P.dma('scalar', x_dst[S * 512 + j * 128: S * 512 + (j + 1) * 128, :], x2[:, :], reads=[('x2B', 0), ('x2B', 1)], sem='x2B')
                    yield

            def run_all(gs):
                gs = [g for g in gs if g is not None]
                while gs:
                    for g in list(gs):
                        try:
                            next(g)
                        except StopIteration:
                            gs.remove(g)
            load_norm(0, 512, gM[:, :], 'gM', x_src, tb=2)
            run_all([g_qproj(0)])
            for S in range(NSUP):
                nxt = None
                if S + 1 < NSUP:
                    load_norm((S + 1) * 512, 512, gM[:, :], 'gM', x_src, tb=2)
                    nxt = g_qproj(S + 1)
                run_all([g_soft(S), g_out(S - 1) if S > 0 else None, nxt])
            run_all([g_out(NSUP - 1)])
        P.barrier()
        esp.__exit__(None, None, None)
    else:
        load_w('w_mq', wq)
        load_w('w_mo', wo)
    if do_sample:
        with ExitStack() as ess:
            NG = 4
            Kb4 = [P.sb([128, NG, 2, D], BF16, "Kb4_%d" % i, ess) for i in range(2)]
            Vb4 = [P.sb([128, NG, 2, D], BF16, "Vb4_%d" % i, ess) for i in range(2)]
            KT4 = P.sb([128, NG, 8, 256], BF16, "KT4", ess)
            hTk = load_norm(T, NS, gM[:, :], 'gM', x_src)
            qTk = qproj(NS, hTk)
            ck = io['cache_mem_k'].rearrange("b (mt p) h d -> b p mt (h d)", p=128)
            cv = io['cache_mem_v'].rearrange("b (mt p) h d -> b p mt (h d)", p=128)
            MEMSET(P, 'vector', k.ps[:, 512 * 2: 512 * 6], 0.0, [('ps', 2), ('ps', 3), ('ps', 4), ('ps', 5)])
            CP(P, 'vector', pn[:, :, :], k.ps[:, 512 * 4: 512 * 6].rearrange("p (h m) -> p h m", h=4), [('ps', 4), ('ps', 5)], ['pnB'])
            CP(P, 'vector', ob[:, :], k.ps[:, 512 * 2: 512 * 4], [('ps', 2), ('ps', 3)], ['obB'])

            def loads(g):
                r = g % 2
                for q in range(NG):
                    bq = NG * g + q
                    for mt in range(2):
                        P.dma('gpsimd', Kb4[r][:, q, mt, :], ck[bq, :, mt, :], writes=[('Kb4', r, q, mt)], sem=('Kb4', r, q, mt), max_dma_last_dim=4096)
                        P.dma('gpsimd', Vb4[r][:, q, mt, :], cv[bq, :, mt, :], writes=[('Vb4', r, q, mt)], sem=('Vb4', r, q, mt), max_dma_last_dim=4096)
            loads(0)
            NR_ = 32 * (NG - 1) + 4
            for g in range(16 // NG):
                r = g % 2
                if g + 1 < 16 // NG:
                    loads(g + 1)
                for q in range(NG):
                    for half in range(2):
                        tb = k.psb[:, 1024 * (6 + half): 1024 * (7 + half)].rearrange("p (c m) -> p c m", c=4)
                        TR(P, [(tb[:, cc, mt * 128:(mt + 1) * 128], Kb4[r][:, q, mt, (half * 4 + cc) * 128:(half * 4 + cc + 1) * 128], k.identb) for cc in range(4) for mt in range(2)],
                           [('Kb4', r, q, 0), ('Kb4', r, q, 1), 'identb'], [('ps', 6 + half)])
                        CP(P, 'vector' if half == 0 else 'scalar', KT4[:, q, half * 4:(half + 1) * 4, :], tb, [('ps', 6 + half)], [('KT4', q, half)])
                for q in range(NG):
                    bq = NG * g + q
                    Sq = k.ps[32 * q:32 * q + 4, 512 * 4: 512 * 6].rearrange("p (h m) -> p h m", h=4)
                    for h in range(4):
                        MM(P, [(Sq[:, h, :], qT[:, 2 * h + dc, bq * 4:(bq + 1) * 4], KT4[:, q, 2 * h + dc, :], dc == 0, dc == 1, (0, 32 * q)) for dc in range(2)],
                           qTk + [('KT4', q, 0), ('KT4', q, 1)], [('ps', 4 + h // 2)])
                Sps = k.ps[0:NR_, 512 * 4: 512 * 6].rearrange("p (h m) -> p h m", h=4)
                softmax_rows(Sps, NR_, [('ps', 4), ('ps', 5)])
                tb = k.psb[:, 1024 * 0: 1024 * 1].rearrange("p (c t) -> p c t", c=8)
                TR(P, [(tb[:, 2 * h + mc, 0:NR_], pn[0:NR_, h, mc * 128:(mc + 1) * 128], k.identb[0:NR_, 0:NR_]) for h in range(4) for mc in range(2)], ['pnB', 'identb'], [('ps', 0)])
                CP(P, 'vector', PT[:, :, 0:NR_], tb[:, :, 0:NR_], [('ps', 0)], [('PTB', 0)])
                for q in range(NG):
                    oq = k.ps[32 * q:32 * q + 4, 512 * 2: 512 * 4].rearrange("p (h d) -> p h d", h=4)
                    for h in range(4):
                        MM(P, [(oq[:, h, :], PT[:, 2 * h + mc, 32 * q:32 * q + 4], Vb4[r][:, q, mc, h * 256:(h + 1) * 256], mc == 0, mc == 1, (0, 32 * q)) for mc in range(2)],
                           [('PTB', 0), ('Vb4', r, q, 0), ('Vb4', r, q, 1)], [('ps', 2 + h // 2)])
                CP(P, 'scalar', ob[0:NR_, :], k.ps[0:NR_, 512 * 2: 512 * 4], [('ps', 2), ('ps', 3)], ['obB'])
                tb2 = k.psb[:, 1024 * 1: 1024 * 2].rearrange("p (c t) -> p c t", c=8)
                TR(P, [(tb2[:, c, 0:NR_], ob[0:NR_, c * 128:(c + 1) * 128], k.identb[0:NR_, 0:NR_]) for c in range(8)], ['obB', 'identb'], [('ps', 1)])
                CP(P, 'vector', oT[:, :, 4 * NG * g:4 * NG * (g + 1)].rearrange("p c (q t) -> p c q t", t=4),
                   tb2[:, :, :].rearrange("p c (q u) -> p c q u", u=32)[:, :, 0:NG, 0:4], [('ps', 1)], [('oTB', 'c')])
            oproj(T, NS, x_src, x_dst, [('oTB', 'c')])


XBC0 = 1024
DT0 = 2560
VP0 = 2576
DIN = 3600
ST_A = 256


def phase_A(P, k, es, io, x_src, x_dst, do_prompt=True, do_sample=True):
    win = P.sb([128, 8, DIN], BF16, "win", es)
    wdt3 = P.sb([128, 8, 96], BF16, "wdt3", es)
    wout = P.sb([128, 16, D], BF16, "wout", es)
    wpool = P.sb([128, 4, 2, 256], BF16, "wpool", es)
    cb2 = P.sb([128, 2048], BF16, "cstb2", es)
    P.dma('gpsimd', cb2[:], io['consts'][:, CST_SMALL + 256:CST_SMALL + 256 + 2048], writes=['cstb'], sem='cstb2', max_dma_last_dim=4096)
    k.e2b = cb2[:, 0:2048]
    win_d = io['w_in'].rearrange("(c p) n -> p c n", p=128)
    MEMSET(P, 'vector', wdt3[:], 0.0, ['wdt3'])
    for nm_, c0, c1 in (('xbc', XBC0, XBC0 + 768), ('xbc2', XBC0 + 768, DT0 + 16), ('z', 0, 1024), ('vp', VP0, DIN)):
        P.dma_multi('gpsimd', [(win[:, :, c0:c1], win_d[:, :, c0:c1])], [('win', nm_)], ('win', nm_), max_dma_last_dim=4096)
    for r in range(3):
        P.dma('gpsimd', wdt3[:, :, 32 * r:32 * r + 16], win_d[:, :, DT0:DT0 + 16], reads=['wdt3'], writes=[('wdt3', r)], sem=('wdt3', r))
    wout_d = io['w_out'].rearrange("(c p) n -> p c n", p=128)
    for gq in range(2):
        P.dma_multi('gpsimd', [(wout[:, 8 * gq:8 * gq + 8, :], wout_d[:, 8 * gq:8 * gq + 8, :])], [('wout', c) for c in range(8 * gq, 8 * gq + 8)], ('wout', gq), max_dma_last_dim=4096)
    wp_d = io['w_pool'].rearrange("g (cc p) d -> p g cc d", p=128)
    P.dma_multi('gpsimd', [(wpool[:, 0:2, :, :], wp_d[:, 0:2, :, :]), (wpool[:, 2:4, :, :], wp_d[:, 2:4, :, :])], [('wpool', g) for g in range(4)], 'wpool')
    wink = [('win', nm_) for nm_ in ('xbc', 'xbc2', 'z', 'vp')]
    wdtk = [('wdt3', r) for r in range(3)]
    woutk = [('wout', c) for c in range(16)]
    wpk = [('wpool', g) for g in range(4)]

    gA = P.sb([128, 8], F32, "gA", es)
    gY = P.sb([128, 8], F32, "gY", es)
    psc = P.sb([128, 8], F32, "psc", es)
    load_fm(P, k, io['norm_mix'].rearrange("(c p) -> c p", p=128), 8, gA[:, :], 'gA', 'gA', es)
    load_fm(P, k, io['ssm_norm'].rearrange("(c p) -> c p", p=128), 8, gY[:, :], 'gY', 'gY', es)
    load_fm(P, k, io['pool_scale'].rearrange("(c p) -> c p", p=128), 8, psc[:, :], 'psc', 'psc', es)
    cw = P.sb([128, 4, 12], F32, "cwA", es)
    cb = P.sb([128, 12], F32, "cbA", es)
    cwd = io['ssm_conv_w'].rearrange("k (c p) -> k c p", p=128)
    for kk in range(4):
        load_fm(P, k, cwd[kk], 12, cw[:, kk, :], ('cwA', kk), 'cwA%d' % kk, es)
    load_fm(P, k, io['ssm_conv_b'].rearrange("(c p) -> c p", p=128), 12, cb[:, :], 'cbA', 'cbA', es)
    cwk = [('cwA', kk) for kk in range(4)] + ['cbA']
    TS(P, 'vector', cw[:], cw[:], 0.5, None, ALU.mult, None, cwk, cwk[:4])
    TS(P, 'vector', cb[:], cb[:], 0.5, None, ALU.mult, None, ['cbA'], ['cbA'])
    hp = P.sb([128, 4], F32, "hpA", es)
    MEMSET(P, 'vector', hp[:], 0.0, ['hpA'])
    for r in range(3):
        P.dma('sync', hp[32 * r:32 * r + 16, 0:1], io['ssm_dt_bias'].rearrange("(h o) -> h o", o=1), reads=['hpA'], writes=[('hpA', r)], sem=('hpA', r))
        P.dma('sync', hp[32 * r:32 * r + 16, 1:2], io['ssm_a_log'].rearrange("(h o) -> h o", o=1), reads=['hpA'], writes=[('hpA', r)], sem=('hpA', r))
    hpk = [('hpA', r) for r in range(3)]
    ACTF(P, hp[0:96, 2:3], hp[0:96, 1:2], AF.Exp, hpk, ['hpA2'])
    TS(P, 'vector', hp[0:96, 2:3], hp[0:96, 2:3], -1.0, None, ALU.mult, None, ['hpA2'], ['hpA2'])
    hb16 = P.sb([128, 48], F32, "hb16", es)
    P.dma('sync', hb16[:, 0:16], io['ssm_d'].partition_broadcast(128), writes=['hb16d'], sem='hb16d')
    P.dma('sync', hb16[:, 16:32], io['ssm_a_log'].partition_broadcast(128), writes=['hb16a'], sem='hb16a')
    ACTF(P, hb16[:, 32:48], hb16[:, 16:32], AF.Exp, ['hb16a'], ['hb16A'])
    TS(P, 'vector', hb16[:, 32:48], hb16[:, 32:48], -1.0, None, ALU.mult, None, ['hb16A'], ['hb16A'])
    Dbc = hb16[:, 0:16]
    Abc = hb16[:, 32:48]
    onesf = k.cs('ones')

    NSUB = ST_A // 128
    xt = [P.sb([128, D], F32, "xtA%d" % i, es) for i in range(2)]
    hb = k.junk
    st = P.sb([128, 64], F32, "stA", es)
    hT = P.sb([128, 8, ST_A], BF16, "hTA", es)
    xcT2 = [P.sb([128, 12, ST_A], BF16, "xcT0", es), None]
    acc = [P.sb([128, ST_A], F32, "accA%d" % i, es) for i in range(2)]
    th = [P.sb([128, ST_A], F32, "thA0", es)] * 2
    sz2 = [P.sb([128, NSUB, D], BF16, "szA0", es), None]
    pl = P.sb([128, 4, ST_A], BF16, "plA", es)
    pmT2 = [P.sb([128, 8, ST_A], BF16, "pmT0", es), None]
    ynT = P.sb([128, 8, ST_A], BF16, "ynT", es)
    dt3 = P.sb([128, ST_A], F32, "dt3A", es)
    a3 = P.sb([128, ST_A], F32, "a3A", es)
    d1 = a3
    acs3 = P.sb([128, ST_A], F32, "acs3A", es)
    stk2 = [P.sb([128, ST_A], F32, "stkA0", es), None]
    hl2 = [P.sb([128, ST_A], BF16, "hlA0", es), None]
    ptmp = P.sb([128, 2, 320], F32, "ptmpA", es)
    zt = P.sb([128, 512], F32, "ztA", es)
    tk = P.sb([128, 128], F32, "tkA", es)
    sml = P.sb([128, 64], F32, "smlA", es)
    xtok = P.sb([128, D], BF16, "xtokA", es)
    xdt = P.sb([128, D], BF16, "xdtA", es)
    xdtE = P.sb([128, D], BF16, "xdtEA", es)
    Btok = P.sb([128, 256], BF16, "BtokA", es)
    dcy = [P.sb([128, 4, 128], F32, "dcyA%d" % i, es) for i in range(2)]
    MT = P.sb([128, 16, 128], BF16, "MTA", es)
    y1 = P.sb([128, D], F32, "y1A", es)
    y2 = P.sb([128, D], F32, "y2A", es)
    yn = P.sb([128, D], BF16, "ynA", es)
    hst = P.sb([128, D], F32, "hstA", es)
    hbf = P.sb([128, D], BF16, "hbfA", es)
    cnt = {'x': 0, 'st': 0, 'p': 0, 'a': 0, 'd': 0}
    MEMSET(P, 'vector', y1[:], 0.0, [('y1A', 0), ('y1A', 1)])
    MEMSET(P, 'vector', ptmp[:], 0.0, [('ptmpA', 0), ('ptmpA', 1)])

    def v3(ap, nseq, a, b):
        return ap.rearrange("p (s l) -> p s l", s=nseq)[:, :, a:b]

    def run_group(tok0, nseq, L, extx, extv, first, negb, rsp, h0_bf_key, gname, sample_fn=None, par=0):
        xcT, sz, pmT, stk, hl = xcT2[par], sz2[par], pmT2[par], stk2[par], hl2[par]
        KP = 'p%d' % par
        ntok = nseq * L
        nsub = (ntok + 127) // 128
        exk = [('extx' + gname, c) for c in range(12)]
        evk = [('extv' + gname, c) for c in range(8)]
        for j in range(nsub):
            n = min(128, ntok - j * 128)
            i = cnt['x'] % 2
            cnt['x'] += 1
            P.dma('sync', xt[i][0:n, :], x_src[tok0 + j * 128: tok0 + j * 128 + n, :], writes=[('xtA', i)], sem=('xtA', i))
            col = (cnt['st'] % 8) * 4
            cnt['st'] += 1
            norm_T(P, k, xt[i][0:n, :], n, [('xtA', i)], gA[:, :], 'gA', hT[:, :, j * 128: j * 128 + n], ('hTA', j), st, col, hb, 'junk', 2)
        hTk = [('hTA', j) for j in range(nsub)]

        def proj_fm(col0, width, lhs_w, wkeys):
            b = cnt['p'] % 2
            cnt['p'] += 1
            pb = k.ps[0:width, 512 * b: 512 * b + ntok]
            MM(P, [(pb, lhs_w[:, kc, col0:col0 + width], hT[:, kc, 0:ntok], kc == 0, kc == 7) for kc in range(8)], wkeys + hTk, [('ps', b)])
            return pb, ('ps', b)

        xck = [('xcT' + KP, c) for c in range(12)]

        def sec_xbc():
            halo, fixx = extx
            fk = ['fixx0' + gname, 'fixx1' + gname, 'fixx2' + gname, 'fixt' + gname]
            wv = [bcast(cw[:, kk, :], 2, nseq) for kk in range(4)]
            hh = [halo[:, :, :, r_] for r_ in range(3)]
            tmpf = fixx[:, :, :, 3]
            TT(P, 'gpsimd', fixx[:, :, :, 0], hh[0], wv[0], ALU.mult, exk + cwk, [fk[0]])
            TT(P, 'gpsimd', tmpf, hh[1], wv[1], ALU.mult, exk + cwk, [fk[3]])
            TT(P, 'gpsimd', fixx[:, :, :, 0], fixx[:, :, :, 0], tmpf, ALU.add, [fk[0], fk[3]], [fk[0]])
            TT(P, 'gpsimd', tmpf, hh[2], wv[2], ALU.mult, exk + cwk + [fk[0]], [fk[3]])
            TT(P, 'gpsimd', fixx[:, :, :, 0], fixx[:, :, :, 0], tmpf, ALU.add, [fk[0], fk[3]], [fk[0]])
            TT(P, 'gpsimd', fixx[:, :, :, 1], hh[1], wv[0], ALU.mult, exk + cwk, [fk[1]])
            TT(P, 'gpsimd', tmpf, hh[2], wv[1], ALU.mult, exk + cwk + [fk[0]], [fk[3]])
            TT(P, 'gpsimd', fixx[:, :, :, 1], fixx[:, :, :, 1], tmpf, ALU.add, [fk[1], fk[3]], [fk[1]])
            TT(P, 'gpsimd', fixx[:, :, :, 2], hh[2], wv[0], ALU.mult, exk + cwk + [fk[1]], [fk[2]])
            fks = fk[:3]

            def xbc_stage1(c):
                pb, pk = proj_fm(XBC0 + c * 128, 128, win, [('win', 'xbc' if c < 6 else 'xbc2')])
                r = c % 2
                a_ap = v3(acc[r][:, 0:ntok], nseq, 0, L)
                p3 = v3(pb, nseq, 0, L)
                ACTF(P, a_ap, p3, AF.Identity, [pk] + cwk, [('accA', r)], bias=cb[:, c:c + 1], scale=cw[:, 3, c:c + 1])
                for sh in (1, 2, 3):
                    STT(P, a_ap[:, :, sh:L], p3[:, :, 0:L - sh], cw[:, 3 - sh, c:c + 1], a_ap[:, :, sh:L], ALU.mult, ALU.add, [pk, ('accA', r)] + cwk, [('accA', r)])
                TT(P, 'vector', a_ap[:, :, 0:3], a_ap[:, :, 0:3], fixx[:, c, :, 0:3], ALU.add, [('accA', r)] + fks, [('accA', r)])
                CP(P, 'vector', halo[:, c, :, :], p3[:, :, L - 3:L], [pk] + fk, [exk[c]])

            def xbc_stage2(c):
                r = c % 2
                a_ap = v3(acc[r][:, 0:ntok], nseq, 0, L)
                t_ap = v3(th[c % 2][:, 0:ntok], nseq, 0, L)
                ACTF(P, t_ap, a_ap, AF.Tanh, [('accA', r)], [('thA', 0)])
                STT(P, v3(xcT[:, c, 0:ntok], nseq, 0, L), t_ap, 1.0, a_ap, ALU.add, ALU.mult, [('thA', 0), ('accA', r)], [('xcT' + KP, c)])
            for c in range(13):
                if c < 12:
                    xbc_stage1(c)
                if c >= 1:
                    xbc_stage2(c - 1)
                yield
            xck = [('xcT' + KP, c) for c in range(12)]

            pb, pk = proj_fm(0, 96, wdt3, wdtk)
            ACTF(P, d1[0:96, 0:ntok], pb, AF.Exp, [pk] + hpk, ['a3A'], bias=hp[0:96, 0:1])
            ACTF(P, dt3[0:96, 0:ntok], d1[0:96, 0:ntok], AF.Ln, ['a3A'], ['dt3A'], bias=1.0)
            TS(P, 'vector', a3[0:96, 0:ntok], dt3[0:96, 0:ntok], hp[0:96, 2:3], None, ALU.mult, None, ['dt3A', 'hpA2'], ['a3A'])
            P.V(lambda e: e.tensor_tensor_scan(out=acs3[0:96, 0:ntok], data0=rsp[0:96, 0:ntok], data1=a3[0:96, 0:ntok], initial=0.0, op0=ALU.mult, op1=ALU.add),
                reads=['a3A', 'cst'], writes=['acs3A'])
            CP(P, 'vector', stk[0:96, 0:ntok], dt3[0:96, 0:ntok], ['dt3A'], ['stkA' + KP])
            CP(P, 'vector', stk[32:48, 0:ntok], acs3[32:48, 0:ntok], ['acs3A', 'stkA' + KP], ['stkA' + KP])
            nch = ntok // min(L, 128)
            Lc = min(L, 128)
            a3v = acs3[64:80, 0:ntok].rearrange("p (s l) -> p s l", l=Lc)
            TT(P, 'vector', stk[64:80, 0:ntok].rearrange("p (s l) -> p s l", l=Lc), a3v, bcast(a3v[:, :, Lc - 1], 2, Lc), ALU.subtract, ['acs3A', 'stkA' + KP], ['stkA' + KP])
            CP(P, 'vector', hl[0:96, 0:ntok], acs3[0:96, 0:ntok], ['acs3A'], ['hlA' + KP])
            TT(P, 'vector', hl[32:48, 0:ntok], acs3[32:48, 0:ntok], hl[32:48, 0:ntok], ALU.subtract, ['acs3A', 'hlA' + KP], ['hlA' + KP])

            yield
            yield

        def sec_z():
            for j in range(nsub):
                n = min(128, ntok - j * 128)
                for hf in range(2):
                    b = cnt['p'] % 2
                    cnt['p'] += 1
                    pb = k.ps[0:n, 512 * b: 512 * b + 512]
                    MM(P, [(pb, hT[:, kc, j * 128: j * 128 + n], win[:, kc, hf * 512:(hf + 1) * 512], kc == 0, kc == 7) for kc in range(8)], [('win', 'z')] + hTk, [('ps', b)])
                    r = cnt['a'] % 2
                    cnt['a'] += 1
                    ACTF(P, zt[0:n, :], pb, AF.Tanh, [('ps', b)], ['ztA'], scale=0.5)
                    STT(P, sz[0:n, j, hf * 512:(hf + 1) * 512], zt[0:n, :], 1.0, pb, ALU.add, ALU.mult, ['ztA', ('ps', b)], [('szA' + KP, j, hf)])
                    yield

            yield

        def sec_pool_a():
            if nseq == 1 and not first:
                CP(P, 'vector', extv[:, :, 0, 0:15], extv[:, :, 0, L:L + 15], evk, evk)
            for c in range(8):
                pb, pk = proj_fm(VP0 + c * 128, 128, win, [('win', 'vp')])
                CP(P, 'scalar', extv[:, c, :, 15:L + 15], v3(pb, nseq, 0, L), [pk], [evk[c]])
                if c % 2 == 1:
                    yield
            yield

        def sec_pool():
            def wpool_chunk(co):
                g, dh = co // 2, co % 2
                b = cnt['p'] % 2
                cnt['p'] += 1
                pb = k.ps[:, 512 * b: 512 * b + ntok]
                MM(P, [(pb, wpool[:, g, cc, dh * 128:(dh + 1) * 128], pl[:, (2 * g + cc) % 4, 0:ntok], cc == 0, cc == 1) for cc in range(2)],
                   wpk + [('plA', (2 * g) % 4), ('plA', (2 * g + 1) % 4)], [('ps', b)])
                ACTF(P, pmT[:, co, 0:ntok], pb, AF.Identity, [('ps', b), 'psc'], [('pmT' + KP, co)], scale=psc[:, co:co + 1])

            for c in range(8):
                gi = c // 2
                w = 2 << gi
                cur = extv[:, c, :, :]
                tot = L + 15
                step = 1
                bufs = [ptmp[:, 0, 0:nseq * tot].rearrange("p (s l) -> p s l", s=nseq), ptmp[:, 1, 0:nseq * tot].rearrange("p (s l) -> p s l", s=nseq)]
                bkeys = [('ptmpA', 0), ('ptmpA', 1)]
                bi = 0
                ckeys = [evk[c]]
                while step < w:
                    o = bufs[bi]
                    TT(P, 'gpsimd', o[:, :, step:tot], cur[:, :, step:tot], cur[:, :, 0:tot - step], ALU.add, ckeys, [bkeys[bi]])
                    cur = o
                    ckeys = [bkeys[bi]]
                    bi ^= 1
                    step *= 2
                o = bufs[bi]
                TS(P, 'gpsimd', o[:, :, 15:tot], cur[:, :, 15:tot], 1.0 / w, 0.0, ALU.mult, ALU.add, ckeys, [bkeys[bi]])
                if first and nseq == 1:
                    TT(P, 'gpsimd', o[:, :, 15:31], cur[:, :, 15:31], k.cs('csc')[:, gi * 16:(gi + 1) * 16].unsqueeze(1), ALU.mult, ckeys + ['cst'], [bkeys[bi]])
                TT(P, 'gpsimd', v3(pl[:, c % 4, 0:ntok], nseq, 0, L), o[:, :, 15:tot], extv[:, c, :, 15:tot], ALU.subtract, [bkeys[bi], evk[c]], [('plA', c % 4)])
                yield
                if c % 2 == 1:
                    for co in (c - 1, c):
                        wpool_chunk(co)
                    yield
            yield

        yield from sec_xbc()
        yield from sec_z()
        yield 'POOLA'
        yield from sec_pool_a()
        pmk = [('pmT' + KP, c) for c in range(8)]
        gpb = sec_pool()

        def adv(nsteps=1):
            for _ in range(nsteps):
                try:
                    next(gpb)
                except StopIteration:
                    return

        yield 'SPLIT'
        for j in range(nsub):
            n = min(128, ntok - j * 128)
            js = slice(j * 128, j * 128 + n)
            xb = k.psb[0:n, 1024 * 2: 1024 * 2 + 1024].rearrange("p (c t) -> p c t", c=8)
            TR(P, [(xb[:, c, :], xcT[:, c, js], k.identb) for c in range(8)], xck + ['identb'], [('ps', 2)])
            sp = k.ps[0:n, 512 * 3: 512 * 3 + 96]
            TR(P, [(sp, stk[0:96, js], k.identf[0:96, 0:96])], ['stkA' + KP, 'cst'], [('ps', 3)])
            CP(P, 'vector', tk[0:n, 0:96], sp, [('ps', 3)], ['tkA'])
            bb = k.psb[0:n, 1024 * 3 + 256: 1024 * 3 + 512].rearrange("p (c t) -> p c t", c=2)
            TR(P, [(bb[:, g, :], xcT[:, 8 + g, js], k.identb) for g in range(2)], xck + ['identb'], [('ps', 3)])
            CP(P, 'vector', Btok[0:n, :].rearrange("p (c t) -> p c t", c=2), bb, [('ps', 3)], ['BtokA'])
            TS(P, 'vector', sml[0:n, 0:16], tk[0:n, 32:48], -1.0, None, ALU.mult, None, ['tkA'], ['nacs'])
            ACTF(P, sml[0:n, 16:32], tk[0:n, 32:48], AF.Exp, ['tkA'], ['eacs'])
            ACTF(P, sml[0:n, 32:48], tk[0:n, 64:80], AF.Exp, ['tkA'], ['dend'], scale=-1.0)
            TT(P, 'vector', tk[0:n, 96:112], tk[0:n, 0:16], Abc[0:n, :], ALU.mult, ['tkA', 'hb16A'], ['atok'])
            CP(P, 'scalar', xtok[0:n, :].rearrange("p (c t) -> p c t", c=8), xb, [('ps', 2)], ['xtokA'])
            TT(P, 'vector', xdt[0:n, :].rearrange("p (h q) -> p h q", h=16), k.psb[0:n, 1024 * 2: 1024 * 2 + 1024].rearrange("p (h q) -> p h q", h=16),
               bcast(tk[0:n, 0:16], 2, 64), ALU.mult, [('ps', 2), 'tkA'], ['xdtA'])
            TT(P, 'vector', xdtE[0:n, :].rearrange("p (h q) -> p h q", h=16), xdt[0:n, :].rearrange("p (h q) -> p h q", h=16),
               bcast(sml[0:n, 32:48], 2, 64), ALU.mult, ['xdtA', 'dend'], ['xdtEA'])
            adv(2)
            yield
            cbp = k.ps[0:n, 512 * 3: 512 * 3 + 2 * n].rearrange("p (g l) -> p g l", g=2)
            MM(P, [(cbp[:, g, :], xcT[:, 8 + g, js], xcT[:, 10 + g, js], True, True) for g in range(2)], xck, [('ps', 3)])
            for q in range(4):
                b = 4 + q % 2
                dp = k.ps[0:n, 512 * b: 512 * b + 4 * n].rearrange("p (h l) -> p h l", h=4)
                items = []
                for hh in range(4):
                    h = 4 * q + hh
                    items.append((dp[:, hh, :], k.e2b[0:64, h * 128: h * 128 + n], hl[0:64, js], True, False))
                    items.append((dp[:, hh, :], k.identb[0:n, 0:n], negb[0:n, 0:n], False, True))
                MM(P, items, ['hlA' + KP, 'cstb', 'identb'], [('ps', b)])
                r = cnt['d'] % 2
                cnt['d'] += 1
                for hh in range(4):
                    h = 4 * q + hh
                    ACTF(P, dcy[r][0:n, hh, 0:n], dp[:, hh, :], AF.Exp, [('ps', b), 'nacs'], [('dcyA', r, hh)], bias=sml[0:n, h:h + 1])
                g = q // 2
                TT(P, 'vector', MT[0:n, 4 * q:4 * q + 4, 0:n], dcy[r][0:n, :, 0:n], bcast(cbp[:, g, :], 1, 4), ALU.mult,
                   [('dcyA', r, hh) for hh in range(4)] + [('ps', 3)], [('MTA', q)])
                adv(2)
                yield
            MTk = [('MTA', q) for q in range(4)]
            yd = k.ps[0:n, 512 * 6: 512 * 8].rearrange("p (h q) -> p h q", h=16)
            for half in range(2):
                MM(P, [(yd[:, h, :], MT[0:n, h, 0:n], xdt[0:n, h * 64:(h + 1) * 64], True, True) for h in range(8 * half, 8 * half + 8)],
                   MTk + ['xdtA'], [('ps', 6 + half)])
            adv(2)
            yield
            yo = k.ps[0:n, 512 * 4: 512 * 6]
            if sample_fn is not None:
                sample_fn(MTk, xck, xcT, hl, 'hlA' + KP)
                h0_bf_key = 'sample'
                TT(P, 'vector', y1[0:n, :].rearrange("p (h q) -> p h q", h=16), yo.rearrange("p (h q) -> p h q", h=16), bcast(sml[0:n, 16:32], 2, 64), ALU.mult,
                   [('ps', 4), ('ps', 5), 'eacs'], [('y1A', 0), ('y1A', 1)])
            elif h0_bf_key is not None:
                for g in range(2):
                    MM(P, [(yo[:, g * 512:(g + 1) * 512], xcT[:, 10 + g, js], hbf[:, g * 512:(g + 1) * 512], True, True)], xck + [h0_bf_key], [('ps', 4 + g)])
                TT(P, 'vector', y1[0:n, :].rearrange("p (h q) -> p h q", h=16), yo.rearrange("p (h q) -> p h q", h=16), bcast(sml[0:n, 16:32], 2, 64), ALU.mult,
                   [('ps', 4), ('ps', 5), 'eacs'], [('y1A', 0), ('y1A', 1)])
            TT(P, 'gpsimd', y2[0:n, :].rearrange("p (h q) -> p h q", h=16), xtok[0:n, :].rearrange("p (h q) -> p h q", h=16), bcast(Dbc[0:n, :], 2, 64), ALU.mult,
               ['xtokA', 'hb16d'], [('y2A', 0), ('y2A', 1)])
            if h0_bf_key is not None:
                TT(P, 'vector', y1[0:n, :], y1[0:n, :], y2[0:n, :], ALU.add, [('y1A', 0), ('y1A', 1), ('y2A', 0), ('y2A', 1)], [('y1A', 0), ('y1A', 1)])
                ysrc, ysk = y1, [('y1A', 0), ('y1A', 1)]
            else:
                ysrc, ysk = y2, [('y2A', 0), ('y2A', 1)]
            TT(P, 'vector', y1[0:n, :], k.ps[0:n, 512 * 6: 512 * 8], ysrc[0:n, :], ALU.add, [('ps', 6), ('ps', 7)] + ysk, [('y1A', 0), ('y1A', 1)])
            TT(P, 'vector', y1[0:n, :], y1[0:n, :], sz[0:n, j, :], ALU.mult, [('y1A', 0), ('y1A', 1), ('szA' + KP, j, 0), ('szA' + KP, j, 1)], [('y1A', 0), ('y1A', 1)])
            adv(2)
            yield
            for g in range(2):
                col = (cnt['st'] % 8) * 4
                cnt['st'] += 1
                r_ap, ks = rstd_op(P, k, y1[0:n, g * 512:(g + 1) * 512], n, [('y1A', 0), ('y1A', 1)], st, col, scale=0.25 / 512, extra=0.5)
                TS(P, 'vector', yn[0:n, g * 512:(g + 1) * 512], y1[0:n, g * 512:(g + 1) * 512], r_ap, None, ALU.mult, None, [('y1A', 0), ('y1A', 1), ks], [('ynA', g)])
            yb = k.psb[:, 1024 * 2: 1024 * 2 + 1024].rearrange("p (c t) -> p c t", c=8)
            TR(P, [(yb[:, c, 0:n], yn[0:n, c * 128:(c + 1) * 128], k.identb[0:n, 0:n]) for c in range(8)], [('ynA', 0), ('ynA', 1), 'identb'], [('ps', 2)])
            TT(P, 'vector', ynT[:, :, js], yb[:, :, 0:n], bcast(gY[:, :], 2, n), ALU.mult, [('ps', 2), 'gY'], [('ynT', j)])
            adv(2)
            yield
            if nseq == 1:
                sp2 = k.ps[:, 512 * 6: 512 * 8]
                for g in range(2):
                    MM(P, [(sp2[:, g * 512:(g + 1) * 512], Btok[0:n, g * 128:(g + 1) * 128], xdtE[0:n, g * 512:(g + 1) * 512], True, True)], ['BtokA', 'xdtEA'], [('ps', 6 + g)])
                cdp = k.ps[:, 512 * 3 + 256: 512 * 3 + 272]
                MM(P, [(cdp, onesf[0:n, :], tk[0:n, 96:112], True, True)], ['cst', 'atok'], [('ps', 3)])
                ACTF(P, tk[:, 112:128], cdp, AF.Exp, [('ps', 3)], ['cdA'])
                if h0_bf_key is not None:
                    TT(P, 'vector', y2[:, :].rearrange("p (h q) -> p h q", h=16), hst[:, :].rearrange("p (h q) -> p h q", h=16), bcast(tk[:, 112:128], 2, 64), ALU.mult,
                       ['hstA', 'cdA'], [('y2A', 0), ('y2A', 1)])
                    TT(P, 'vector', hst[:, :], sp2, y2[:, :], ALU.add, [('ps', 6), ('ps', 7), ('y2A', 0), ('y2A', 1)], ['hstA'])
                else:
                    CP(P, 'vector', hst[:, :], sp2, [('ps', 6), ('ps', 7)], ['hstA'])
                CP(P, 'scalar', hbf[:, :], hst[:, :], ['hstA'], ['hbfA'])
                h0_bf_key = 'hbfA'
            adv(2)
            yield
        ynk = [('ynT', j) for j in range(nsub)]
        adv(100)
        for j in range(nsub):
            n = min(128, ntok - j * 128)
            js = slice(j * 128, j * 128 + n)
            i = cnt['x'] % 2
            cnt['x'] += 1
            P.dma('sync', xt[i][0:n, :], x_src[tok0 + j * 128: tok0 + j * 128 + n, :], writes=[('xtA', i)], sem=('xtA', i))
            for hf in range(2):
                b = cnt['p'] % 2
                cnt['p'] += 1
                pb = k.ps[0:n, 512 * b: 512 * b + 512]
                items = [(pb, ynT[:, c, js], wout[:, c, hf * 512:(hf + 1) * 512], c == 0, False) for c in range(8)]
                items += [(pb, pmT[:, c, js], wout[:, 8 + c, hf * 512:(hf + 1) * 512], False, c == 7) for c in range(8)]
                MM(P, items, ynk + pmk + woutk, [('ps', b)])
                TT(P, 'vector', xt[i][0:n, hf * 512:(hf + 1) * 512], pb, xt[i][0:n, hf * 512:(hf + 1) * 512], ALU.add, [('ps', b), ('xtA', i)], [('xtA', i)])
            P.dma('sync', x_dst[tok0 + j * 128: tok0 + j * 128 + n, :], xt[i][0:n, :], reads=[('xtA', i)], sem=('x1o', i))
            adv(2)
            yield

    def rows_out(src_fm, nrows_per, nchunks, skeys, dst_rows, tag):
        R = nrows_per
        for q in range(0, nchunks, 4):
            m = min(4, nchunks - q)
            pb = k.ps[0:R, 512 * 3: 512 * 3 + m * 128]
            for u in range(m):
                sap = src_fm(q + u)
                if len(sap.shape) > 2:
                    CP(P, 'vector', tk[:, 0:R].rearrange("p (a b) -> p a b", a=sap.shape[1]), sap, skeys, ['tkA'])
                    sap, sk2 = tk[:, 0:R], ['tkA']
                else:
                    sk2 = skeys
                TR(P, [(pb[:, u * 128:(u + 1) * 128], sap, k.identf)], sk2 + ['cst'], [('ps', 3)])
            CP(P, 'vector', y2[0:R, 0:m * 128], pb, [('ps', 3)], [('y2A', 0), ('y2A', 1)])
            P.dma('sync', dst_rows[:, q * 128:(q + m) * 128], y2[0:R, 0:m * 128], reads=[('y2A', 0), ('y2A', 1)], sem='rows' + tag)

    if do_prompt:
        with ExitStack() as esp:
            xcT2[1] = P.sb([128, 12, ST_A], BF16, "xcT1", esp)
            sz2[1] = P.sb([128, NSUB, D], BF16, "szA1", esp)
            pmT2[1] = P.sb([128, 8, ST_A], BF16, "pmT1", esp)
            stk2[1] = P.sb([128, ST_A], F32, "stkA1", esp)
            hl2[1] = P.sb([128, ST_A], BF16, "hlA1", esp)
            haloP = P.sb([128, 12, 1, 3], F32, "haloxP", esp)
            fixP = P.sb([128, 12, 1, 4], F32, "fixxP", esp)
            extx = (haloP, fixP)
            extv = P.sb([128, 8, 1, ST_A + 15], F32, "extvP", esp)
            MEMSET(P, 'vector', haloP[:], 0.0, [('extxP', c) for c in range(12)])
            MEMSET(P, 'vector', extv[:], 0.0, [('extvP', c) for c in range(8)])
            NSUP = T // ST_A
            RB, RF = 1, 1
            gens = [run_group(S * ST_A, 1, ST_A, extx, extv, S == 0, k.negcb, k.cs('rsp'), (None if S == 0 else 'hbfA'), 'P', par=S % 2) for S in range(NSUP)]

            hold = {}

            def step(g, front, back_alive=False):
                if front and hold.get(id(g)) and back_alive:
                    return True
                hold.pop(id(g), None)
                try:
                    v = next(g)
                except StopIteration:
                    return False
                if front and v == 'POOLA' and back_alive:
                    hold[id(g)] = True
                return not (front and v == 'SPLIT')
            while step(gens[0], True):
                pass
            for S in range(NSUP):
                gb = gens[S]
                gf = gens[S + 1] if S + 1 < NSUP else None
                ab, af = True, gf is not None
                while ab or af:
                    for _ in range(RB):
                        if ab:
                            ab = step(gb, False)
                    for _ in range(RF):
                        if af:
                            af = step(gf, True, ab)
            rows_out(lambda c: haloP[:, c, 0, :], 3, 12, [('extxP', c) for c in range(12)], io['conv_prompt'], 'cp')
            rows_out(lambda c: extv[:, c, 0, ST_A:ST_A + 15], 15, 8, [('extvP', c) for c in range(8)], io['pool_prompt'], 'pp')
            for half in range(2):
                pb = k.ps[:, 512 * (4 + half): 512 * (5 + half)]
                TR(P, [(pb[:, u * 128:(u + 1) * 128], hst[:, (4 * half + u) * 128:(4 * half + u + 1) * 128], k.identf) for u in range(4)], ['hstA', 'cst'], [('ps', 4 + half)])
                CP(P, 'vector', y1[:, half * 512:(half + 1) * 512], pb, [('ps', 4 + half)], [('y1A', half)])
            P.dma('sync', io['ssm_prompt'].rearrange("(c p) n -> p c n", p=128), y1[:, :].rearrange("p (c n) -> p c n", c=8), reads=[('y1A', 0), ('y1A', 1)], sem='ssmP')
            P.barrier()

    if do_sample:
        with ExitStack() as ess:
            haloS = P.sb([128, 12, 16, 3], F32, "haloxS", ess)
            fixS = P.sb([128, 12, 16, 4], F32, "fixxS", ess)
            extx = (haloS, fixS)
            extv = P.sb([128, 8, 16, 19], F32, "extvS", ess)
            h0a = P.sb([128, 8, 128], F32, "h0S", ess)
            cdT = P.sb([128, 8, 16], F32, "cdTS", ess)
            cb3 = P.sb([128, 1024], BF16, "cstb3", ess)
            P.dma('gpsimd', cb3[:], io['consts'][:, CST_SMALL + 256 + 2048:CST_SMALL + 256 + 3072], writes=['cstb3'], sem='cstb3', max_dma_last_dim=4096)
            k.expdb = cb3[:, :]
            exk = [('extxS', c) for c in range(12)]
            evk = [('extvS', c) for c in range(8)]
            y1k = [('y1A', 0), ('y1A', 1)]
            y2k = [('y2A', 0), ('y2A', 1)]
            scv = io['state_ssm_conv'].rearrange("b r c -> (b r) c")
            P.dma('sync', y1[0:48, :], scv[:, 0:1024], writes=y1k, sem='stS1')
            P.dma('sync', y2[0:48, 0:512], scv[:, 1024:1536], writes=y2k, sem='stS2')
            for c in range(12):
                src = y1[0:48, c * 128:(c + 1) * 128] if c < 8 else y2[0:48, (c - 8) * 128:(c - 7) * 128]
                bq_ = 3 - c % 2
                pb = k.ps[:, 512 * bq_: 512 * bq_ + 48]
                TR(P, [(pb, src, k.identf[0:48, 0:48])], y1k + y2k + ['cst'], [('ps', bq_)])
                CP(P, 'vector' if c % 2 == 0 else 'scalar', haloS[:, c, :, :], pb.rearrange("p (b r) -> p b r", r=3), [('ps', bq_)], [exk[c]])
            spv = io['state_pool'].rearrange("b r c -> (b r) c")
            for half in range(2):
                P.dma('sync', y1[0:120, :], spv[half * 120:(half + 1) * 120, :], reads=y1k, writes=y1k, sem=('stS3', half))
                for c in range(8):
                    bq_ = 3 - c % 2
                    pb = k.ps[:, 512 * bq_: 512 * bq_ + 120]
                    TR(P, [(pb, y1[0:120, c * 128:(c + 1) * 128], k.identf[0:120, 0:120])], y1k + ['cst'], [('ps', bq_)])
                    CP(P, 'vector' if c % 2 == 0 else 'scalar', extv[:, c, 8 * half:8 * half + 8, 0:15], pb.rearrange("p (b r) -> p b r", r=15), [('ps', bq_)], [evk[c]])
            ssd = io['state_ssm'].rearrange("b (c q) n -> b q c n", q=128)
            sso = io['ssm_sample'].rearrange("b (c q) n -> b q c n", q=128)

            def sample_fn(MTk, xck, xcT, hl, hlk):
                n = NS
                MEMSET(P, 'vector', MT[:], 0.0, MTk)
                ctm_diag = bass.AP(MT.tensor if hasattr(MT, 'tensor') else MT, 0, [[2048, 128], [1024, 2], [68, 16], [1, 4]])
                CP(P, 'vector', ctm_diag, xcT[:, 10:12, 0:64].rearrange("p g (b t) -> p g b t", t=4), xck + MTk, MTk)
                CTm = MT[:].rearrange("p h l -> p (h l)").rearrange("p (g b t) -> p g b t", g=2, b=16)
                cdp = k.ps[:, 512 * 3: 512 * 3 + 128].rearrange("p (c b) -> p c b", c=8)
                hl_last = hl[0:64, 0:64].rearrange("p (b t) -> p b t", t=4)[:, :, 3]
                MM(P, [(cdp[:, jc, :], k.expdb[0:64, jc * 128:(jc + 1) * 128], hl_last, True, True) for jc in range(8)], [hlk, 'cstb3'], [('ps', 3)])
                ACTF(P, cdT[:, :, :], cdp, AF.Exp, [('ps', 3)], ['cdTS'])
                for b in range(16):
                    h0 = h0a[:] if b % 2 == 0 else hst[:, :].rearrange("p (c n) -> p c n", c=8)
                    h0k = 'h0S' if b % 2 == 0 else 'hstA'
                    hn = (y1 if b % 2 == 0 else y2)[:, :].rearrange("p (c n) -> p c n", c=8)
                    hnk = y1k if b % 2 == 0 else y2k
                    if b == 0:
                        P.dma('sync', h0, ssd[0], writes=[h0k], sem=('h0S', 0))
                    if b + 1 < 16:
                        h0n = h0a[:] if (b + 1) % 2 == 0 else hst[:, :].rearrange("p (c n) -> p c n", c=8)
                        P.dma('sync', h0n, ssd[b + 1], writes=['h0S' if (b + 1) % 2 == 0 else 'hstA'], sem=('h0S', (b + 1) % 2))
                    tp = k.ps[:, 512 * 2: 512 * 4]
                    for half in range(2):
                        TR(P, [(tp[:, (4 * half + u) * 128:(4 * half + u + 1) * 128], h0[:, 4 * half + u, :], k.identf) for u in range(4)], [h0k, 'cst'], [('ps', 2 + half)])
                    CP(P, 'scalar', hbf[:, 0:512], tp[:, 0:512], [('ps', 2)], [('hbfA', 0)])
                    CP(P, 'vector', hbf[:, 512:1024], tp[:, 512:1024], [('ps', 3)], [('hbfA', 1)])
                    for g in range(2):
                        MM(P, [(k.ps[0:n, 512 * (4 + g): 512 * (5 + g)], CTm[:, g, b, :], hbf[:, g * 512:(g + 1) * 512], b == 0, b == 15)], MTk + [('hbfA', g)], [('ps', 4 + g)])
                    ACTF(P, xdt[0:n, :], xdtE[0:n, :], AF.Identity, ['xdtEA', 'cst'], ['xdtA'], scale=k.cs('blk')[0:n, b:b + 1])
                    sp = k.ps[:, 0:1024].rearrange("p (c n) -> p c n", c=8)
                    for half in range(2):
                        MM(P, [(sp[:, jc, :], xdt[0:n, jc * 128:(jc + 1) * 128], Btok[0:n, (jc // 4) * 128:(jc // 4 + 1) * 128], True, True) for jc in range(4 * half, 4 * half + 4)],
                           ['xdtA', 'BtokA'], [('ps', half)])
                    TT(P, 'gpsimd', hn, h0, bcast(cdT[:, :, b], 2, 128), ALU.mult, [h0k, 'cdTS'], hnk)
                    TT(P, 'vector', hn.rearrange("p c n -> p (c n)"), hn.rearrange("p c n -> p (c n)"), k.ps[:, 0:1024], ALU.add, hnk + [('ps', 0), ('ps', 1)], hnk)
                    P.dma('sync', sso[b], hn, reads=hnk, sem=('hnS', b % 2))

            for _ in run_group(T, 16, 4, extx, extv, False, k.negsb, k.cs('rss'), None, 'S', sample_fn=sample_fn, par=0):
                pass
            cso = io['conv_sample'].rearrange("b r c -> (b r) c")
            rows_out(lambda c: haloS[:, c, :, :], 48, 12, exk, cso, 'cs')
            pso = io['pool_sample'].rearrange("b r c -> (b r) c")
            for half in range(2):
                rows_out(lambda c, half=half: extv[:, c, 8 * half:8 * half + 8, 4:19], 120, 8, evk, pso[half * 120:(half + 1) * 120, :], 'ps%d' % half)
            P.barrier()


N_CORES = 8
_IN_SPECS = [
    ('x_src', [T + NS, D]), ('consts', [128, CST_W]), ('mem_prompt', [256, D]),
    ('state_ssm', [16, 1024, 128]), ('state_ssm_conv', [16, 3, 1536]), ('state_pool', [16, 15, 1024]),
    ('state_ffn_conv', [16, 2, 2 * DFF]), ('cache_mem_k', [16, 256, 4, 256]), ('cache_mem_v', [16, 256, 4, 256]),
    ('norm_mix', [D]), ('w_in', [D, DIN]), ('ssm_conv_w', [4, 1536]), ('ssm_conv_b', [1536]),
    ('ssm_dt_bias', [16]), ('ssm_a_log', [16]), ('ssm_d', [16]), ('ssm_norm', [D]),
    ('w_pool', [4, 256, 256]), ('pool_scale', [D]), ('w_out', [2 * D, D]), ('norm_mem', [D]), ('norm_memkv', [D]),
    ('w_mq', [D, D]), ('w_mk', [D, D]), ('w_mv', [D, D]), ('w_mo', [D, D]), ('norm_ffn', [D]),
    ('w_up', [D, 2 * DFF]), ('ffn_conv_w', [3, 2 * DFF]), ('ffn_conv_b', [2 * DFF]), ('w_down', [DFF, D]), ('final_norm', [D]),
]
_OUT_SPECS = [
    ('y_prompt', [T, D]), ('y_sample', [NS, D]), ('ssm_prompt', [1024, 128]), ('ssm_sample', [16, 1024, 128]),
    ('conv_prompt', [3, 1536]), ('conv_sample', [16, 3, 1536]), ('pool_prompt', [15, 1024]), ('pool_sample', [16, 15, 1024]),
    ('ffn_prompt', [2, 2 * DFF]), ('ffn_sample', [16, 2, 2 * DFF]), ('mem_k_prompt', [256, D]), ('mem_v_prompt', [256, D]),
]


def build_program():
    nc = bass.Bass("TRN2", target_bir_lowering=False)
    io = {}
    for name, shape in _IN_SPECS:
        io[name] = nc.dram_tensor(name, list(shape), F32, kind="ExternalInput").ap()
    for name, shape in _OUT_SPECS:
        io[name] = nc.dram_tensor(name, list(shape), F32, kind="ExternalOutput").ap()
    x1 = nc.dram_tensor("x1_scratch", [T + NS, D], F32, kind="Internal").ap()
    x2 = nc.dram_tensor("x2_scratch", [T + NS, D], F32, kind="Internal").ap()
    with ExitStack() as es:
        P = Prog(nc, es)
        k = K()
        setup_common(P, k, io['consts'])
        with ExitStack() as es2:
            phase_A(P, k, es2, io, io['x_src'], x1)
            P.end_phase()
        with ExitStack() as es2:
            phase_B(P, k, es2, io, x1, x2)
            P.end_phase()
        with ExitStack() as es2:
            phase_C(P, k, es2, io, x2)
            P.end_phase()
        P.emit()
    return nc


_PROG = {}


def kernel(**inputs):
    f = lambda a: np.ascontiguousarray(np.asarray(a, dtype=np.float32))
    if 'nc' not in _PROG:
        _PROG['nc'] = build_program()
    nc = _PROG['nc']
    consts = make_consts()
    xp, xs = f(inputs['x_prompt']), f(inputs['x_sample'])
    shared = {'consts': consts}
    for name in ['norm_mix', 'w_in', 'ssm_conv_w', 'ssm_conv_b', 'ssm_dt_bias', 'ssm_a_log', 'ssm_d', 'ssm_norm', 'w_pool',
                 'pool_scale', 'w_out', 'norm_mem', 'norm_memkv', 'w_mq', 'w_mk', 'w_mv', 'w_mo', 'norm_ffn', 'w_up',
                 'ffn_conv_w', 'ffn_conv_b', 'w_down']:
        shared[name] = f(inputs[name])[0]
    shared['final_norm'] = f(inputs['final_norm'])
    st_ssm, st_conv = f(inputs['state_ssm'])[0], f(inputs['state_ssm_conv'])[0]
    st_pool, st_ffn = f(inputs['state_pool'])[0], f(inputs['state_ffn_conv'])[0]
    ck, cv, mem = f(inputs['cache_mem_k'])[0], f(inputs['cache_mem_v'])[0], f(inputs['mem_prompt'])
    in_maps = []
    for i in range(N_CORES):
        sl = slice(16 * i, 16 * i + 16)
        m = dict(shared)
        m['x_src'] = np.concatenate([xp[i], xs[sl].reshape(NS, D)], axis=0)
        m['mem_prompt'] = mem[i]
        m['state_ssm'] = st_ssm[sl].reshape(16, 1024, 128)
        m['state_ssm_conv'] = st_conv[sl]
        m['state_pool'] = st_pool[sl]
        m['state_ffn_conv'] = st_ffn[sl]
        m['cache_mem_k'] = ck[sl]
        m['cache_mem_v'] = cv[sl]
        in_maps.append(m)
    res = run_bass_kernel_spmd(nc, in_maps, core_ids=list(range(N_CORES)))
    R = res.results
    g = lambda name: np.stack([np.asarray(R[i][name], dtype=np.float32) for i in range(N_CORES)], axis=0)
    y_prompt = g('y_prompt')
    y_sample = g('y_sample').reshape(128, 4, D)
    ssm_p = g('ssm_prompt').reshape(1, 8, 16, 64, 128)
    ssm_s = g('ssm_sample').reshape(1, 128, 16, 64, 128)
    conv_p = g('conv_prompt').reshape(1, 8, 3, 1536)
    conv_s = g('conv_sample').reshape(1, 128, 3, 1536)
    pool_p = g('pool_prompt').reshape(1, 8, 15, 1024)
    pool_s = g('pool_sample').reshape(1, 128, 15, 1024)
    ffn_p = g('ffn_prompt').reshape(1, 8, 2, 2 * DFF)
    ffn_s = g('ffn_sample').reshape(1, 128, 2, 2 * DFF)
    mk_p = g('mem_k_prompt').reshape(1, 8, 256, 4, 256)
    mv_p = g('mem_v_prompt').reshape(1, 8, 256, 4, 256)
    return (y_prompt, y_sample, ssm_p, ssm_s, conv_p, conv_s, pool_p, pool_s, ffn_p, ffn_s, mk_p, mv_p)
```

```python
import numpy as np
import concourse.bass as bass
import concourse.mybir as mybir
from concourse.bass_utils import run_bass_kernel_spmd
from contextlib import ExitStack

F32, BF16 = mybir.dt.float32, mybir.dt.bfloat16
AF = mybir.ActivationFunctionType
ALU = mybir.AluOpType
AX = mybir.AxisListType

SAME_ENGINE_SYNC = True
CHECK_CLOBBER = False
D = 1024
T = 2048
NS = 64
DFF = 2816
EPS = 1e-6


class Prog:
    ENG = ('tensor', 'vector', 'scalar', 'gpsimd', 'sync')

    def __init__(self, nc, es):
        self.nc, self.es = nc, es
        self.ops = {e: [] for e in self.ENG}
        self.sem, self.cnt = {}, {}
        self.phase, self.free, self.retired, self.nsem, self.semcls = 0, {'sw': [], 'hw': []}, set(), 0, {}
        for e in self.ENG:
            self._mksem('E_' + e)
        self.seen = {e: {} for e in self.ENG}
        self.lastw, self.readers = {}, {}
        self.nbuf = 0

    def _mksem(self, name, q=None):
        if name.startswith('D_'):
            name = name + '@%d' % self.phase
        if name not in self.sem:
            cls = 'sw' if q == 'gpsimd' else 'hw'
            if name.startswith('D_'):
                self.semcls[name] = cls
            if name.startswith('D_') and self.free[cls]:
                h, c = self.free[cls].pop()
                self.sem[name] = h
                self.cnt[name] = c
            else:
                self.nsem += 1
                self.sem[name] = self.es.enter_context(self.nc.semaphore('s%d' % self.nsem))
                self.cnt[name] = 0
        return name

    def end_phase(self):
        self.barrier()
        for name in list(self.sem):
            if name.startswith('D_') and name not in self.retired:
                self.retired.add(name)
                self.free[self.semcls[name]].append((self.sem[name], self.cnt[name]))
        self.phase += 1

    def dma_multi(self, q, pairs, keys, sem, **kw):
        s = self._mksem('D_' + str(sem), q)
        for (o, i) in pairs:
            self.cnt[s] += 16
            self.ops[q].append(([], (lambda e, o=o, i=i: e.dma_start(out=o, in_=i, **kw)), (s, 16)))
        tok = (s, self.cnt[s])
        for k in keys:
            self.lastw[k] = tok
            self.readers[k] = []

    def sb(self, shape, dt, name=None, es=None):
        self.nbuf += 1
        return (es or self.es).enter_context(self.nc.sbuf_tensor(name or f"b{self.nbuf}", list(shape), dt))

    def op(self, eng, fn, reads=(), writes=(), dma=None):
        need = {}

        def want(tok, kind):
            if tok is None:
                return
            s, v = tok
            if s == 'E_' + eng:
                if eng == 'tensor' or not SAME_ENGINE_SYNC:
                    return
            if self.seen[eng].get(s, 0) >= v:
                return
            if need.get(s, 0) < v:
                need[s] = v
        for k in reads:
            want(self.lastw.get(k), 'raw')
            if isinstance(k, tuple) and k[0] == 'ps':
                for r in self.readers.get(k, ()):
                    want(r, 'war')
        for k in writes:
            if CHECK_CLOBBER and isinstance(k, tuple) and k[0] == 'ps' and k in self.lastw and not self.readers.get(k):
                import traceback
                print("CLOBBER? unread PSUM", k, [f.lineno for f in traceback.extract_stack()[-6:-1]])
            want(self.lastw.get(k), 'waw')
            for r in self.readers.get(k, ()):
                want(r, 'war')
        for s, v in need.items():
            self.seen[eng][s] = v
        if dma is not None:
            s = self._mksem('D_' + str(dma), eng)
            self.cnt[s] += 16
            tok = (s, self.cnt[s])
            inc = (s, 16)
        else:
            s = 'E_' + eng
            self.cnt[s] += 1
            tok = (s, self.cnt[s])
            inc = (s, 1)
        for k in writes:
            self.lastw[k] = tok
            self.readers[k] = []
        for k in reads:
            self.readers.setdefault(k, []).append(tok)
        self.ops[eng].append((list(need.items()), fn, inc))
        return tok

    def V(self, fn, reads=(), writes=()):
        return self.op('vector', fn, reads, writes)

    def A(self, fn, reads=(), writes=()):
        return self.op('scalar', fn, reads, writes)

    def G(self, fn, reads=(), writes=()):
        return self.op('gpsimd', fn, reads, writes)

    def PE(self, fn, reads=(), writes=()):
        return self.op('tensor', fn, reads, writes)

    def dma(self, q, out, in_, reads=(), writes=(), sem=None, **kw):
        return self.op(q, lambda e: e.dma_start(out=out, in_=in_, **kw), reads, writes, dma=sem)

    def barrier(self):
        allc = [(s_, c_) for s_, c_ in self.cnt.items() if c_ > 0]
        for e in self.ENG:
            w = [(s_, c_) for s_, c_ in allc if self.seen[e].get(s_, 0) < c_]
            for s_, c_ in w:
                self.seen[e][s_] = c_
            self.ops[e].append((w, None, None))

    def emit(self):
        fin = [(s, c) for s, c in self.cnt.items() if c > 0 and s != 'E_sync']
        self.ops['sync'].append((fin, None, None))
        with self.nc.Block() as block:
            for e in self.ENG:
                def body(eng, e=e):
                    for waits, fn, inc in self.ops[e]:
                        for s, v in waits:
                            eng.wait_ge(self.sem[s], v)
                        if fn is None:
                            continue
                        ins = fn(eng)
                        ins.then_inc(self.sem[inc[0]], inc[1])
                getattr(block, e)(body)


def bcast(ap, axis, n):
    u = ap.unsqueeze(axis)
    shp = list(u.shape)
    shp[axis] = n
    return u.broadcast_to(shp)


class K:
    pass


CST_LAYOUT = [('ident', 128), ('mhalf', 1), ('ones', 128), ('rsp', 256), ('rss', 64), ('csc', 64), ('blk', 16),
              ('negc', 128), ('negs', 128), ('e2', 2048), ('expd', 1024)]
CST_SMALL = 128 + 1 + 128 + 256 + 64 + 64 + 16
CST_OFF = {}
_o = 0
for _n, _w in CST_LAYOUT:
    CST_OFF[_n] = (_o, _w)
    _o += _w
CST_W = _o


def make_consts():
    c = np.zeros((128, CST_W), np.float32)

    def put(name, arr):
        o, w = CST_OFF[name]
        c[:arr.shape[0], o:o + arr.shape[1]] = arr
    put('ident', np.eye(128, dtype=np.float32))
    put('mhalf', np.full((128, 1), -0.5, np.float32))
    put('ones', np.ones((128, 128), np.float32))
    s_ = np.arange(128)[:, None]
    l_ = np.arange(128)[None, :]
    put('negc', np.where(l_ >= s_, 0.0, -30000.0).astype(np.float32))
    put('negs', np.where((l_ >= s_) & (l_ // 4 == s_ // 4), 0.0, -30000.0).astype(np.float32))
    e2 = np.zeros((128, 16, 128), np.float32)
    for h in range(16):
        e2[h, h, :] = 1.0
        e2[32 + h, h, :] = 1.0
    put('e2', e2.reshape(128, 2048))
    rsp = np.ones((128, 256), np.float32)
    rsp[:, 0] = 0.0
    rsp[:, 128] = 0.0
    put('rsp', rsp)
    rss = np.ones((128, 64), np.float32)
    rss[:, 0::4] = 0.0
    put('rss', rss)
    csc = np.zeros((128, 4, 16), np.float32)
    for gi, w in enumerate((2, 4, 8, 16)):
        for t in range(16):
            csc[:, gi, t] = 1.0 / min(t + 1, w)
    put('csc', csc.reshape(128, 64))
    blk = np.zeros((128, 16), np.float32)
    for r in range(64):
        blk[r, r // 4] = 1.0
    put('blk', blk)
    expd = np.zeros((128, 8, 128), np.float32)
    for j in range(8):
        for m in range(128):
            expd[2 * j + m // 64, j, m] = 1.0
            expd[32 + 2 * j + m // 64, j, m] = 1.0
    put('expd', expd.reshape(128, 1024))
    return c


def setup_common(P, k, consts):
    nc = P.nc
    k.ps = P.es.enter_context(nc.psum_tensor("ps", [128, 4096], F32))
    k.psb = k.ps[:].bitcast(BF16)
    k.cst = P.sb([128, CST_SMALL], F32, "cst")
    P.dma('sync', k.cst[:], consts[:, 0:CST_SMALL], writes=['cst'], sem='cst')

    def cs(name, rows=128):
        o, w = CST_OFF[name]
        return k.cst[0:rows, o:o + w]
    k.cs = cs
    k.identf = cs('ident')
    k.mhalf = cs('mhalf')
    k.cstb = P.sb([128, 384], BF16, "cstb")
    k.junk = P.sb([128, 1024], BF16, 'junk')
    CP(P, 'vector', k.cstb[:, 0:128], cs('ident'), ['cst'], ['identb'])
    P.dma('gpsimd', k.cstb[:, 128:384], consts[:, CST_SMALL:CST_SMALL + 256], writes=['cstb'], sem='cstbn')
    k.identb = k.cstb[:, 0:128]
    k.negcb = k.cstb[:, 128:256]
    k.negsb = k.cstb[:, 256:384]


def bank(k, b, n=512, bf=False, nb=1):
    if bf:
        return k.psb[:, 1024 * b: 1024 * b + n]
    return k.ps[:, 512 * b: 512 * b + n]


def load_fm(P, k, rows_ap, R, out_ap, key, tag, es=None):
    t = P.sb([128, 128], F32, "lfm_" + tag, es)
    P.dma('sync', t[0:R, :], rows_ap, writes=['lfm_' + tag], sem='lfm_' + tag)
    pb = bank(k, 7)
    P.PE(lambda e: e.transpose(out=pb[:, 0:R], in_=t[0:R, :], identity=k.identf[0:R, 0:R]),
         reads=['lfm_' + tag, 'cst'], writes=[('ps', 7)])
    P.V(lambda e: e.tensor_copy(out=out_ap, in_=pb[:, 0:R]), reads=[('ps', 7)], writes=[key])


def TT(P, eng, out, in0, in1, op, reads, writes):
    return P.op(eng, lambda e: e.tensor_tensor(out=out, in0=in0, in1=in1, op=op), reads, writes)


def TS(P, eng, out, in0, s1, s2, op0, op1, reads, writes):
    if op1 is None:
        return P.op(eng, lambda e: e.tensor_scalar(out=out, in0=in0, scalar1=s1, scalar2=None, op0=op0), reads, writes)
    return P.op(eng, lambda e: e.tensor_scalar(out=out, in0=in0, scalar1=s1, scalar2=s2, op0=op0, op1=op1), reads, writes)


def STT(P, out, in0, scalar, in1, op0, op1, reads, writes):
    return P.op('vector', lambda e: e.scalar_tensor_tensor(out=out, in0=in0, scalar=scalar, in1=in1, op0=op0, op1=op1), reads, writes)


def ACTF(P, out, in_, func, reads, writes, bias=None, scale=None, accum=None):
    kw = {}
    if bias is not None:
        kw['bias'] = bias
    if scale is not None:
        kw['scale'] = scale
    if accum is not None:
        kw['accum_out'] = accum
    return P.op('scalar', lambda e: e.activation(out=out, in_=in_, func=func, **kw), reads, writes)


def CP(P, eng, out, in_, reads, writes):
    if eng == 'scalar':
        return P.op(eng, lambda e: e.activation(out=out, in_=in_, func=AF.Identity), reads, writes)
    return P.op(eng, lambda e: e.tensor_copy(out=out, in_=in_), reads, writes)


def MM(P, items, reads, writes):
    items = list(items)

    def f(e):
        for it in items:
            (o, l, r, st, sp) = it[:5]
            if len(it) > 5:
                ins = e.matmul(o, lhsT=l, rhs=r, start=st, stop=sp, tile_position=it[5])
            else:
                ins = e.matmul(o, lhsT=l, rhs=r, start=st, stop=sp)
        return ins
    return P.op('tensor', f, reads, writes)


def TR(P, items, reads, writes):
    items = list(items)

    def f(e):
        for (o, i, idn) in items:
            ins = e.transpose(out=o, in_=i, identity=idn)
        return ins
    return P.op('tensor', f, reads, writes)


def MEMSET(P, eng, ap, val, writes):
    return P.op(eng, lambda e: e.memset(ap, val), (), writes)


def rstd_op(P, k, x_ap, n, xkeys, st, col, scale=1.0 / D, extra=None):
    junk = k.junk
    ks = ('st', id(st), col)
    P.A(lambda e: e.activation(out=junk[0:n, 0:x_ap.shape[1]], in_=x_ap, func=AF.Square, accum_out=st[0:n, col:col + 1]),
        reads=xkeys, writes=['junk', ks])
    P.V(lambda e: e.tensor_scalar(out=st[0:n, col + 1:col + 2], in0=st[0:n, col:col + 1], scalar1=scale, scalar2=EPS,
                                  op0=ALU.mult, op1=ALU.add), reads=[ks], writes=[ks])
    P.G(lambda e: e.tensor_tensor(out=st[0:n, col + 2:col + 3], in0=st[0:n, col + 1:col + 2], in1=k.mhalf[0:n, :], op=ALU.pow),
        reads=[ks, 'cst'], writes=[ks])
    if extra is not None:
        P.V(lambda e: e.tensor_scalar(out=st[0:n, col + 2:col + 3], in0=st[0:n, col + 2:col + 3], scalar1=extra, scalar2=None,
                                      op0=ALU.mult), reads=[ks], writes=[ks])
    return st[0:n, col + 2:col + 3], ks


def norm_T(P, k, x_ap, n, xkeys, gain_fm, gkey, hT_ap, hTkey, st, col, hb, hbkey, tb):
    r, ks = rstd_op(P, k, x_ap, n, xkeys, st, col)
    P.V(lambda e: e.tensor_scalar(out=hb[0:n, :], in0=x_ap, scalar1=r, scalar2=None, op0=ALU.mult),
        reads=list(xkeys) + [ks], writes=[hbkey])
    pb = k.psb[:, 1024 * tb: 1024 * tb + 1024].rearrange("p (c t) -> p c t", c=8)

    def tr(e):
        for c in range(8):
            ins = e.transpose(out=pb[:, c, 0:n], in_=hb[0:n, c * 128:(c + 1) * 128], identity=k.identb[0:n, 0:n])
        return ins
    P.PE(tr, reads=[hbkey, 'identb'], writes=[('ps', tb)])
    P.V(lambda e: e.tensor_tensor(out=hT_ap, in0=pb[:, :, 0:n], in1=bcast(gain_fm, 2, n), op=ALU.mult),
        reads=[('ps', tb), gkey], writes=[hTkey])


def phase_C(P, k, es, io, x_src, do_prompt=True, do_sample=True):
    wup = P.sb([128, 8, 2 * DFF], BF16, "wup", es)
    wdn = P.sb([128, 22, D], BF16, "wdn", es)
    wup_d = io['w_up'].rearrange("(c p) n -> p c n", p=128)
    wdn_d = io['w_down'].rearrange("(c p) n -> p c n", p=128)
    NCB = 4
    cbw = DFF // NCB
    def load_wup_block(q):
        prs = []
        for br in range(2):
            c0 = br * DFF + q * cbw
            prs.append((wup[:, :, c0:c0 + cbw], wup_d[:, :, c0:c0 + cbw]))
        P.dma_multi('gpsimd', prs, [('wup', q)], ('wup', q), max_dma_last_dim=4096)

    def load_rest_weights():
        for q in range(1, NCB):
            load_wup_block(q)
        for gq in range(2):
            P.dma_multi('gpsimd', [(wdn[:, 11 * gq:11 * gq + 11, :], wdn_d[:, 11 * gq:11 * gq + 11, :])], [('wdn', c) for c in range(11 * gq, 11 * gq + 11)], ('wdn', gq), max_dma_last_dim=4096)
    load_wup_block(0)
    wupk = None
    gC = P.sb([128, 8], F32, "gC", es)
    load_fm(P, k, io['norm_ffn'].rearrange("(c p) -> c p", p=128), 8, gC[:, :], 'gC', 'gC', es)
    cw = P.sb([128, 3, 44], F32, "cwC", es)
    cb = P.sb([128, 44], F32, "cbC", es)
    cwd = io['ffn_conv_w'].rearrange("k (c p) -> k c p", p=128)
    for kk in range(3):
        load_fm(P, k, cwd[kk], 44, cw[:, kk, :], ('cwC', kk), 'cwC%d' % kk, es)
    load_fm(P, k, io['ffn_conv_b'].rearrange("(c p) -> c p", p=128), 44, cb[:, :], 'cbC', 'cbC', es)
    cwk = [('cwC', kk) for kk in range(3)] + ['cbC']
    fgb = P.sb([128, D], F32, "fgb", es)
    P.dma('sync', fgb[:], io['final_norm'].partition_broadcast(128), writes=['fgb'], sem='fgb')

    xt = [P.sb([128, D], F32, "xtC%d" % i, es) for i in range(2)]
    hb = [k.junk] * 2
    st = P.sb([128, 64], F32, "stC", es)
    aT = P.sb([128, 22, 512], BF16, "aTC", es)
    NR = 3
    B_ = {}

    def alloc_group(stack, ntok, nseq, tag):
        B_['hT'] = P.sb([128, 8, ntok], BF16, "hTC" + tag, stack)
        B_['accg'] = [P.sb([128, ntok], F32, "accg%d%s" % (i, tag), stack) for i in range(NR)]
        B_['accv'] = [P.sb([128, ntok], F32, "accv%d%s" % (i, tag), stack) for i in range(NR)]
        B_['th'] = [P.sb([128, ntok], F32, "th%d%s" % (i, tag), stack) for i in range(2)]
        halo = P.sb([128, 44, nseq, 2], F32, "halo" + tag, stack)
        fix = P.sb([128, 44, nseq, 2], F32, "fix" + tag, stack)
        return halo, fix
    x3 = [P.sb([128, D], F32, "x3C%d" % i, es) for i in range(1)] * 2
    stg = [P.sb([32, 512], F32, "stgC%d" % i, es) for i in range(2)]
    cnt = {'x': 0, 'st': 0, 'p': 0, 'o': 0}

    def run_group(tok0, nseq, L, halo, fix, y_dst, gname, do=('norm', 'loop', 'down'), js=None):
        ntok = nseq * L
        nsub = (ntok + 127) // 128
        hT, accg, accv, th = B_['hT'], B_['accg'], B_['accv'], B_['th']
        hk = [('halo' + gname, c) for c in range(44)]
        fk = 'fix' + gname
        if 'norm' in do:
            for j in (range(nsub) if js is None else js):
                n = min(128, ntok - j * 128)
                i = cnt['x'] % 2
                cnt['x'] += 1
                P.dma('sync', xt[i][0:n, :], x_src[tok0 + j * 128: tok0 + j * 128 + n, :], writes=[('xtC', i)], sem=('xtC', i))
                col = (cnt['st'] % 8) * 4
                cnt['st'] += 1
                norm_T(P, k, xt[i][0:n, :], n, [('xtC', i)], gC[:, :], 'gC', hT[:, :, j * 128: j * 128 + n], ('hTC', j),
                       st, col, hb[i], 'junk', 4)
        hTk = [('hTC', j) for j in range(nsub)]
        if 'loop' in do:
            w0 = bcast(cw[:, 0, :], 2, nseq)
            w1 = bcast(cw[:, 1, :], 2, nseq)
            P.G(lambda e: e.tensor_tensor(out=fix[:, :, :, 0], in0=halo[:, :, :, 0], in1=w0, op=ALU.mult), reads=hk + cwk, writes=[fk])
            P.G(lambda e: e.tensor_tensor(out=fix[:, :, :, 1], in0=halo[:, :, :, 1], in1=w1, op=ALU.mult), reads=hk + cwk, writes=[fk + 'b'])
            P.G(lambda e: e.tensor_tensor(out=fix[:, :, :, 0], in0=fix[:, :, :, 0], in1=fix[:, :, :, 1], op=ALU.add), reads=[fk, fk + 'b'], writes=[fk])
            P.G(lambda e: e.tensor_tensor(out=fix[:, :, :, 1], in0=halo[:, :, :, 1], in1=w0, op=ALU.mult), reads=hk + cwk + [fk], writes=[fk + 'b'])
            fks = [fk, fk + 'b']

            def v3(ap, a, b):
                return ap.rearrange("p (s l) -> p s l", s=nseq)[:, :, a:b]

            def stage1(jj):
                for br, acc, nm in ((0, accg, 'accg'), (1, accv, 'accv')):
                    c = br * 22 + jj
                    b = (cnt['p'] % 4)
                    cnt['p'] += 1
                    pb = bank(k, b, ntok)

                    def mm(e, c=c, pb=pb):
                        for kc in range(8):
                            ins = e.matmul(pb, lhsT=wup[:, kc, c * 128:(c + 1) * 128], rhs=hT[:, kc, 0:ntok], start=(kc == 0), stop=(kc == 7))
                        return ins
                    P.PE(mm, reads=[('wup', min(NCB - 1, (jj * 128) // cbw)), ('wup', min(NCB - 1, (jj * 128 + 127) // cbw))] + hTk, writes=[('ps', b)])
                    r = jj % NR
                    a_ap = acc[r][:, 0:ntok]
                    P.A(lambda e, c=c, pb=pb, a_ap=a_ap: e.activation(out=a_ap, in_=pb, func=AF.Identity, bias=cb[:, c:c + 1], scale=cw[:, 2, c:c + 1]),
                        reads=[('ps', b)] + cwk, writes=[(nm, r)])
                    P.V(lambda e, c=c, pb=pb, a_ap=a_ap: e.scalar_tensor_tensor(out=v3(a_ap, 1, L), in0=v3(pb, 0, L - 1), scalar=cw[:, 1, c:c + 1], in1=v3(a_ap, 1, L), op0=ALU.mult, op1=ALU.add),
                        reads=[('ps', b), (nm, r)] + cwk, writes=[(nm, r)])
                    P.V(lambda e, c=c, pb=pb, a_ap=a_ap: e.scalar_tensor_tensor(out=v3(a_ap, 2, L), in0=v3(pb, 0, L - 2), scalar=cw[:, 0, c:c + 1], in1=v3(a_ap, 2, L), op0=ALU.mult, op1=ALU.add),
                        reads=[('ps', b), (nm, r)] + cwk, writes=[(nm, r)])
                    P.V(lambda e, c=c, a_ap=a_ap: e.tensor_tensor(out=v3(a_ap, 0, 2), in0=v3(a_ap, 0, 2), in1=fix[:, c, :, :], op=ALU.add),
                        reads=[(nm, r)] + fks, writes=[(nm, r)])
                    P.V(lambda e, c=c, pb=pb: e.tensor_copy(out=halo[:, c, :, :], in_=v3(pb, L - 2, L)),
                        reads=[('ps', b)] + fks, writes=[hk[c]])

            def stage2(jj):
                r = jj % NR
                P.A(lambda e: e.activation(out=th[jj % 2][:, 0:ntok], in_=accg[r][:, 0:ntok], func=AF.Silu), reads=[('accg', r)], writes=[('th', jj % 2)])
                P.G(lambda e: e.tensor_tensor(out=aT[:, jj, 0:ntok], in0=th[jj % 2][:, 0:ntok], in1=accv[r][:, 0:ntok], op=ALU.mult),
                    reads=[('th', jj % 2), ('accv', r)], writes=[('aT', jj)])
            SK = 1
            for step in range(22 + SK):
                if step < 22:
                    stage1(step)
                if step >= SK:
                    stage2(step - SK)
        aTk = [('aT', jj) for jj in range(22)]
        wdk = [('wdn', c) for c in range(22)]
        if 'down' in do:
            for j in (range(nsub) if js is None else js):
                n = min(128, ntok - j * 128)
                i = cnt['x'] % 2
                cnt['x'] += 1
                P.dma('sync', xt[i][0:n, :], x_src[tok0 + j * 128: tok0 + j * 128 + n, :], writes=[('xtC', i)], sem=('xtC', i))
                o = cnt['o'] % 2
                cnt['o'] += 1
                for hf in range(2):
                    b = 5 + hf
                    pb = bank(k, b)[0:n, :]

                    def mm(e, pb=pb, hf=hf, n=n, j=j):
                        for jj in range(22):
                            ins = e.matmul(pb, lhsT=aT[:, jj, j * 128: j * 128 + n], rhs=wdn[:, jj, hf * 512:(hf + 1) * 512], start=(jj == 0), stop=(jj == 21))
                        return ins
                    P.PE(mm, reads=aTk + wdk, writes=[('ps', b)])
                    P.V(lambda e, pb=pb, hf=hf, n=n, i=i, o=o: e.tensor_tensor(out=x3[o][0:n, hf * 512:(hf + 1) * 512], in0=pb, in1=xt[i][0:n, hf * 512:(hf + 1) * 512], op=ALU.add),
                        reads=[('ps', b), ('xtC', i)], writes=[('x3', 0, hf)])
                col = (cnt['st'] % 8) * 4
                cnt['st'] += 1
                r, ks = rstd_op(P, k, x3[o][0:n, :], n, [('x3', 0, 0), ('x3', 0, 1)], st, col)
                P.V(lambda e, n=n, o=o, r=r: e.scalar_tensor_tensor(out=x3[o][0:n, :], in0=x3[o][0:n, :], scalar=r, in1=fgb[0:n, :], op0=ALU.mult, op1=ALU.mult),
                    reads=[('x3', 0, 0), ('x3', 0, 1), ks, 'fgb'], writes=[('x3', 0, 0), ('x3', 0, 1)])
                P.dma('scalar', y_dst[j * 128: j * 128 + n, :], x3[o][0:n, :], reads=[('x3', 0, 0), ('x3', 0, 1)], sem=('yout', o))
        return hk

    def state_out(halo, hk, nseq, dst):
        R = nseq * 2
        for q in range(11):
            pb = bank(k, 7)

            def tr(e, q=q, pb=pb):
                for u in range(4):
                    c = q * 4 + u
                    ins = e.transpose(out=pb[0:R, u * 128:(u + 1) * 128], in_=halo[:, c, :, :].rearrange("p s r -> p (s r)"), identity=k.identf)
                return ins
            P.PE(tr, reads=hk[q * 4:q * 4 + 4] + ['cst'], writes=[('ps', 7)])
            P.V(lambda e, q=q, pb=pb: e.tensor_copy(out=stg[q % 2][0:R, :], in_=pb[0:R, :]), reads=[('ps', 7)], writes=[('stgC', q % 2)])
            P.dma('sync', dst[:, q * 512:(q + 1) * 512], stg[q % 2][0:R, :], reads=[('stgC', q % 2)], sem=('stgC', q % 2))

    if do_prompt:
        with ExitStack() as esp:
            haloP, fixP = alloc_group(esp, 512, 1, 'P')
            MEMSET(P, 'vector', haloP[:], 0.0, [('haloP', c) for c in range(44)])
            NSUP = T // 512
            ga = lambda S: (S * 512, 1, 512, haloP, fixP, io['y_prompt'][S * 512:(S + 1) * 512, :], 'P')
            run_group(*ga(0), do=('norm',))
            load_rest_weights()
            for S in range(NSUP):
                hk = run_group(*ga(S), do=('loop',))
                if S + 1 < NSUP:
                    run_group(*ga(S + 1), do=('norm',))
                run_group(*ga(S), do=('down',))
            state_out(haloP, hk, 1, io['ffn_prompt'])
            P.barrier()
    if not do_prompt:
        load_rest_weights()
    if do_sample:
        with ExitStack() as ess:
            haloS, fixS = alloc_group(ess, NS, 16, 'S')
            hkS = [('haloS', c) for c in range(44)]
            sfv = io['state_ffn_conv'].rearrange("b r c -> (b r) c")
            for q in range(11):
                P.dma('sync', stg[q % 2][0:32, :], sfv[:, q * 512:(q + 1) * 512], writes=[('stgC', q % 2)], sem=('stgC', q % 2))
                pb = bank(k, 7)[:, 0:128].rearrange("p (u r) -> p u r", u=4)
                TR(P, [(pb[:, u, :], stg[q % 2][0:32, u * 128:(u + 1) * 128], k.identf[0:32, 0:32]) for u in range(4)], [('stgC', q % 2), 'cst'], [('ps', 7)])
                CP(P, 'vector', haloS[:, 4 * q:4 * q + 4, :, :].rearrange("p u s r -> p u (s r)"), pb, [('ps', 7)], hkS[4 * q:4 * q + 4])
            hk = run_group(T, 16, 4, haloS, fixS, io['y_sample'], 'S')
            state_out(haloS, hk, 16, io['ffn_sample'].rearrange("b r c -> (b r) c"))
            P.barrier()


def phase_B(P, k, es, io, x_src, x_dst, do_prompt=True, do_sample=True):
    wq = P.sb([128, 8, D], BF16, "wmq", es)
    wo = P.sb([128, 8, D], BF16, "wmo", es)

    def load_w(nm, w):
        wd = io[nm].rearrange("(c p) n -> p c n", p=128)
        for gq in range(2):
            P.dma_multi('gpsimd', [(w[:, 4 * gq:4 * gq + 4, :], wd[:, 4 * gq:4 * gq + 4, :])], [(nm, c) for c in range(4 * gq, 4 * gq + 4)], (nm, gq), max_dma_last_dim=4096)
    kk_ = lambda nm: [(nm, c) for c in range(8)]
    gM = P.sb([128, 8], F32, "gM", es)
    load_fm(P, k, io['norm_mem'].rearrange("(c p) -> c p", p=128), 8, gM[:, :], 'gM', 'gM', es)
    xt = [P.sb([128, D], F32, "xtB%d" % i, es) for i in range(2)]
    hb = k.junk
    st = P.sb([128, 64], F32, "stB", es)
    hT = P.sb([128, 8, 512], BF16, "hTB", es)
    qT = P.sb([128, 8, 512], BF16, "qTB", es)
    PT = P.sb([128, 8, 512], BF16, "PTB", es)
    oT = P.sb([128, 8, 512], BF16, "oTB", es)
    pe = P.sb([128, 4, 256], F32, "peB", es)
    pn = P.sb([128, 4, 256], BF16, "pnB", es)
    KT, Vb = [None], [None]
    sm = P.sb([128, 32], F32, "smB", es)
    ob = P.sb([128, D], BF16, "obB", es)
    x2 = P.sb([128, D], F32, "x2B", es)
    cnt = {'x': 0, 'st': 0}

    def load_norm(tok0, ntok, gain, gkey, src, tb=4):
        nsub = (ntok + 127) // 128
        for j in range(nsub):
            n = min(128, ntok - j * 128)
            i = cnt['x'] % 2
            cnt['x'] += 1
            P.dma('sync', xt[i][0:n, :], src[tok0 + j * 128: tok0 + j * 128 + n, :], writes=[('xtB', i)], sem=('xtB', i))
            col = (cnt['st'] % 8) * 4
            cnt['st'] += 1
            norm_T(P, k, xt[i][0:n, :], n, [('xtB', i)], gain, gkey, hT[:, :, j * 128: j * 128 + n], ('hTB', j), st, col, hb, 'junk', tb)
        return [('hTB', j) for j in range(nsub)]

    def softmax_rows(S_ap, n, skeys, pn_out=None, pn_key='pnB'):
        pn_ = pn if pn_out is None else pn_out
        P.V(lambda e: e.tensor_reduce(out=sm[0:n, 0:4], in_=S_ap, axis=AX.X, op=ALU.max), reads=skeys, writes=['smB'])
        P.V(lambda e: e.tensor_scalar(out=sm[0:n, 4:8], in0=sm[0:n, 0:4], scalar1=-1.0, scalar2=None, op0=ALU.mult), reads=['smB'], writes=['smB'])
        for h in range(4):
            P.A(lambda e, h=h: e.activation(out=pe[0:n, h, :], in_=S_ap[:, h, :], func=AF.Exp, bias=sm[0:n, 4 + h:5 + h], accum_out=sm[0:n, 8 + h:9 + h]),
                reads=skeys + ['smB'], writes=[('peB', h), ('smB', h)])
        smk = [('smB', h) for h in range(4)]
        P.V(lambda e: e.reciprocal(out=sm[0:n, 12:16], in_=sm[0:n, 8:12]), reads=smk, writes=['smB2'])
        P.V(lambda e: e.tensor_tensor(out=pn_[0:n, :, :], in0=pe[0:n, :, :], in1=bcast(sm[0:n, 12:16], 2, 256), op=ALU.mult),
            reads=[('peB', h) for h in range(4)] + ['smB2'], writes=[pn_key])

    def qproj(ntok, hTk):
        for c in range(8):
            b = c % 2
            pb = bank(k, b, ntok)

            def mm(e, c=c, pb=pb):
                for kc in range(8):
                    ins = e.matmul(pb, lhsT=wq[:, kc, c * 128:(c + 1) * 128], rhs=hT[:, kc, 0:ntok], start=(kc == 0), stop=(kc == 7))
                return ins
            P.PE(mm, reads=kk_('w_mq') + hTk, writes=[('ps', b)])
            P.A(lambda e, c=c, pb=pb: e.activation(out=qT[:, c, 0:ntok], in_=pb, func=AF.Identity, scale=1.0 / 16.0), reads=[('ps', b)], writes=[('qTB', c)])
        return [('qTB', c) for c in range(8)]

    def oproj(tok0, ntok, src, dst, oTk):
        nsub = (ntok + 127) // 128
        for j in range(nsub):
            n = min(128, ntok - j * 128)
            i = cnt['x'] % 2
            cnt['x'] += 1
            P.dma('sync', xt[i][0:n, :], src[tok0 + j * 128: tok0 + j * 128 + n, :], writes=[('xtB', i)], sem=('xtB', i))
            for hf in range(2):
                b = 2 + hf
                pb = bank(k, b)[0:n, :]

                def mm(e, pb=pb, hf=hf, n=n, j=j):
                    for c in range(8):
                        ins = e.matmul(pb, lhsT=oT[:, c, j * 128: j * 128 + n], rhs=wo[:, c, hf * 512:(hf + 1) * 512], start=(c == 0), stop=(c == 7))
                    return ins
                P.PE(mm, reads=oTk + kk_('w_mo'), writes=[('ps', b)])
                P.V(lambda e, pb=pb, hf=hf, n=n, i=i: e.tensor_tensor(out=x2[0:n, hf * 512:(hf + 1) * 512], in0=pb, in1=xt[i][0:n, hf * 512:(hf + 1) * 512], op=ALU.add),
                    reads=[('ps', b), ('xtB', i)], writes=[('x2B', hf)])
            P.dma('sync', dst[tok0 + j * 128: tok0 + j * 128 + n, :], x2[0:n, :], reads=[('x2B', 0), ('x2B', 1)], sem='x2B')

    BSTEP = 99
    if do_prompt and BSTEP >= 1:
        esp = ExitStack()
        esp.__enter__()
        wk = P.sb([128, 8, D], BF16, "wmk", esp)
        wv = P.sb([128, 8, D], BF16, "wmv", esp)
        gKV = P.sb([128, 8], F32, "gKV", esp)
        KT[0] = P.sb([128, 8, 256], BF16, "KTB0", esp)
        Vb[0] = P.sb([128, 2, D], BF16, "VbB0", esp)
        kvf = P.sb([128, D], F32, "kvf", esp)
        load_w('w_mk', wk)
        load_w('w_mv', wv)
        load_w('w_mq', wq)
        load_w('w_mo', wo)
        load_fm(P, k, io['norm_memkv'].rearrange("(c p) -> c p", p=128), 8, gKV[:, :], 'gKV', 'gKV', esp)
        if True:
            hmk = load_norm(0, 256, gKV[:, :], 'gKV', io['mem_prompt'])
            for nm, w, dst in (('w_mk', wk, io['mem_k_prompt']), ('w_mv', wv, io['mem_v_prompt'])):
                if BSTEP < 2:
                    break
                for mt in range(2):
                    for hf in range(2):
                        b = hf
                        pb = bank(k, b)

                        def mm(e, pb=pb, hf=hf, mt=mt, w=w):
                            for kc in range(8):
                                ins = e.matmul(pb, lhsT=hT[:, kc, mt * 128:(mt + 1) * 128], rhs=w[:, kc, hf * 512:(hf + 1) * 512], start=(kc == 0), stop=(kc == 7))
                            return ins
                        P.PE(mm, reads=kk_(nm) + hmk, writes=[('ps', b)])
                        P.A(lambda e, pb=pb, hf=hf: e.activation(out=kvf[:, hf * 512:(hf + 1) * 512], in_=pb, func=AF.Identity), reads=[('ps', b)], writes=[('kvf', hf)])
                        if nm == 'w_mv':
                            P.V(lambda e, pb=pb, hf=hf, mt=mt: e.tensor_copy(out=Vb[0][:, mt, hf * 512:(hf + 1) * 512], in_=pb), reads=[('ps', b)], writes=[('VbB', 0, mt, hf)])
                    P.dma('sync', dst[mt * 128:(mt + 1) * 128, :], kvf[:, :], reads=[('kvf', 0), ('kvf', 1)], sem='kvf')
            for c in range(8 if BSTEP >= 3 else 0):
                b = c % 2
                pb = bank(k, b, 256)

                def mm(e, c=c, pb=pb):
                    for kc in range(8):
                        ins = e.matmul(pb, lhsT=wk[:, kc, c * 128:(c + 1) * 128], rhs=hT[:, kc, 0:256], start=(kc == 0), stop=(kc == 7))
                    return ins
                P.PE(mm, reads=kk_('w_mk') + hmk, writes=[('ps', b)])
                P.V(lambda e, c=c, pb=pb: e.tensor_copy(out=KT[0][:, c, :], in_=pb), reads=[('ps', b)], writes=[('KTB', 0, c)])
            KTk = [('KTB', 0, c) for c in range(8)]
            Vk = [('VbB', 0, mt, hf) for mt in range(2) for hf in range(2)]
            pn2 = P.sb([128, 4, 256], BF16, "pnB2", esp)
            qT1 = P.sb([128, 8, 512], BF16, "qTB1", esp)
            PT1 = P.sb([128, 8, 512], BF16, "PTB1", esp)
            qTs, PTs = [qT, qT1], [PT, PT1]
            NSUP = T // 512

            def g_qproj(S):
                hTk = [('hTB', j) for j in range(4)]
                q_ = qTs[S % 2]
                for c in range(8):
                    b_ = c % 2
                    pb = bank(k, b_, 512)
                    MM(P, [(pb, wq[:, kc, c * 128:(c + 1) * 128], hT[:, kc, 0:512], kc == 0, kc == 7) for kc in range(8)], kk_('w_mq') + hTk, [('ps', b_)])
                    ACTF(P, q_[:, c, :], pb, AF.Identity, [('ps', b_)], [('qTB', S % 2, c)], scale=1.0 / 16.0)
                    yield

            def g_soft(S):
                q_, pt_ = qTs[S % 2], PTs[S % 2]
                qk = [('qTB', S % 2, c) for c in range(8)]
                pend = None

                def ptrans(j, pnj, pk):
                    tb = k.psb[:, 1024 * 2: 1024 * 3].rearrange("p (c t) -> p c t", c=8)
                    TR(P, [(tb[:, 2 * h + mc, :], pnj[:, h, mc * 128:(mc + 1) * 128], k.identb) for h in range(4) for mc in range(2)], [pk, 'identb'], [('ps', 2)])
                    CP(P, 'vector', pt_[:, :, j * 128:(j + 1) * 128], tb, [('ps', 2)], [('PTB', S % 2, j)])
                for j in range(4):
                    Sps = k.ps[:, 512 * 4: 512 * 6].rearrange("p (h m) -> p h m", h=4)
                    for h in range(4):
                        MM(P, [(Sps[:, h, :], q_[:, 2 * h + dc, j * 128:(j + 1) * 128], KT[0][:, 2 * h + dc, :], dc == 0, dc == 1) for dc in range(2)],
                           qk + KTk, [('ps', 4 + h // 2)])
                    yield
                    pnj = pn if j % 2 == 0 else pn2
                    pk = 'pnB' if j % 2 == 0 else 'pnB2'
                    softmax_rows(Sps, 128, [('ps', 4), ('ps', 5)], pn_out=pnj, pn_key=pk)
                    yield
                    if pend is not None:
                        ptrans(*pend)
                        yield
                    pend = (j, pnj, pk)
                ptrans(*pend)
                yield

            def g_out(S):
                pt_ = PTs[S % 2]
                PTk = [('PTB', S % 2, j) for j in range(4)]
                for c in range(8):
                    h, dc = c // 2, c % 2
                    pb = bank(k, 3)
                    MM(P, [(pb, Vb[0][:, mc, h * 256 + dc * 128: h * 256 + dc * 128 + 128], pt_[:, 2 * h + mc, :], mc == 0, mc == 1) for mc in range(2)], PTk + Vk, [('ps', 3)])
                    CP(P, 'scalar', oT[:, c, :], pb, [('ps', 3)], [('oTB', c)])
                    if c % 2 == 1:
                        yield
                oTk = [('oTB', c) for c in range(8)]
                for j in range(4):
                    i = cnt['x'] % 2
                    cnt['x'] += 1
                    P.dma('sync', xt[i][:, :], x_src[S * 512 + j * 128: S * 512 + (j + 1) * 128, :], writes=[('xtB', i)], sem=('xtB', i))
                    for hf in range(2):
                        pb = bank(k, 6 + hf)
                        MM(P, [(pb, oT[:, c, j * 128:(j + 1) * 128], wo[:, c, hf * 512:(hf + 1) * 512], c == 0, c == 7) for c in range(8)], oTk + kk_('w_mo'), [('ps', 6 + hf)])
                        TT(P, 'vector', x2[:, hf * 512:(hf + 1) * 512], pb, xt[i][:, hf * 512:(hf + 1) * 512], ALU.add, [('ps', 6 + hf), ('xtB', i)], [('x2B', hf)])
                    P.dma('scalar', x_dst[S * 512 + j * 128: S * 512 + (j + 1) * 128, :], x2[:, :], reads=[('x2B', 0), ('x2B', 1)], sem='x2B')
                    yield

            def run_all(gs):
                gs = [g for g in gs if g is not None]
                while gs:
                    for g in list(gs):
                        try:
                            next(g)
                        except StopIteration:
                            gs.remove(g)
            load_norm(0, 512, gM[:, :], 'gM', x_src, tb=2)
            run_all([g_qproj(0)])
            for S in range(NSUP):
                nxt = None
                if S + 1 < NSUP:
                    load_norm((S + 1) * 512, 512, gM[:, :], 'gM', x_src, tb=2)
                    nxt = g_qproj(S + 1)
                run_all([g_soft(S), g_out(S - 1) if S > 0 else None, nxt])
            run_all([g_out(NSUP - 1)])
        P.barrier()
        esp.__exit__(None, None, None)
    else:
        load_w('w_mq', wq)
        load_w('w_mo', wo)
    if do_sample:
        with ExitStack() as ess:
            NG = 4
            Kb4 = [P.sb([128, NG, 2, D], BF16, "Kb4_%d" % i, ess) for i in range(2)]
            Vb4 = [P.sb([128, NG, 2, D], BF16, "Vb4_%d" % i, ess) for i in range(2)]
            KT4 = P.sb([128, NG, 8, 256], BF16, "KT4", ess)
            hTk = load_norm(T, NS, gM[:, :], 'gM', x_src)
            qTk = qproj(NS, hTk)
            ck = io['cache_mem_k'].rearrange("b (mt p) h d -> b p mt (h d)", p=128)
            cv = io['cache_mem_v'].rearrange("b (mt p) h d -> b p mt (h d)", p=128)
            MEMSET(P, 'vector', k.ps[:, 512 * 2: 512 * 6], 0.0, [('ps', 2), ('ps', 3), ('ps', 4), ('ps', 5)])
            CP(P, 'vector', pn[:, :, :], k.ps[:, 512 * 4: 512 * 6].rearrange("p (h m) -> p h m", h=4), [('ps', 4), ('ps', 5)], ['pnB'])
            CP(P, 'vector', ob[:, :], k.ps[:, 512 * 2: 512 * 4], [('ps', 2), ('ps', 3)], ['obB'])

            def loads(g):
                r = g % 2
                for q in range(NG):
                    bq = NG * g + q
                    for mt in range(2):
                        P.dma('gpsimd', Kb4[r][:, q, mt, :], ck[bq, :, mt, :], writes=[('Kb4', r, q, mt)], sem=('Kb4', r, q, mt), max_dma_last_dim=4096)
                        P.dma('gpsimd', Vb4[r][:, q, mt, :], cv[bq, :, mt, :], writes=[('Vb4', r, q, mt)], sem=('Vb4', r, q, mt), max_dma_last_dim=4096)
            loads(0)
            NR_ = 32 * (NG - 1) + 4
            for g in range(16 // NG):
                r = g % 2
                if g + 1 < 16 // NG:
                    loads(g + 1)
                for q in range(NG):
                    for half in range(2):
                        tb = k.psb[:, 1024 * (6 + half): 1024 * (7 + half)].rearrange("p (c m) -> p c m", c=4)
                        TR(P, [(tb[:, cc, mt * 128:(mt + 1) * 128], Kb4[r][:, q, mt, (half * 4 + cc) * 128:(half * 4 + cc + 1) * 128], k.identb) for cc in range(4) for mt in range(2)],
                           [('Kb4', r, q, 0), ('Kb4', r, q, 1), 'identb'], [('ps', 6 + half)])
                        CP(P, 'vector' if half == 0 else 'scalar', KT4[:, q, half * 4:(half + 1) * 4, :], tb, [('ps', 6 + half)], [('KT4', q, half)])
                for q in range(NG):
                    bq = NG * g + q
                    Sq = k.ps[32 * q:32 * q + 4, 512 * 4: 512 * 6].rearrange("p (h m) -> p h m", h=4)
                    for h in range(4):
                        MM(P, [(Sq[:, h, :], qT[:, 2 * h + dc, bq * 4:(bq + 1) * 4], KT4[:, q, 2 * h + dc, :], dc == 0, dc == 1, (0, 32 * q)) for dc in range(2)],
                           qTk + [('KT4', q, 0), ('KT4', q, 1)], [('ps', 4 + h // 2)])
                Sps = k.ps[0:NR_, 512 * 4: 512 * 6].rearrange("p (h m) -> p h m", h=4)
                softmax_rows(Sps, NR_, [('ps', 4), ('ps', 5)])
                tb = k.psb[:, 1024 * 0: 1024 * 1].rearrange("p (c t) -> p c t", c=8)
                TR(P, [(tb[:, 2 * h + mc, 0:NR_], pn[0:NR_, h, mc * 128:(mc + 1) * 128], k.identb[0:NR_, 0:NR_]) for h in range(4) for mc in range(2)], ['pnB', 'identb'], [('ps', 0)])
                CP(P, 'vector', PT[:, :, 0:NR_], tb[:, :, 0:NR_], [('ps', 0)], [('PTB', 0)])
                for q in range(NG):
                    oq = k.ps[32 * q:32 * q + 4, 512 * 2: 512 * 4].rearrange("p (h d) -> p h d", h=4)
                    for h in range(4):
                        MM(P, [(oq[:, h, :], PT[:, 2 * h + mc, 32 * q:32 * q + 4], Vb4[r][:, q, mc, h * 256:(h + 1) * 256], mc == 0, mc == 1, (0, 32 * q)) for mc in range(2)],
                           [('PTB', 0), ('Vb4', r, q, 0), ('Vb4', r, q, 1)], [('ps', 2 + h // 2)])
                CP(P, 'scalar', ob[0:NR_, :], k.ps[0:NR_, 512 * 2: 512 * 4], [('ps', 2), ('ps', 3)], ['obB'])
                tb2 = k.psb[:, 1024 * 1: 1024 * 2].rearrange("p (c t) -> p c t", c=8)
                TR(P, [(tb2[:, c, 0:NR_], ob[0:NR_, c * 128:(c + 1) * 128], k.identb[0:NR_, 0:NR_]) for c in range(8)], ['obB', 'identb'], [('ps', 1)])
                CP(P, 'vector', oT[:, :, 4 * NG * g:4 * NG * (g + 1)].rearrange("p c (q t) -> p c q t", t=4),
                   tb2[:, :, :].rearrange("p c (q u) -> p c q u", u=32)[:, :, 0:NG, 0:4], [('ps', 1)], [('oTB', 'c')])
            oproj(T, NS, x_src, x_dst, [('oTB', 'c')])


XBC0 = 1024
DT0 = 2560
VP0 = 2576
DIN = 3600
ST_A = 256


def phase_A(P, k, es, io, x_src, x_dst, do_prompt=True, do_sample=True):
    win = P.sb([128, 8, DIN], BF16, "win", es)
    wdt3 = P.sb([128, 8, 96], BF16, "wdt3", es)
    wout = P.sb([128, 16, D], BF16, "wout", es)
    wpool = P.sb([128, 4, 2, 256], BF16, "wpool", es)
    cb2 = P.sb([128, 2048], BF16, "cstb2", es)
    P.dma('gpsimd', cb2[:], io['consts'][:, CST_SMALL + 256:CST_SMALL + 256 + 2048], writes=['cstb'], sem='cstb2', max_dma_last_dim=4096)
    k.e2b = cb2[:, 0:2048]
    win_d = io['w_in'].rearrange("(c p) n -> p c n", p=128)
    MEMSET(P, 'vector', wdt3[:], 0.0, ['wdt3'])
    for nm_, c0, c1 in (('xbc', XBC0, XBC0 + 768), ('xbc2', XBC0 + 768, DT0 + 16), ('z', 0, 1024), ('vp', VP0, DIN)):
        P.dma_multi('gpsimd', [(win[:, :, c0:c1], win_d[:, :, c0:c1])], [('win', nm_)], ('win', nm_), max_dma_last_dim=4096)
    for r in range(3):
        P.dma('gpsimd', wdt3[:, :, 32 * r:32 * r + 16], win_d[:, :, DT0:DT0 + 16], reads=['wdt3'], writes=[('wdt3', r)], sem=('wdt3', r))
    wout_d = io['w_out'].rearrange("(c p) n -> p c n", p=128)
    for gq in range(2):
        P.dma_multi('gpsimd', [(wout[:, 8 * gq:8 * gq + 8, :], wout_d[:, 8 * gq:8 * gq + 8, :])], [('wout', c) for c in range(8 * gq, 8 * gq + 8)], ('wout', gq), max_dma_last_dim=4096)
    wp_d = io['w_pool'].rearrange("g (cc p) d -> p g cc d", p=128)
    P.dma_multi('gpsimd', [(wpool[:, 0:2, :, :], wp_d[:, 0:2, :, :]), (wpool[:, 2:4, :, :], wp_d[:, 2:4, :, :])], [('wpool', g) for g in range(4)], 'wpool')
    wink = [('win', nm_) for nm_ in ('xbc', 'xbc2', 'z', 'vp')]
    wdtk = [('wdt3', r) for r in range(3)]
    woutk = [('wout', c) for c in range(16)]
    wpk = [('wpool', g) for g in range(4)]

    gA = P.sb([128, 8], F32, "gA", es)
    gY = P.sb([128, 8], F32, "gY", es)
    psc = P.sb([128, 8], F32, "psc", es)
    load_fm(P, k, io['norm_mix'].rearrange("(c p) -> c p", p=128), 8, gA[:, :], 'gA', 'gA', es)
    load_fm(P, k, io['ssm_norm'].rearrange("(c p) -> c p", p=128), 8, gY[:, :], 'gY', 'gY', es)
    load_fm(P, k, io['pool_scale'].rearrange("(c p) -> c p", p=128), 8, psc[:, :], 'psc', 'psc', es)
    cw = P.sb([128, 4, 12], F32, "cwA", es)
    cb = P.sb([128, 12], F32, "cbA", es)
    cwd = io['ssm_conv_w'].rearrange("k (c p) -> k c p", p=128)
    for kk in range(4):
        load_fm(P, k, cwd[kk], 12, cw[:, kk, :], ('cwA', kk), 'cwA%d' % kk, es)
    load_fm(P, k, io['ssm_conv_b'].rearrange("(c p) -> c p", p=128), 12, cb[:, :], 'cbA', 'cbA', es)
    cwk = [('cwA', kk) for kk in range(4)] + ['cbA']
    TS(P, 'vector', cw[:], cw[:], 0.5, None, ALU.mult, None, cwk, cwk[:4])
    TS(P, 'vector', cb[:], cb[:], 0.5, None, ALU.mult, None, ['cbA'], ['cbA'])
    hp = P.sb([128, 4], F32, "hpA", es)
    MEMSET(P, 'vector', hp[:], 0.0, ['hpA'])
    for r in range(3):
        P.dma('sync', hp[32 * r:32 * r + 16, 0:1], io['ssm_dt_bias'].rearrange("(h o) -> h o", o=1), reads=['hpA'], writes=[('hpA', r)], sem=('hpA', r))
        P.dma('sync', hp[32 * r:32 * r + 16, 1:2], io['ssm_a_log'].rearrange("(h o) -> h o", o=1), reads=['hpA'], writes=[('hpA', r)], sem=('hpA', r))
    hpk = [('hpA', r) for r in range(3)]
    ACTF(P, hp[0:96, 2:3], hp[0:96, 1:2], AF.Exp, hpk, ['hpA2'])
    TS(P, 'vector', hp[0:96, 2:3], hp[0:96, 2:3], -1.0, None, ALU.mult, None, ['hpA2'], ['hpA2'])
    hb16 = P.sb([128, 48], F32, "hb16", es)
    P.dma('sync', hb16[:, 0:16], io['ssm_d'].partition_broadcast(128), writes=['hb16d'], sem='hb16d')
    P.dma('sync', hb16[:, 16:32], io['ssm_a_log'].partition_broadcast(128), writes=['hb16a'], sem='hb16a')
    ACTF(P, hb16[:, 32:48], hb16[:, 16:32], AF.Exp, ['hb16a'], ['hb16A'])
    TS(P, 'vector', hb16[:, 32:48], hb16[:, 32:48], -1.0, None, ALU.mult, None, ['hb16A'], ['hb16A'])
    Dbc = hb16[:, 0:16]
    Abc = hb16[:, 32:48]
    onesf = k.cs('ones')

    NSUB = ST_A // 128
    xt = [P.sb([128, D], F32, "xtA%d" % i, es) for i in range(2)]
    hb = k.junk
    st = P.sb([128, 64], F32, "stA", es)
    hT = P.sb([128, 8, ST_A], BF16, "hTA", es)
    xcT2 = [P.sb([128, 12, ST_A], BF16, "xcT0", es), None]
    acc = [P.sb([128, ST_A], F32, "accA%d" % i, es) for i in range(2)]
    th = [P.sb([128, ST_A], F32, "thA0", es)] * 2
    sz2 = [P.sb([128, NSUB, D], BF16, "szA0", es), None]
    pl = P.sb([128, 4, ST_A], BF16, "plA", es)
    pmT2 = [P.sb([128, 8, ST_A], BF16, "pmT0", es), None]
    ynT = P.sb([128, 8, ST_A], BF16, "ynT", es)
    dt3 = P.sb([128, ST_A], F32, "dt3A", es)
    a3 = P.sb([128, ST_A], F32, "a3A", es)
    d1 = a3
    acs3 = P.sb([128, ST_A], F32, "acs3A", es)
    stk2 = [P.sb([128, ST_A], F32, "stkA0", es), None]
    hl2 = [P.sb([128, ST_A], BF16, "hlA0", es), None]
    ptmp = P.sb([128, 2, 320], F32, "ptmpA", es)
    zt = P.sb([128, 512], F32, "ztA", es)
    tk = P.sb([128, 128], F32, "tkA", es)
    sml = P.sb([128, 64], F32, "smlA", es)
    xtok = P.sb([128, D], BF16, "xtokA", es)
    xdt = P.sb([128, D], BF16, "xdtA", es)
    xdtE = P.sb([128, D], BF16, "xdtEA", es)
    Btok = P.sb([128, 256], BF16, "BtokA", es)
    dcy = [P.sb([128, 4, 128], F32, "dcyA%d" % i, es) for i in range(2)]
    MT = P.sb([128, 16, 128], BF16, "MTA", es)
    y1 = P.sb([128, D], F32, "y1A", es)
    y2 = P.sb([128, D], F32, "y2A", es)
    yn = P.sb([128, D], BF16, "ynA", es)
    hst = P.sb([128, D], F32, "hstA", es)
    hbf = P.sb([128, D], BF16, "hbfA", es)
    cnt = {'x': 0, 'st': 0, 'p': 0, 'a': 0, 'd': 0}
    MEMSET(P, 'vector', y1[:], 0.0, [('y1A', 0), ('y1A', 1)])
    MEMSET(P, 'vector', ptmp[:], 0.0, [('ptmpA', 0), ('ptmpA', 1)])

    def v3(ap, nseq, a, b):
        return ap.rearrange("p (s l) -> p s l", s=nseq)[:, :, a:b]

    def run_group(tok0, nseq, L, extx, extv, first, negb, rsp, h0_bf_key, gname, sample_fn=None, par=0):
        xcT, sz, pmT, stk, hl = xcT2[par], sz2[par], pmT2[par], stk2[par], hl2[par]
        KP = 'p%d' % par
        ntok = nseq * L
        nsub = (ntok + 127) // 128
        exk = [('extx' + gname, c) for c in range(12)]
        evk = [('extv' + gname, c) for c in range(8)]
        for j in range(nsub):
            n = min(128, ntok - j * 128)
            i = cnt['x'] % 2
            cnt['x'] += 1
            P.dma('sync', xt[i][0:n, :], x_src[tok0 + j * 128: tok0 + j * 128 + n, :], writes=[('xtA', i)], sem=('xtA', i))
            col = (cnt['st'] % 8) * 4
            cnt['st'] += 1
            norm_T(P, k, xt[i][0:n, :], n, [('xtA', i)], gA[:, :], 'gA', hT[:, :, j * 128: j * 128 + n], ('hTA', j), st, col, hb, 'junk', 2)
        hTk = [('hTA', j) for j in range(nsub)]

        def proj_fm(col0, width, lhs_w, wkeys):
            b = cnt['p'] % 2
            cnt['p'] += 1
            pb = k.ps[0:width, 512 * b: 512 * b + ntok]
            MM(P, [(pb, lhs_w[:, kc, col0:col0 + width], hT[:, kc, 0:ntok], kc == 0, kc == 7) for kc in range(8)], wkeys + hTk, [('ps', b)])
            return pb, ('ps', b)

        xck = [('xcT' + KP, c) for c in range(12)]

        def sec_xbc():
            halo, fixx = extx
            fk = ['fixx0' + gname, 'fixx1' + gname, 'fixx2' + gname, 'fixt' + gname]
            wv = [bcast(cw[:, kk, :], 2, nseq) for kk in range(4)]
            hh = [halo[:, :, :, r_] for r_ in range(3)]
            tmpf = fixx[:, :, :, 3]
            TT(P, 'gpsimd', fixx[:, :, :, 0], hh[0], wv[0], ALU.mult, exk + cwk, [fk[0]])
            TT(P, 'gpsimd', tmpf, hh[1], wv[1], ALU.mult, exk + cwk, [fk[3]])
            TT(P, 'gpsimd', fixx[:, :, :, 0], fixx[:, :, :, 0], tmpf, ALU.add, [fk[0], fk[3]], [fk[0]])
            TT(P, 'gpsimd', tmpf, hh[2], wv[2], ALU.mult, exk + cwk + [fk[0]], [fk[3]])
            TT(P, 'gpsimd', fixx[:, :, :, 0], fixx[:, :, :, 0], tmpf, ALU.add, [fk[0], fk[3]], [fk[0]])
            TT(P, 'gpsimd', fixx[:, :, :, 1], hh[1], wv[0], ALU.mult, exk + cwk, [fk[1]])
            TT(P, 'gpsimd', tmpf, hh[2], wv[1], ALU.mult, exk + cwk + [fk[0]], [fk[3]])
            TT(P, 'gpsimd', fixx[:, :, :, 1], fixx[:, :, :, 1], tmpf, ALU.add, [fk[1], fk[3]], [fk[1]])
            TT(P, 'gpsimd', fixx[:, :, :, 2], hh[2], wv[0], ALU.mult, exk + cwk + [fk[1]], [fk[2]])
            fks = fk[:3]

            def xbc_stage1(c):
                pb, pk = proj_fm(XBC0 + c * 128, 128, win, [('win', 'xbc' if c < 6 else 'xbc2')])
                r = c % 2
                a_ap = v3(acc[r][:, 0:ntok], nseq, 0, L)
                p3 = v3(pb, nseq, 0, L)
                ACTF(P, a_ap, p3, AF.Identity, [pk] + cwk, [('accA', r)], bias=cb[:, c:c + 1], scale=cw[:, 3, c:c + 1])
                for sh in (1, 2, 3):
                    STT(P, a_ap[:, :, sh:L], p3[:, :, 0:L - sh], cw[:, 3 - sh, c:c + 1], a_ap[:, :, sh:L], ALU.mult, ALU.add, [pk, ('accA', r)] + cwk, [('accA', r)])
                TT(P, 'vector', a_ap[:, :, 0:3], a_ap[:, :, 0:3], fixx[:, c, :, 0:3], ALU.add, [('accA', r)] + fks, [('accA', r)])
                CP(P, 'vector', halo[:, c, :, :], p3[:, :, L - 3:L], [pk] + fk, [exk[c]])

            def xbc_stage2(c):
                r = c % 2
                a_ap = v3(acc[r][:, 0:ntok], nseq, 0, L)
                t_ap = v3(th[c % 2][:, 0:ntok], nseq, 0, L)
                ACTF(P, t_ap, a_ap, AF.Tanh, [('accA', r)], [('thA', 0)])
                STT(P, v3(xcT[:, c, 0:ntok], nseq, 0, L), t_ap, 1.0, a_ap, ALU.add, ALU.mult, [('thA', 0), ('accA', r)], [('xcT' + KP, c)])
            for c in range(13):
                if c < 12:
                    xbc_stage1(c)
                if c >= 1:
                    xbc_stage2(c - 1)
                yield
            xck = [('xcT' + KP, c) for c in range(12)]

            pb, pk = proj_fm(0, 96, wdt3, wdtk)
            ACTF(P, d1[0:96, 0:ntok], pb, AF.Exp, [pk] + hpk, ['a3A'], bias=hp[0:96, 0:1])
            ACTF(P, dt3[0:96, 0:ntok], d1[0:96, 0:ntok], AF.Ln, ['a3A'], ['dt3A'], bias=1.0)
            TS(P, 'vector', a3[0:96, 0:ntok], dt3[0:96, 0:ntok], hp[0:96, 2:3], None, ALU.mult, None, ['dt3A', 'hpA2'], ['a3A'])
            P.V(lambda e: e.tensor_tensor_scan(out=acs3[0:96, 0:ntok], data0=rsp[0:96, 0:ntok], data1=a3[0:96, 0:ntok], initial=0.0, op0=ALU.mult, op1=ALU.add),
                reads=['a3A', 'cst'], writes=['acs3A'])
            CP(P, 'vector', stk[0:96, 0:ntok], dt3[0:96, 0:ntok], ['dt3A'], ['stkA' + KP])
            CP(P, 'vector', stk[32:48, 0:ntok], acs3[32:48, 0:ntok], ['acs3A', 'stkA' + KP], ['stkA' + KP])
            nch = ntok // min(L, 128)
            Lc = min(L, 128)
            a3v = acs3[64:80, 0:ntok].rearrange("p (s l) -> p s l", l=Lc)
            TT(P, 'vector', stk[64:80, 0:ntok].rearrange("p (s l) -> p s l", l=Lc), a3v, bcast(a3v[:, :, Lc - 1], 2, Lc), ALU.subtract, ['acs3A', 'stkA' + KP], ['stkA' + KP])
            CP(P, 'vector', hl[0:96, 0:ntok], acs3[0:96, 0:ntok], ['acs3A'], ['hlA' + KP])
            TT(P, 'vector', hl[32:48, 0:ntok], acs3[32:48, 0:ntok], hl[32:48, 0:ntok], ALU.subtract, ['acs3A', 'hlA' + KP], ['hlA' + KP])

            yield
            yield

        def sec_z():
            for j in range(nsub):
                n = min(128, ntok - j * 128)
                for hf in range(2):
                    b = cnt['p'] % 2
                    cnt['p'] += 1
                    pb = k.ps[0:n, 512 * b: 512 * b + 512]
                    MM(P, [(pb, hT[:, kc, j * 128: j * 128 + n], win[:, kc, hf * 512:(hf + 1) * 512], kc == 0, kc == 7) for kc in range(8)], [('win', 'z')] + hTk, [('ps', b)])
                    r = cnt['a'] % 2
                    cnt['a'] += 1
                    ACTF(P, zt[0:n, :], pb, AF.Tanh, [('ps', b)], ['ztA'], scale=0.5)
                    STT(P, sz[0:n, j, hf * 512:(hf + 1) * 512], zt[0:n, :], 1.0, pb, ALU.add, ALU.mult, ['ztA', ('ps', b)], [('szA' + KP, j, hf)])
                    yield

            yield

        def sec_pool_a():
            if nseq == 1 and not first:
                CP(P, 'vector', extv[:, :, 0, 0:15], extv[:, :, 0, L:L + 15], evk, evk)
            for c in range(8):
                pb, pk = proj_fm(VP0 + c * 128, 128, win, [('win', 'vp')])
                CP(P, 'scalar', extv[:, c, :, 15:L + 15], v3(pb, nseq, 0, L), [pk], [evk[c]])
                if c % 2 == 1:
                    yield
            yield

        def sec_pool():
            def wpool_chunk(co):
                g, dh = co // 2, co % 2
                b = cnt['p'] % 2
                cnt['p'] += 1
                pb = k.ps[:, 512 * b: 512 * b + ntok]
                MM(P, [(pb, wpool[:, g, cc, dh * 128:(dh + 1) * 128], pl[:, (2 * g + cc) % 4, 0:ntok], cc == 0, cc == 1) for cc in range(2)],
                   wpk + [('plA', (2 * g) % 4), ('plA', (2 * g + 1) % 4)], [('ps', b)])
                ACTF(P, pmT[:, co, 0:ntok], pb, AF.Identity, [('ps', b), 'psc'], [('pmT' + KP, co)], scale=psc[:, co:co + 1])

            for c in range(8):
                gi = c // 2
                w = 2 << gi
                cur = extv[:, c, :, :]
                tot = L + 15
                step = 1
                bufs = [ptmp[:, 0, 0:nseq * tot].rearrange("p (s l) -> p s l", s=nseq), ptmp[:, 1, 0:nseq * tot].rearrange("p (s l) -> p s l", s=nseq)]
                bkeys = [('ptmpA', 0), ('ptmpA', 1)]
                bi = 0
                ckeys = [evk[c]]
                while step < w:
                    o = bufs[bi]
                    TT(P, 'gpsimd', o[:, :, step:tot], cur[:, :, step:tot], cur[:, :, 0:tot - step], ALU.add, ckeys, [bkeys[bi]])
                    cur = o
                    ckeys = [bkeys[bi]]
                    bi ^= 1
                    step *= 2
                o = bufs[bi]
                TS(P, 'gpsimd', o[:, :, 15:tot], cur[:, :, 15:tot], 1.0 / w, 0.0, ALU.mult, ALU.add, ckeys, [bkeys[bi]])
                if first and nseq == 1:
                    TT(P, 'gpsimd', o[:, :, 15:31], cur[:, :, 15:31], k.cs('csc')[:, gi * 16:(gi + 1) * 16].unsqueeze(1), ALU.mult, ckeys + ['cst'], [bkeys[bi]])
                TT(P, 'gpsimd', v3(pl[:, c % 4, 0:ntok], nseq, 0, L), o[:, :, 15:tot], extv[:, c, :, 15:tot], ALU.subtract, [bkeys[bi], evk[c]], [('plA', c % 4)])
                yield
                if c % 2 == 1:
                    for co in (c - 1, c):
                        wpool_chunk(co)
                    yield
            yield

        yield from sec_xbc()
        yield from sec_z()
        yield 'POOLA'
        yield from sec_pool_a()
        pmk = [('pmT' + KP, c) for c in range(8)]
        gpb = sec_pool()

        def adv(nsteps=1):
            for _ in range(nsteps):
                try:
                    next(gpb)
                except StopIteration:
                    return

        yield 'SPLIT'
        for j in range(nsub):
            n = min(128, ntok - j * 128)
            js = slice(j * 128, j * 128 + n)
            xb = k.psb[0:n, 1024 * 2: 1024 * 2 + 1024].rearrange("p (c t) -> p c t", c=8)
            TR(P, [(xb[:, c, :], xcT[:, c, js], k.identb) for c in range(8)], xck + ['identb'], [('ps', 2)])
            sp = k.ps[0:n, 512 * 3: 512 * 3 + 96]
            TR(P, [(sp, stk[0:96, js], k.identf[0:96, 0:96])], ['stkA' + KP, 'cst'], [('ps', 3)])
            CP(P, 'vector', tk[0:n, 0:96], sp, [('ps', 3)], ['tkA'])
            bb = k.psb[0:n, 1024 * 3 + 256: 1024 * 3 + 512].rearrange("p (c t) -> p c t", c=2)
            TR(P, [(bb[:, g, :], xcT[:, 8 + g, js], k.identb) for g in range(2)], xck + ['identb'], [('ps', 3)])
            CP(P, 'vector', Btok[0:n, :].rearrange("p (c t) -> p c t", c=2), bb, [('ps', 3)], ['BtokA'])
            TS(P, 'vector', sml[0:n, 0:16], tk[0:n, 32:48], -1.0, None, ALU.mult, None, ['tkA'], ['nacs'])
            ACTF(P, sml[0:n, 16:32], tk[0:n, 32:48], AF.Exp, ['tkA'], ['eacs'])
            ACTF(P, sml[0:n, 32:48], tk[0:n, 64:80], AF.Exp, ['tkA'], ['dend'], scale=-1.0)
            TT(P, 'vector', tk[0:n, 96:112], tk[0:n, 0:16], Abc[0:n, :], ALU.mult, ['tkA', 'hb16A'], ['atok'])
            CP(P, 'scalar', xtok[0:n, :].rearrange("p (c t) -> p c t", c=8), xb, [('ps', 2)], ['xtokA'])
            TT(P, 'vector', xdt[0:n, :].rearrange("p (h q) -> p h q", h=16), k.psb[0:n, 1024 * 2: 1024 * 2 + 1024].rearrange("p (h q) -> p h q", h=16),
               bcast(tk[0:n, 0:16], 2, 64), ALU.mult, [('ps', 2), 'tkA'], ['xdtA'])
            TT(P, 'vector', xdtE[0:n, :].rearrange("p (h q) -> p h q", h=16), xdt[0:n, :].rearrange("p (h q) -> p h q", h=16),
               bcast(sml[0:n, 32:48], 2, 64), ALU.mult, ['xdtA', 'dend'], ['xdtEA'])
            adv(2)
            yield
            cbp = k.ps[0:n, 512 * 3: 512 * 3 + 2 * n].rearrange("p (g l) -> p g l", g=2)
            MM(P, [(cbp[:, g, :], xcT[:, 8 + g, js], xcT[:, 10 + g, js], True, True) for g in range(2)], xck, [('ps', 3)])
            for q in range(4):
                b = 4 + q % 2
                dp = k.ps[0:n, 512 * b: 512 * b + 4 * n].rearrange("p (h l) -> p h l", h=4)
                items = []
                for hh in range(4):
                    h = 4 * q + hh
                    items.append((dp[:, hh, :], k.e2b[0:64, h * 128: h * 128 + n], hl[0:64, js], True, False))
                    items.append((dp[:, hh, :], k.identb[0:n, 0:n], negb[0:n, 0:n], False, True))
                MM(P, items, ['hlA' + KP, 'cstb', 'identb'], [('ps', b)])
                r = cnt['d'] % 2
                cnt['d'] += 1
                for hh in range(4):
                    h = 4 * q + hh
                    ACTF(P, dcy[r][0:n, hh, 0:n], dp[:, hh, :], AF.Exp, [('ps', b), 'nacs'], [('dcyA', r, hh)], bias=sml[0:n, h:h + 1])
                g = q // 2
                TT(P, 'vector', MT[0:n, 4 * q:4 * q + 4, 0:n], dcy[r][0:n, :, 0:n], bcast(cbp[:, g, :], 1, 4), ALU.mult,
                   [('dcyA', r, hh) for hh in range(4)] + [('ps', 3)], [('MTA', q)])
                adv(2)
                yield
            MTk = [('MTA', q) for q in range(4)]
            yd = k.ps[0:n, 512 * 6: 512 * 8].rearrange("p (h q) -> p h q", h=16)
            for half in range(2):
                MM(P, [(yd[:, h, :], MT[0:n, h, 0:n], xdt[0:n, h * 64:(h + 1) * 64], True, True) for h in range(8 * half, 8 * half + 8)],
                   MTk + ['xdtA'], [('ps', 6 + half)])
            adv(2)
            yield
            yo = k.ps[0:n, 512 * 4: 512 * 6]
            if sample_fn is not None:
                sample_fn(MTk, xck, xcT, hl, 'hlA' + KP)
                h0_bf_key = 'sample'
                TT(P, 'vector', y1[0:n, :].rearrange("p (h q) -> p h q", h=16), yo.rearrange("p (h q) -> p h q", h=16), bcast(sml[0:n, 16:32], 2, 64), ALU.mult,
                   [('ps', 4), ('ps', 5), 'eacs'], [('y1A', 0), ('y1A', 1)])
            elif h0_bf_key is not None:
                for g in range(2):
                    MM(P, [(yo[:, g * 512:(g + 1) * 512], xcT[:, 10 + g, js], hbf[:, g * 512:(g + 1) * 512], True, True)], xck + [h0_bf_key], [('ps', 4 + g)])
                TT(P, 'vector', y1[0:n, :].rearrange("p (h q) -> p h q", h=16), yo.rearrange("p (h q) -> p h q", h=16), bcast(sml[0:n, 16:32], 2, 64), ALU.mult,
                   [('ps', 4), ('ps', 5), 'eacs'], [('y1A', 0), ('y1A', 1)])
            TT(P, 'gpsimd', y2[0:n, :].rearrange("p (h q) -> p h q", h=16), xtok[0:n, :].rearrange("p (h q) -> p h q", h=16), bcast(Dbc[0:n, :], 2, 64), ALU.mult,
               ['xtokA', 'hb16d'], [('y2A', 0), ('y2A', 1)])
            if h0_bf_key is not None:
                TT(P, 'vector', y1[0:n, :], y1[0:n, :], y2[0:n, :], ALU.add, [('y1A', 0), ('y1A', 1), ('y2A', 0), ('y2A', 1)], [('y1A', 0), ('y1A', 1)])
                ysrc, ysk = y1, [('y1A', 0), ('y1A', 1)]
            else:
                ysrc, ysk = y2, [('y2A', 0), ('y2A', 1)]
            TT(P, 'vector', y1[0:n, :], k.ps[0:n, 512 * 6: 512 * 8], ysrc[0:n, :], ALU.add, [('ps', 6), ('ps', 7)] + ysk, [('y1A', 0), ('y1A', 1)])
            TT(P, 'vector', y1[0:n, :], y1[0:n, :], sz[0:n, j, :], ALU.mult, [('y1A', 0), ('y1A', 1), ('szA' + KP, j, 0), ('szA' + KP, j, 1)], [('y1A', 0), ('y1A', 1)])
            adv(2)
            yield
            for g in range(2):
                col = (cnt['st'] % 8) * 4
                cnt['st'] += 1
                r_ap, ks = rstd_op(P, k, y1[0:n, g * 512:(g + 1) * 512], n, [('y1A', 0), ('y1A', 1)], st, col, scale=0.25 / 512, extra=0.5)
                TS(P, 'vector', yn[0:n, g * 512:(g + 1) * 512], y1[0:n, g * 512:(g + 1) * 512], r_ap, None, ALU.mult, None, [('y1A', 0), ('y1A', 1), ks], [('ynA', g)])
            yb = k.psb[:, 1024 * 2: 1024 * 2 + 1024].rearrange("p (c t) -> p c t", c=8)
            TR(P, [(yb[:, c, 0:n], yn[0:n, c * 128:(c + 1) * 128], k.identb[0:n, 0:n]) for c in range(8)], [('ynA', 0), ('ynA', 1), 'identb'], [('ps', 2)])
            TT(P, 'vector', ynT[:, :, js], yb[:, :, 0:n], bcast(gY[:, :], 2, n), ALU.mult, [('ps', 2), 'gY'], [('ynT', j)])
            adv(2)
            yield
            if nseq == 1:
                sp2 = k.ps[:, 512 * 6: 512 * 8]
                for g in range(2):
                    MM(P, [(sp2[:, g * 512:(g + 1) * 512], Btok[0:n, g * 128:(g + 1) * 128], xdtE[0:n, g * 512:(g + 1) * 512], True, True)], ['BtokA', 'xdtEA'], [('ps', 6 + g)])
                cdp = k.ps[:, 512 * 3 + 256: 512 * 3 + 272]
                MM(P, [(cdp, onesf[0:n, :], tk[0:n, 96:112], True, True)], ['cst', 'atok'], [('ps', 3)])
                ACTF(P, tk[:, 112:128], cdp, AF.Exp, [('ps', 3)], ['cdA'])
                if h0_bf_key is not None:
                    TT(P, 'vector', y2[:, :].rearrange("p (h q) -> p h q", h=16), hst[:, :].rearrange("p (h q) -> p h q", h=16), bcast(tk[:, 112:128], 2, 64), ALU.mult,
                       ['hstA', 'cdA'], [('y2A', 0), ('y2A', 1)])
                    TT(P, 'vector', hst[:, :], sp2, y2[:, :], ALU.add, [('ps', 6), ('ps', 7), ('y2A', 0), ('y2A', 1)], ['hstA'])
                else:
                    CP(P, 'vector', hst[:, :], sp2, [('ps', 6), ('ps', 7)], ['hstA'])
                CP(P, 'vector', hbf[:, :], hst[:, :], ['hstA'], ['hbfA'])
                h0_bf_key = 'hbfA'
            adv(2)
            yield
        ynk = [('ynT', j) for j in range(nsub)]
        adv(100)
        for j in range(nsub):
            n = min(128, ntok - j * 128)
            js = slice(j * 128, j * 128 + n)
            i = cnt['x'] % 2
            cnt['x'] += 1
            P.dma('sync', xt[i][0:n, :], x_src[tok0 + j * 128: tok0 + j * 128 + n, :], writes=[('xtA', i)], sem=('xtA', i))
            for hf in range(2):
                b = cnt['p'] % 2
                cnt['p'] += 1
                pb = k.ps[0:n, 512 * b: 512 * b + 512]
                items = [(pb, ynT[:, c, js], wout[:, c, hf * 512:(hf + 1) * 512], c == 0, False) for c in range(8)]
                items += [(pb, pmT[:, c, js], wout[:, 8 + c, hf * 512:(hf + 1) * 512], False, c == 7) for c in range(8)]
                MM(P, items, ynk + pmk + woutk, [('ps', b)])
                TT(P, 'vector', xt[i][0:n, hf * 512:(hf + 1) * 512], pb, xt[i][0:n, hf * 512:(hf + 1) * 512], ALU.add, [('ps', b), ('xtA', i)], [('xtA', i)])
            P.dma('sync', x_dst[tok0 + j * 128: tok0 + j * 128 + n, :], xt[i][0:n, :], reads=[('xtA', i)], sem=('x1o', i))
            adv(2)
            yield

    def rows_out(src_fm, nrows_per, nchunks, skeys, dst_rows, tag):
        R = nrows_per
        for q in range(0, nchunks, 4):
            m = min(4, nchunks - q)
            pb = k.ps[0:R, 512 * 3: 512 * 3 + m * 128]
            for u in range(m):
                sap = src_fm(q + u)
                if len(sap.shape) > 2:
                    CP(P, 'vector', tk[:, 0:R].rearrange("p (a b) -> p a b", a=sap.shape[1]), sap, skeys, ['tkA'])
                    sap, sk2 = tk[:, 0:R], ['tkA']
                else:
                    sk2 = skeys
                TR(P, [(pb[:, u * 128:(u + 1) * 128], sap, k.identf)], sk2 + ['cst'], [('ps', 3)])
            CP(P, 'vector', y2[0:R, 0:m * 128], pb, [('ps', 3)], [('y2A', 0), ('y2A', 1)])
            P.dma('sync', dst_rows[:, q * 128:(q + m) * 128], y2[0:R, 0:m * 128], reads=[('y2A', 0), ('y2A', 1)], sem='rows' + tag)

    if do_prompt:
        with ExitStack() as esp:
            xcT2[1] = P.sb([128, 12, ST_A], BF16, "xcT1", esp)
            sz2[1] = P.sb([128, NSUB, D], BF16, "szA1", esp)
            pmT2[1] = P.sb([128, 8, ST_A], BF16, "pmT1", esp)
            stk2[1] = P.sb([128, ST_A], F32, "stkA1", esp)
            hl2[1] = P.sb([128, ST_A], BF16, "hlA1", esp)
            haloP = P.sb([128, 12, 1, 3], F32, "haloxP", esp)
            fixP = P.sb([128, 12, 1, 4], F32, "fixxP", esp)
            extx = (haloP, fixP)
            extv = P.sb([128, 8, 1, ST_A + 15], F32, "extvP", esp)
            MEMSET(P, 'vector', haloP[:], 0.0, [('extxP', c) for c in range(12)])
            MEMSET(P, 'vector', extv[:], 0.0, [('extvP', c) for c in range(8)])
            NSUP = T // ST_A
            RB, RF = 1, 1
            gens = [run_group(S * ST_A, 1, ST_A, extx, extv, S == 0, k.negcb, k.cs('rsp'), (None if S == 0 else 'hbfA'), 'P', par=S % 2) for S in range(NSUP)]

            hold = {}

            def step(g, front, back_alive=False):
                if front and hold.get(id(g)) and back_alive:
                    return True
                hold.pop(id(g), None)
                try:
                    v = next(g)
                except StopIteration:
                    return False
                if front and v == 'POOLA' and back_alive:
                    hold[id(g)] = True
                return not (front and v == 'SPLIT')
            while step(gens[0], True):
                pass
            for S in range(NSUP):
                gb = gens[S]
                gf = gens[S + 1] if S + 1 < NSUP else None
                ab, af = True, gf is not None
                while ab or af:
                    for _ in range(RB):
                        if ab:
                            ab = step(gb, False)
                    for _ in range(RF):
                        if af:
                            af = step(gf, True, ab)
            rows_out(lambda c: haloP[:, c, 0, :], 3, 12, [('extxP', c) for c in range(12)], io['conv_prompt'], 'cp')
            rows_out(lambda c: extv[:, c, 0, ST_A:ST_A + 15], 15, 8, [('extvP', c) for c in range(8)], io['pool_prompt'], 'pp')
            for half in range(2):
                pb = k.ps[:, 512 * (4 + half): 512 * (5 + half)]
                TR(P, [(pb[:, u * 128:(u + 1) * 128], hst[:, (4 * half + u) * 128:(4 * half + u + 1) * 128], k.identf) for u in range(4)], ['hstA', 'cst'], [('ps', 4 + half)])
                CP(P, 'vector', y1[:, half * 512:(half + 1) * 512], pb, [('ps', 4 + half)], [('y1A', half)])
            P.dma('sync', io['ssm_prompt'].rearrange("(c p) n -> p c n", p=128), y1[:, :].rearrange("p (c n) -> p c n", c=8), reads=[('y1A', 0), ('y1A', 1)], sem='ssmP')
            P.barrier()

    if do_sample:
        with ExitStack() as ess:
            haloS = P.sb([128, 12, 16, 3], F32, "haloxS", ess)
            fixS = P.sb([128, 12, 16, 4], F32, "fixxS", ess)
            extx = (haloS, fixS)
            extv = P.sb([128, 8, 16, 19], F32, "extvS", ess)
            h0a = P.sb([128, 8, 128], F32, "h0S", ess)
            cdT = P.sb([128, 8, 16], F32, "cdTS", ess)
            cb3 = P.sb([128, 1024], BF16, "cstb3", ess)
            P.dma('gpsimd', cb3[:], io['consts'][:, CST_SMALL + 256 + 2048:CST_SMALL + 256 + 3072], writes=['cstb3'], sem='cstb3', max_dma_last_dim=4096)
            k.expdb = cb3[:, :]
            exk = [('extxS', c) for c in range(12)]
            evk = [('extvS', c) for c in range(8)]
            y1k = [('y1A', 0), ('y1A', 1)]
            y2k = [('y2A', 0), ('y2A', 1)]
            scv = io['state_ssm_conv'].rearrange("b r c -> (b r) c")
            P.dma('sync', y1[0:48, :], scv[:, 0:1024], writes=y1k, sem='stS1')
            P.dma('sync', y2[0:48, 0:512], scv[:, 1024:1536], writes=y2k, sem='stS2')
            for c in range(12):
                src = y1[0:48, c * 128:(c + 1) * 128] if c < 8 else y2[0:48, (c - 8) * 128:(c - 7) * 128]
                bq_ = 3 - c % 2
                pb = k.ps[:, 512 * bq_: 512 * bq_ + 48]
                TR(P, [(pb, src, k.identf[0:48, 0:48])], y1k + y2k + ['cst'], [('ps', bq_)])
                CP(P, 'vector' if c % 2 == 0 else 'scalar', haloS[:, c, :, :], pb.rearrange("p (b r) -> p b r", r=3), [('ps', bq_)], [exk[c]])
            spv = io['state_pool'].rearrange("b r c -> (b r) c")
            for half in range(2):
                P.dma('sync', y1[0:120, :], spv[half * 120:(half + 1) * 120, :], reads=y1k, writes=y1k, sem=('stS3', half))
                for c in range(8):
                    bq_ = 3 - c % 2
                    pb = k.ps[:, 512 * bq_: 512 * bq_ + 120]
                    TR(P, [(pb, y1[0:120, c * 128:(c + 1) * 128], k.identf[0:120, 0:120])], y1k + ['cst'], [('ps', bq_)])
                    CP(P, 'vector' if c % 2 == 0 else 'scalar', extv[:, c, 8 * half:8 * half + 8, 0:15], pb.rearrange("p (b r) -> p b r", r=15), [('ps', bq_)], [evk[c]])
            ssd = io['state_ssm'].rearrange("b (c q) n -> b q c n", q=128)
            sso = io['ssm_sample'].rearrange("b (c q) n -> b q c n", q=128)

            def sample_fn(MTk, xck, xcT, hl, hlk):
                n = NS
                MEMSET(P, 'vector', MT[:], 0.0, MTk)
                ctm_diag = bass.AP(MT.tensor if hasattr(MT, 'tensor') else MT, 0, [[2048, 128], [1024, 2], [68, 16], [1, 4]])
                CP(P, 'vector', ctm_diag, xcT[:, 10:12, 0:64].rearrange("p g (b t) -> p g b t", t=4), xck + MTk, MTk)
                CTm = MT[:].rearrange("p h l -> p (h l)").rearrange("p (g b t) -> p g b t", g=2, b=16)
                cdp = k.ps[:, 512 * 3: 512 * 3 + 128].rearrange("p (c b) -> p c b", c=8)
                hl_last = hl[0:64, 0:64].rearrange("p (b t) -> p b t", t=4)[:, :, 3]
                MM(P, [(cdp[:, jc, :], k.expdb[0:64, jc * 128:(jc + 1) * 128], hl_last, True, True) for jc in range(8)], [hlk, 'cstb3'], [('ps', 3)])
                ACTF(P, cdT[:, :, :], cdp, AF.Exp, [('ps', 3)], ['cdTS'])
                for b in range(16):
                    h0 = h0a[:] if b % 2 == 0 else hst[:, :].rearrange("p (c n) -> p c n", c=8)
                    h0k = 'h0S' if b % 2 == 0 else 'hstA'
                    hn = (y1 if b % 2 == 0 else y2)[:, :].rearrange("p (c n) -> p c n", c=8)
                    hnk = y1k if b % 2 == 0 else y2k
                    if b == 0:
                        P.dma('sync', h0, ssd[0], writes=[h0k], sem=('h0S', 0))
                    if b + 1 < 16:
                        h0n = h0a[:] if (b + 1) % 2 == 0 else hst[:, :].rearrange("p (c n) -> p c n", c=8)
                        P.dma('sync', h0n, ssd[b + 1], writes=['h0S' if (b + 1) % 2 == 0 else 'hstA'], sem=('h0S', (b + 1) % 2))
                    tp = k.ps[:, 512 * 2: 512 * 4]
                    for half in range(2):
                        TR(P, [(tp[:, (4 * half + u) * 128:(4 * half + u + 1) * 128], h0[:, 4 * half + u, :], k.identf) for u in range(4)], [h0k, 'cst'], [('ps', 2 + half)])
                    CP(P, 'scalar', hbf[:, 0:512], tp[:, 0:512], [('ps', 2)], [('hbfA', 0)])
                    CP(P, 'vector', hbf[:, 512:1024], tp[:, 512:1024], [('ps', 3)], [('hbfA', 1)])
                    for g in range(2):
                        MM(P, [(k.ps[0:n, 512 * (4 + g): 512 * (5 + g)], CTm[:, g, b, :], hbf[:, g * 512:(g + 1) * 512], b == 0, b == 15)], MTk + [('hbfA', g)], [('ps', 4 + g)])
                    ACTF(P, xdt[0:n, :], xdtE[0:n, :], AF.Identity, ['xdtEA', 'cst'], ['xdtA'], scale=k.cs('blk')[0:n, b:b + 1])
                    sp = k.ps[:, 0:1024].rearrange("p (c n) -> p c n", c=8)
                    for half in range(2):
                        MM(P, [(sp[:, jc, :], xdt[0:n, jc * 128:(jc + 1) * 128], Btok[0:n, (jc // 4) * 128:(jc // 4 + 1) * 128], True, True) for jc in range(4 * half, 4 * half + 4)],
                           ['xdtA', 'BtokA'], [('ps', half)])
                    TT(P, 'gpsimd', hn, h0, bcast(cdT[:, :, b], 2, 128), ALU.mult, [h0k, 'cdTS'], hnk)
                    TT(P, 'vector', hn.rearrange("p c n -> p (c n)"), hn.rearrange("p c n -> p (c n)"), k.ps[:, 0:1024], ALU.add, hnk + [('ps', 0), ('ps', 1)], hnk)
                    P.dma('sync', sso[b], hn, reads=hnk, sem=('hnS', b % 2))

            for _ in run_group(T, 16, 4, extx, extv, False, k.negsb, k.cs('rss'), None, 'S', sample_fn=sample_fn, par=0):
                pass
            cso = io['conv_sample'].rearrange("b r c -> (b r) c")
            rows_out(lambda c: haloS[:, c, :, :], 48, 12, exk, cso, 'cs')
            pso = io['pool_sample'].rearrange("b r c -> (b r) c")
            for half in range(2):
                rows_out(lambda c, half=half: extv[:, c, 8 * half:8 * half + 8, 4:19], 120, 8, evk, pso[half * 120:(half + 1) * 120, :], 'ps%d' % half)
            P.barrier()


N_CORES = 8
_IN_SPECS = [
    ('x_src', [T + NS, D]), ('consts', [128, CST_W]), ('mem_prompt', [256, D]),
    ('state_ssm', [16, 1024, 128]), ('state_ssm_conv', [16, 3, 1536]), ('state_pool', [16, 15, 1024]),
    ('state_ffn_conv', [16, 2, 2 * DFF]), ('cache_mem_k', [16, 256, 4, 256]), ('cache_mem_v', [16, 256, 4, 256]),
    ('norm_mix', [D]), ('w_in', [D, DIN]), ('ssm_conv_w', [4, 1536]), ('ssm_conv_b', [1536]),
    ('ssm_dt_bias', [16]), ('ssm_a_log', [16]), ('ssm_d', [16]), ('ssm_norm', [D]),
    ('w_pool', [4, 256, 256]), ('pool_scale', [D]), ('w_out', [2 * D, D]), ('norm_mem', [D]), ('norm_memkv', [D]),
    ('w_mq', [D, D]), ('w_mk', [D, D]), ('w_mv', [D, D]), ('w_mo', [D, D]), ('norm_ffn', [D]),
    ('w_up', [D, 2 * DFF]), ('ffn_conv_w', [3, 2 * DFF]), ('ffn_conv_b', [2 * DFF]), ('w_down', [DFF, D]), ('final_norm', [D]),
]
_OUT_SPECS = [
    ('y_prompt', [T, D]), ('y_sample', [NS, D]), ('ssm_prompt', [1024, 128]), ('ssm_sample', [16, 1024, 128]),
    ('conv_prompt', [3, 1536]), ('conv_sample', [16, 3, 1536]), ('pool_prompt', [15, 1024]), ('pool_sample', [16, 15, 1024]),
    ('ffn_prompt', [2, 2 * DFF]), ('ffn_sample', [16, 2, 2 * DFF]), ('mem_k_prompt', [256, D]), ('mem_v_prompt', [256, D]),
]


def build_program():
    nc = bass.Bass("TRN2", target_bir_lowering=False)
    io = {}
    for name, shape in _IN_SPECS:
        io[name] = nc.dram_tensor(name, list(shape), F32, kind="ExternalInput").ap()
    for name, shape in _OUT_SPECS:
        io[name] = nc.dram_tensor(name, list(shape), F32, kind="ExternalOutput").ap()
    x1 = nc.dram_tensor("x1_scratch", [T + NS, D], F32, kind="Internal").ap()
    x2 = nc.dram_tensor("x2_scratch", [T + NS, D], F32, kind="Internal").ap()
    with ExitStack() as es:
        P = Prog(nc, es)
        k = K()
        setup_common(P, k, io['consts'])
        with ExitStack() as es2:
            phase_A(P, k, es2, io, io['x_src'], x1)
            P.end_phase()
        with ExitStack() as es2:
            phase_B(P, k, es2, io, x1, x2)
            P.end_phase()
        with ExitStack() as es2:
            phase_C(P, k, es2, io, x2)
            P.end_phase()
        P.emit()
    return nc


_PROG = {}


def kernel(**inputs):
    f = lambda a: np.ascontiguousarray(np.asarray(a, dtype=np.float32))
    if 'nc' not in _PROG:
        _PROG['nc'] = build_program()
    nc = _PROG['nc']
    consts = make_consts()
    xp, xs = f(inputs['x_prompt']), f(inputs['x_sample'])
    shared = {'consts': consts}
    for name in ['norm_mix', 'w_in', 'ssm_conv_w', 'ssm_conv_b', 'ssm_dt_bias', 'ssm_a_log', 'ssm_d', 'ssm_norm', 'w_pool',
                 'pool_scale', 'w_out', 'norm_mem', 'norm_memkv', 'w_mq', 'w_mk', 'w_mv', 'w_mo', 'norm_ffn', 'w_up',
                 'ffn_conv_w', 'ffn_conv_b', 'w_down']:
        shared[name] = f(inputs[name])[0]
    shared['final_norm'] = f(inputs['final_norm'])
    st_ssm, st_conv = f(inputs['state_ssm'])[0], f(inputs['state_ssm_conv'])[0]
    st_pool, st_ffn = f(inputs['state_pool'])[0], f(inputs['state_ffn_conv'])[0]
    ck, cv, mem = f(inputs['cache_mem_k'])[0], f(inputs['cache_mem_v'])[0], f(inputs['mem_prompt'])
    in_maps = []
    for i in range(N_CORES):
        sl = slice(16 * i, 16 * i + 16)
        m = dict(shared)
        m['x_src'] = np.concatenate([xp[i], xs[sl].reshape(NS, D)], axis=0)
        m['mem_prompt'] = mem[i]
        m['state_ssm'] = st_ssm[sl].reshape(16, 1024, 128)
        m['state_ssm_conv'] = st_conv[sl]
        m['state_pool'] = st_pool[sl]
        m['state_ffn_conv'] = st_ffn[sl]
        m['cache_mem_k'] = ck[sl]
        m['cache_mem_v'] = cv[sl]
        in_maps.append(m)
    res = run_bass_kernel_spmd(nc, in_maps, core_ids=list(range(N_CORES)))
    R = res.results
    g = lambda name: np.stack([np.asarray(R[i][name], dtype=np.float32) for i in range(N_CORES)], axis=0)
    y_prompt = g('y_prompt')
    y_sample = g('y_sample').reshape(128, 4, D)
    ssm_p = g('ssm_prompt').reshape(1, 8, 16, 64, 128)
    ssm_s = g('ssm_sample').reshape(1, 128, 16, 64, 128)
    conv_p = g('conv_prompt').reshape(1, 8, 3, 1536)
    conv_s = g('conv_sample').reshape(1, 128, 3, 1536)
    pool_p = g('pool_prompt').reshape(1, 8, 15, 1024)
    pool_s = g('pool_sample').reshape(1, 128, 15, 1024)
    ffn_p = g('ffn_prompt').reshape(1, 8, 2, 2 * DFF)
    ffn_s = g('ffn_sample').reshape(1, 128, 2, 2 * DFF)
    mk_p = g('mem_k_prompt').reshape(1, 8, 256, 4, 256)
    mv_p = g('mem_v_prompt').reshape(1, 8, 256, 4, 256)
    return (y_prompt, y_sample, ssm_p, ssm_s, conv_p, conv_s, pool_p, pool_s, ffn_p, ffn_s, mk_p, mv_p)
```

```python
import numpy as np
import concourse.bass as bass
import concourse.mybir as mybir
from concourse.bass_utils import run_bass_kernel_spmd
from contextlib import ExitStack

F32, BF16 = mybir.dt.float32, mybir.dt.bfloat16
AF = mybir.ActivationFunctionType
ALU = mybir.AluOpType
AX = mybir.AxisListType

SAME_ENGINE_SYNC = True
CHECK_CLOBBER = False
D = 1024
T = 2048
NS = 64
DFF = 2816
EPS = 1e-6


class Prog:
    ENG = ('tensor', 'vector', 'scalar', 'gpsimd', 'sync')

    def __init__(self, nc, es):
        self.nc, self.es = nc, es
        self.ops = {e: [] for e in self.ENG}
        self.sem, self.cnt = {}, {}
        self.phase, self.free, self.retired, self.nsem, self.semcls = 0, {'sw': [], 'hw': []}, set(), 0, {}
        for e in self.ENG:
            self._mksem('E_' + e)
        self.seen = {e: {} for e in self.ENG}
        self.lastw, self.readers = {}, {}
        self.nbuf = 0

    def _mksem(self, name, q=None):
        if name.startswith('D_'):
            name = name + '@%d' % self.phase
        if name not in self.sem:
            cls = 'sw' if q == 'gpsimd' else 'hw'
            if name.startswith('D_'):
                self.semcls[name] = cls
            if name.startswith('D_') and self.free[cls]:
                h, c = self.free[cls].pop()
                self.sem[name] = h
                self.cnt[name] = c
            else:
                self.nsem += 1
                self.sem[name] = self.es.enter_context(self.nc.semaphore('s%d' % self.nsem))
                self.cnt[name] = 0
        return name

    def end_phase(self):
        self.barrier()
        for name in list(self.sem):
            if name.startswith('D_') and name not in self.retired:
                self.retired.add(name)
                self.free[self.semcls[name]].append((self.sem[name], self.cnt[name]))
        self.phase += 1

    def dma_multi(self, q, pairs, keys, sem, **kw):
        s = self._mksem('D_' + str(sem), q)
        for (o, i) in pairs:
            self.cnt[s] += 16
            self.ops[q].append(([], (lambda e, o=o, i=i: e.dma_start(out=o, in_=i, **kw)), (s, 16)))
        tok = (s, self.cnt[s])
        for k in keys:
            self.lastw[k] = tok
            self.readers[k] = []

    def sb(self, shape, dt, name=None, es=None):
        self.nbuf += 1
        return (es or self.es).enter_context(self.nc.sbuf_tensor(name or f"b{self.nbuf}", list(shape), dt))

    def op(self, eng, fn, reads=(), writes=(), dma=None):
        need = {}

        def want(tok, kind):
            if tok is None:
                return
            s, v = tok
            if s == 'E_' + eng:
                if eng == 'tensor' or not SAME_ENGINE_SYNC:
                    return
            if self.seen[eng].get(s, 0) >= v:
                return
            if need.get(s, 0) < v:
                need[s] = v
        for k in reads:
            want(self.lastw.get(k), 'raw')
            if isinstance(k, tuple) and k[0] == 'ps':
                for r in self.readers.get(k, ()):
                    want(r, 'war')
        for k in writes:
            if CHECK_CLOBBER and isinstance(k, tuple) and k[0] == 'ps' and k in self.lastw and not self.readers.get(k):
                import traceback
                print("CLOBBER? unread PSUM", k, [f.lineno for f in traceback.extract_stack()[-6:-1]])
            want(self.lastw.get(k), 'waw')
            for r in self.readers.get(k, ()):
                want(r, 'war')
        for s, v in need.items():
            self.seen[eng][s] = v
        if dma is not None:
            s = self._mksem('D_' + str(dma), eng)
            self.cnt[s] += 16
            tok = (s, self.cnt[s])
            inc = (s, 16)
        else:
            s = 'E_' + eng
            self.cnt[s] += 1
            tok = (s, self.cnt[s])
            inc = (s, 1)
        for k in writes:
            self.lastw[k] = tok
            self.readers[k] = []
        for k in reads:
            self.readers.setdefault(k, []).append(tok)
        self.ops[eng].append((list(need.items()), fn, inc))
        return tok

    def V(self, fn, reads=(), writes=()):
        return self.op('vector', fn, reads, writes)

    def A(self, fn, reads=(), writes=()):
        return self.op('scalar', fn, reads, writes)

    def G(self, fn, reads=(), writes=()):
        return self.op('gpsimd', fn, reads, writes)

    def PE(self, fn, reads=(), writes=()):
        return self.op('tensor', fn, reads, writes)

    def dma(self, q, out, in_, reads=(), writes=(), sem=None, **kw):
        return self.op(q, lambda e: e.dma_start(out=out, in_=in_, **kw), reads, writes, dma=sem)

    def barrier(self):
        allc = [(s_, c_) for s_, c_ in self.cnt.items() if c_ > 0]
        for e in self.ENG:
            w = [(s_, c_) for s_, c_ in allc if self.seen[e].get(s_, 0) < c_]
            for s_, c_ in w:
                self.seen[e][s_] = c_
            self.ops[e].append((w, None, None))

    def emit(self):
        fin = [(s, c) for s, c in self.cnt.items() if c > 0 and s != 'E_sync']
        self.ops['sync'].append((fin, None, None))
        with self.nc.Block() as block:
            for e in self.ENG:
                def body(eng, e=e):
                    for waits, fn, inc in self.ops[e]:
                        for s, v in waits:
                            eng.wait_ge(self.sem[s], v)
                        if fn is None:
                            continue
                        ins = fn(eng)
                        ins.then_inc(self.sem[inc[0]], inc[1])
                getattr(block, e)(body)


def bcast(ap, axis, n):
    u = ap.unsqueeze(axis)
    shp = list(u.shape)
    shp[axis] = n
    return u.broadcast_to(shp)


class K:
    pass


CST_LAYOUT = [('ident', 128), ('mhalf', 1), ('ones', 128), ('rsp', 256), ('rss', 64), ('csc', 64), ('blk', 16),
              ('negc', 128), ('negs', 128), ('e2', 2048), ('expd', 1024)]
CST_SMALL = 128 + 1 + 128 + 256 + 64 + 64 + 16
CST_OFF = {}
_o = 0
for _n, _w in CST_LAYOUT:
    CST_OFF[_n] = (_o, _w)
    _o += _w
CST_W = _o


def make_consts():
    c = np.zeros((128, CST_W), np.float32)

    def put(name, arr):
        o, w = CST_OFF[name]
        c[:arr.shape[0], o:o + arr.shape[1]] = arr
    put('ident', np.eye(128, dtype=np.float32))
    put('mhalf', np.full((128, 1), -0.5, np.float32))
    put('ones', np.ones((128, 128), np.float32))
    s_ = np.arange(128)[:, None]
    l_ = np.arange(128)[None, :]
    put('negc', np.where(l_ >= s_, 0.0, -30000.0).astype(np.float32))
    put('negs', np.where((l_ >= s_) & (l_ // 4 == s_ // 4), 0.0, -30000.0).astype(np.float32))
    e2 = np.zeros((128, 16, 128), np.float32)
    for h in range(16):
        e2[h, h, :] = 1.0
        e2[32 + h, h, :] = 1.0
    put('e2', e2.reshape(128, 2048))
    rsp = np.ones((128, 256), np.float32)
    rsp[:, 0] = 0.0
    rsp[:, 128] = 0.0
    put('rsp', rsp)
    rss = np.ones((128, 64), np.float32)
    rss[:, 0::4] = 0.0
    put('rss', rss)
    csc = np.zeros((128, 4, 16), np.float32)
    for gi, w in enumerate((2, 4, 8, 16)):
        for t in range(16):
            csc[:, gi, t] = 1.0 / min(t + 1, w)
    put('csc', csc.reshape(128, 64))
    blk = np.zeros((128, 16), np.float32)
    for r in range(64):
        blk[r, r // 4] = 1.0
    put('blk', blk)
    expd = np.zeros((128, 8, 128), np.float32)
    for j in range(8):
        for m in range(128):
            expd[2 * j + m // 64, j, m] = 1.0
            expd[32 + 2 * j + m // 64, j, m] = 1.0
    put('expd', expd.reshape(128, 1024))
    return c


def setup_common(P, k, consts):
    nc = P.nc
    k.ps = P.es.enter_context(nc.psum_tensor("ps", [128, 4096], F32))
    k.psb = k.ps[:].bitcast(BF16)
    k.cst = P.sb([128, CST_SMALL], F32, "cst")
    P.dma('sync', k.cst[:], consts[:, 0:CST_SMALL], writes=['cst'], sem='cst')

    def cs(name, rows=128):
        o, w = CST_OFF[name]
        return k.cst[0:rows, o:o + w]
    k.cs = cs
    k.identf = cs('ident')
    k.mhalf = cs('mhalf')
    k.cstb = P.sb([128, 384], BF16, "cstb")
    k.junk = P.sb([128, 1024], BF16, 'junk')
    CP(P, 'vector', k.cstb[:, 0:128], cs('ident'), ['cst'], ['identb'])
    P.dma('gpsimd', k.cstb[:, 128:384], consts[:, CST_SMALL:CST_SMALL + 256], writes=['cstb'], sem='cstbn')
    k.identb = k.cstb[:, 0:128]
    k.negcb = k.cstb[:, 128:256]
    k.negsb = k.cstb[:, 256:384]


def bank(k, b, n=512, bf=False, nb=1):
    if bf:
        return k.psb[:, 1024 * b: 1024 * b + n]
    return k.ps[:, 512 * b: 512 * b + n]


def load_fm(P, k, rows_ap, R, out_ap, key, tag, es=None):
    t = P.sb([128, 128], F32, "lfm_" + tag, es)
    P.dma('sync', t[0:R, :], rows_ap, writes=['lfm_' + tag], sem='lfm_' + tag)
    pb = bank(k, 7)
    P.PE(lambda e: e.transpose(out=pb[:, 0:R], in_=t[0:R, :], identity=k.identf[0:R, 0:R]),
         reads=['lfm_' + tag, 'cst'], writes=[('ps', 7)])
    P.V(lambda e: e.tensor_copy(out=out_ap, in_=pb[:, 0:R]), reads=[('ps', 7)], writes=[key])


def TT(P, eng, out, in0, in1, op, reads, writes):
    return P.op(eng, lambda e: e.tensor_tensor(out=out, in0=in0, in1=in1, op=op), reads, writes)


def TS(P, eng, out, in0, s1, s2, op0, op1, reads, writes):
    if op1 is None:
        return P.op(eng, lambda e: e.tensor_scalar(out=out, in0=in0, scalar1=s1, scalar2=None, op0=op0), reads, writes)
    return P.op(eng, lambda e: e.tensor_scalar(out=out, in0=in0, scalar1=s1, scalar2=s2, op0=op0, op1=op1), reads, writes)


def STT(P, out, in0, scalar, in1, op0, op1, reads, writes):
    return P.op('vector', lambda e: e.scalar_tensor_tensor(out=out, in0=in0, scalar=scalar, in1=in1, op0=op0, op1=op1), reads, writes)


def ACTF(P, out, in_, func, reads, writes, bias=None, scale=None, accum=None):
    kw = {}
    if bias is not None:
        kw['bias'] = bias
    if scale is not None:
        kw['scale'] = scale
    if accum is not None:
        kw['accum_out'] = accum
    return P.op('scalar', lambda e: e.activation(out=out, in_=in_, func=func, **kw), reads, writes)


def CP(P, eng, out, in_, reads, writes):
    if eng == 'scalar':
        return P.op(eng, lambda e: e.activation(out=out, in_=in_, func=AF.Identity), reads, writes)
    return P.op(eng, lambda e: e.tensor_copy(out=out, in_=in_), reads, writes)


def MM(P, items, reads, writes):
    items = list(items)

    def f(e):
        for it in items:
            (o, l, r, st, sp) = it[:5]
            if len(it) > 5:
                ins = e.matmul(o, lhsT=l, rhs=r, start=st, stop=sp, tile_position=it[5])
            else:
                ins = e.matmul(o, lhsT=l, rhs=r, start=st, stop=sp)
        return ins
    return P.op('tensor', f, reads, writes)


def TR(P, items, reads, writes):
    items = list(items)

    def f(e):
        for (o, i, idn) in items:
            ins = e.transpose(out=o, in_=i, identity=idn)
        return ins
    return P.op('tensor', f, reads, writes)


def MEMSET(P, eng, ap, val, writes):
    return P.op(eng, lambda e: e.memset(ap, val), (), writes)


def rstd_op(P, k, x_ap, n, xkeys, st, col, scale=1.0 / D, extra=None):
    junk = k.junk
    ks = ('st', id(st), col)
    P.A(lambda e: e.activation(out=junk[0:n, 0:x_ap.shape[1]], in_=x_ap, func=AF.Square, accum_out=st[0:n, col:col + 1]),
        reads=xkeys, writes=['junk', ks])
    P.V(lambda e: e.tensor_scalar(out=st[0:n, col + 1:col + 2], in0=st[0:n, col:col + 1], scalar1=scale, scalar2=EPS,
                                  op0=ALU.mult, op1=ALU.add), reads=[ks], writes=[ks])
    P.G(lambda e: e.tensor_tensor(out=st[0:n, col + 2:col + 3], in0=st[0:n, col + 1:col + 2], in1=k.mhalf[0:n, :], op=ALU.pow),
        reads=[ks, 'cst'], writes=[ks])
    if extra is not None:
        P.V(lambda e: e.tensor_scalar(out=st[0:n, col + 2:col + 3], in0=st[0:n, col + 2:col + 3], scalar1=extra, scalar2=None,
                                      op0=ALU.mult), reads=[ks], writes=[ks])
    return st[0:n, col + 2:col + 3], ks


def norm_T(P, k, x_ap, n, xkeys, gain_fm, gkey, hT_ap, hTkey, st, col, hb, hbkey, tb):
    r, ks = rstd_op(P, k, x_ap, n, xkeys, st, col)
    P.V(lambda e: e.tensor_scalar(out=hb[0:n, :], in0=x_ap, scalar1=r, scalar2=None, op0=ALU.mult),
        reads=list(xkeys) + [ks], writes=[hbkey])
    pb = k.psb[:, 1024 * tb: 1024 * tb + 1024].rearrange("p (c t) -> p c t", c=8)

    def tr(e):
        for c in range(8):
            ins = e.transpose(out=pb[:, c, 0:n], in_=hb[0:n, c * 128:(c + 1) * 128], identity=k.identb[0:n, 0:n])
        return ins
    P.PE(tr, reads=[hbkey, 'identb'], writes=[('ps', tb)])
    P.V(lambda e: e.tensor_tensor(out=hT_ap, in0=pb[:, :, 0:n], in1=bcast(gain_fm, 2, n), op=ALU.mult),
        reads=[('ps', tb), gkey], writes=[hTkey])


def phase_C(P, k, es, io, x_src, do_prompt=True, do_sample=True):
    wup = P.sb([128, 8, 2 * DFF], BF16, "wup", es)
    wdn = P.sb([128, 22, D], BF16, "wdn", es)
    wup_d = io['w_up'].rearrange("(c p) n -> p c n", p=128)
    wdn_d = io['w_down'].rearrange("(c p) n -> p c n", p=128)
    NCB = 4
    cbw = DFF // NCB
    def load_wup_block(q):
        prs = []
        for br in range(2):
            c0 = br * DFF + q * cbw
            prs.append((wup[:, :, c0:c0 + cbw], wup_d[:, :, c0:c0 + cbw]))
        P.dma_multi('gpsimd', prs, [('wup', q)], ('wup', q), max_dma_last_dim=4096)

    def load_rest_weights():
        for q in range(1, NCB):
            load_wup_block(q)
        for gq in range(2):
            P.dma_multi('gpsimd', [(wdn[:, 11 * gq:11 * gq + 11, :], wdn_d[:, 11 * gq:11 * gq + 11, :])], [('wdn', c) for c in range(11 * gq, 11 * gq + 11)], ('wdn', gq), max_dma_last_dim=4096)
    load_wup_block(0)
    wupk = None
    gC = P.sb([128, 8], F32, "gC", es)
    load_fm(P, k, io['norm_ffn'].rearrange("(c p) -> c p", p=128), 8, gC[:, :], 'gC', 'gC', es)
    cw = P.sb([128, 3, 44], F32, "cwC", es)
    cb = P.sb([128, 44], F32, "cbC", es)
    cwd = io['ffn_conv_w'].rearrange("k (c p) -> k c p", p=128)
    for kk in range(3):
        load_fm(P, k, cwd[kk], 44, cw[:, kk, :], ('cwC', kk), 'cwC%d' % kk, es)
    load_fm(P, k, io['ffn_conv_b'].rearrange("(c p) -> c p", p=128), 44, cb[:, :], 'cbC', 'cbC', es)
    cwk = [('cwC', kk) for kk in range(3)] + ['cbC']
    fgb = P.sb([128, D], F32, "fgb", es)
    P.dma('sync', fgb[:], io['final_norm'].partition_broadcast(128), writes=['fgb'], sem='fgb')

    xt = [P.sb([128, D], F32, "xtC%d" % i, es) for i in range(2)]
    hb = [k.junk] * 2
    st = P.sb([128, 64], F32, "stC", es)
    aT = P.sb([128, 22, 512], BF16, "aTC", es)
    NR = 3
    B_ = {}

    def alloc_group(stack, ntok, nseq, tag):
        B_['hT'] = P.sb([128, 8, ntok], BF16, "hTC" + tag, stack)
        B_['accg'] = [P.sb([128, ntok], F32, "accg%d%s" % (i, tag), stack) for i in range(NR)]
        B_['accv'] = [P.sb([128, ntok], F32, "accv%d%s" % (i, tag), stack) for i in range(NR)]
        B_['th'] = [P.sb([128, ntok], F32, "th%d%s" % (i, tag), stack) for i in range(2)]
        halo = P.sb([128, 44, nseq, 2], F32, "halo" + tag, stack)
        fix = P.sb([128, 44, nseq, 2], F32, "fix" + tag, stack)
        return halo, fix
    x3 = [P.sb([128, D], F32, "x3C%d" % i, es) for i in range(1)] * 2
    stg = [P.sb([32, 512], F32, "stgC%d" % i, es) for i in range(2)]
    cnt = {'x': 0, 'st': 0, 'p': 0, 'o': 0}

    def run_group(tok0, nseq, L, halo, fix, y_dst, gname, do=('norm', 'loop', 'down'), js=None):
        ntok = nseq * L
        nsub = (ntok + 127) // 128
        hT, accg, accv, th = B_['hT'], B_['accg'], B_['accv'], B_['th']
        hk = [('halo' + gname, c) for c in range(44)]
        fk = 'fix' + gname
        if 'norm' in do:
            for j in (range(nsub) if js is None else js):
                n = min(128, ntok - j * 128)
                i = cnt['x'] % 2
                cnt['x'] += 1
                P.dma('sync', xt[i][0:n, :], x_src[tok0 + j * 128: tok0 + j * 128 + n, :], writes=[('xtC', i)], sem=('xtC', i))
                col = (cnt['st'] % 8) * 4
                cnt['st'] += 1
                norm_T(P, k, xt[i][0:n, :], n, [('xtC', i)], gC[:, :], 'gC', hT[:, :, j * 128: j * 128 + n], ('hTC', j),
                       st, col, hb[i], 'junk', 4)
        hTk = [('hTC', j) for j in range(nsub)]
        if 'loop' in do:
            w0 = bcast(cw[:, 0, :], 2, nseq)
            w1 = bcast(cw[:, 1, :], 2, nseq)
            P.G(lambda e: e.tensor_tensor(out=fix[:, :, :, 0], in0=halo[:, :, :, 0], in1=w0, op=ALU.mult), reads=hk + cwk, writes=[fk])
            P.G(lambda e: e.tensor_tensor(out=fix[:, :, :, 1], in0=halo[:, :, :, 1], in1=w1, op=ALU.mult), reads=hk + cwk, writes=[fk + 'b'])
            P.G(lambda e: e.tensor_tensor(out=fix[:, :, :, 0], in0=fix[:, :, :, 0], in1=fix[:, :, :, 1], op=ALU.add), reads=[fk, fk + 'b'], writes=[fk])
            P.G(lambda e: e.tensor_tensor(out=fix[:, :, :, 1], in0=halo[:, :, :, 1], in1=w0, op=ALU.mult), reads=hk + cwk + [fk], writes=[fk + 'b'])
            fks = [fk, fk + 'b']

            def v3(ap, a, b):
                return ap.rearrange("p (s l) -> p s l", s=nseq)[:, :, a:b]

            def stage1(jj):
                for br, acc, nm in ((0, accg, 'accg'), (1, accv, 'accv')):
                    c = br * 22 + jj
                    b = (cnt['p'] % 4)
                    cnt['p'] += 1
                    pb = bank(k, b, ntok)

                    def mm(e, c=c, pb=pb):
                        for kc in range(8):
                            ins = e.matmul(pb, lhsT=wup[:, kc, c * 128:(c + 1) * 128], rhs=hT[:, kc, 0:ntok], start=(kc == 0), stop=(kc == 7))
                        return ins
                    P.PE(mm, reads=[('wup', min(NCB - 1, (jj * 128) // cbw)), ('wup', min(NCB - 1, (jj * 128 + 127) // cbw))] + hTk, writes=[('ps', b)])
                    r = jj % NR
                    a_ap = acc[r][:, 0:ntok]
                    P.A(lambda e, c=c, pb=pb, a_ap=a_ap: e.activation(out=a_ap, in_=pb, func=AF.Identity, bias=cb[:, c:c + 1], scale=cw[:, 2, c:c + 1]),
                        reads=[('ps', b)] + cwk, writes=[(nm, r)])
                    P.V(lambda e, c=c, pb=pb, a_ap=a_ap: e.scalar_tensor_tensor(out=v3(a_ap, 1, L), in0=v3(pb, 0, L - 1), scalar=cw[:, 1, c:c + 1], in1=v3(a_ap, 1, L), op0=ALU.mult, op1=ALU.add),
                        reads=[('ps', b), (nm, r)] + cwk, writes=[(nm, r)])
                    P.V(lambda e, c=c, pb=pb, a_ap=a_ap: e.scalar_tensor_tensor(out=v3(a_ap, 2, L), in0=v3(pb, 0, L - 2), scalar=cw[:, 0, c:c + 1], in1=v3(a_ap, 2, L), op0=ALU.mult, op1=ALU.add),
                        reads=[('ps', b), (nm, r)] + cwk, writes=[(nm, r)])
                    P.G(lambda e, c=c, a_ap=a_ap: e.tensor_tensor(out=v3(a_ap, 0, 2), in0=v3(a_ap, 0, 2), in1=fix[:, c, :, :], op=ALU.add),
                        reads=[(nm, r)] + fks, writes=[(nm, r)])
                    P.V(lambda e, c=c, pb=pb: e.tensor_copy(out=halo[:, c, :, :], in_=v3(pb, L - 2, L)),
                        reads=[('ps', b)] + fks, writes=[hk[c]])

            def stage2(jj):
                r = jj % NR
                P.A(lambda e: e.activation(out=th[jj % 2][:, 0:ntok], in_=accg[r][:, 0:ntok], func=AF.Silu), reads=[('accg', r)], writes=[('th', jj % 2)])
                P.G(lambda e: e.tensor_tensor(out=aT[:, jj, 0:ntok], in0=th[jj % 2][:, 0:ntok], in1=accv[r][:, 0:ntok], op=ALU.mult),
                    reads=[('th', jj % 2), ('accv', r)], writes=[('aT', jj)])
            SK = 1
            for step in range(22 + SK):
                if step < 22:
                    stage1(step)
                if step >= SK:
                    stage2(step - SK)
        aTk = [('aT', jj) for jj in range(22)]
        wdk = [('wdn', c) for c in range(22)]
        if 'down' in do:
            for j in (range(nsub) if js is None else js):
                n = min(128, ntok - j * 128)
                i = cnt['x'] % 2
                cnt['x'] += 1
                P.dma('sync', xt[i][0:n, :], x_src[tok0 + j * 128: tok0 + j * 128 + n, :], writes=[('xtC', i)], sem=('xtC', i))
                o = cnt['o'] % 2
                cnt['o'] += 1
                for hf in range(2):
                    b = 5 + hf
                    pb = bank(k, b)[0:n, :]

                    def mm(e, pb=pb, hf=hf, n=n, j=j):
                        for jj in range(22):
                            ins = e.matmul(pb, lhsT=aT[:, jj, j * 128: j * 128 + n], rhs=wdn[:, jj, hf * 512:(hf + 1) * 512], start=(jj == 0), stop=(jj == 21))
                        return ins
                    P.PE(mm, reads=aTk + wdk, writes=[('ps', b)])
                    P.V(lambda e, pb=pb, hf=hf, n=n, i=i, o=o: e.tensor_tensor(out=x3[o][0:n, hf * 512:(hf + 1) * 512], in0=pb, in1=xt[i][0:n, hf * 512:(hf + 1) * 512], op=ALU.add),
                        reads=[('ps', b), ('xtC', i)], writes=[('x3', 0, hf)])
                col = (cnt['st'] % 8) * 4
                cnt['st'] += 1
                r, ks = rstd_op(P, k, x3[o][0:n, :], n, [('x3', 0, 0), ('x3', 0, 1)], st, col)
                P.V(lambda e, n=n, o=o, r=r: e.scalar_tensor_tensor(out=x3[o][0:n, :], in0=x3[o][0:n, :], scalar=r, in1=fgb[0:n, :], op0=ALU.mult, op1=ALU.mult),
                    reads=[('x3', 0, 0), ('x3', 0, 1), ks, 'fgb'], writes=[('x3', 0, 0), ('x3', 0, 1)])
                P.dma('scalar', y_dst[j * 128: j * 128 + n, :], x3[o][0:n, :], reads=[('x3', 0, 0), ('x3', 0, 1)], sem=('yout', o))
        return hk

    def state_out(halo, hk, nseq, dst):
        R = nseq * 2
        for q in range(11):
            pb = bank(k, 7)

            def tr(e, q=q, pb=pb):
                for u in range(4):
                    c = q * 4 + u
                    ins = e.transpose(out=pb[0:R, u * 128:(u + 1) * 128], in_=halo[:, c, :, :].rearrange("p s r -> p (s r)"), identity=k.identf)
                return ins
            P.PE(tr, reads=hk[q * 4:q * 4 + 4] + ['cst'], writes=[('ps', 7)])
            P.V(lambda e, q=q, pb=pb: e.tensor_copy(out=stg[q % 2][0:R, :], in_=pb[0:R, :]), reads=[('ps', 7)], writes=[('stgC', q % 2)])
            P.dma('sync', dst[:, q * 512:(q + 1) * 512], stg[q % 2][0:R, :], reads=[('stgC', q % 2)], sem=('stgC', q % 2))

    if do_prompt:
        with ExitStack() as esp:
            haloP, fixP = alloc_group(esp, 512, 1, 'P')
            MEMSET(P, 'vector', haloP[:], 0.0, [('haloP', c) for c in range(44)])
            NSUP = T // 512
            ga = lambda S: (S * 512, 1, 512, haloP, fixP, io['y_prompt'][S * 512:(S + 1) * 512, :], 'P')
            run_group(*ga(0), do=('norm',))
            load_rest_weights()
            for S in range(NSUP):
                hk = run_group(*ga(S), do=('loop',))
                if S + 1 < NSUP:
                    run_group(*ga(S + 1), do=('norm',))
                run_group(*ga(S), do=('down',))
            state_out(haloP, hk, 1, io['ffn_prompt'])
            P.barrier()
    if not do_prompt:
        load_rest_weights()
    if do_sample:
        with ExitStack() as ess:
            haloS, fixS = alloc_group(ess, NS, 16, 'S')
            hkS = [('haloS', c) for c in range(44)]
            sfv = io['state_ffn_conv'].rearrange("b r c -> (b r) c")
            for q in range(11):
                P.dma('sync', stg[q % 2][0:32, :], sfv[:, q * 512:(q + 1) * 512], writes=[('stgC', q % 2)], sem=('stgC', q % 2))
                pb = bank(k, 7)[:, 0:128].rearrange("p (u r) -> p u r", u=4)
                TR(P, [(pb[:, u, :], stg[q % 2][0:32, u * 128:(u + 1) * 128], k.identf[0:32, 0:32]) for u in range(4)], [('stgC', q % 2), 'cst'], [('ps', 7)])
                CP(P, 'vector', haloS[:, 4 * q:4 * q + 4, :, :].rearrange("p u s r -> p u (s r)"), pb, [('ps', 7)], hkS[4 * q:4 * q + 4])
            hk = run_group(T, 16, 4, haloS, fixS, io['y_sample'], 'S')
            state_out(haloS, hk, 16, io['ffn_sample'].rearrange("b r c -> (b r) c"))
            P.barrier()


def phase_B(P, k, es, io, x_src, x_dst, do_prompt=True, do_sample=True):
    wq = P.sb([128, 8, D], BF16, "wmq", es)
    wo = P.sb([128, 8, D], BF16, "wmo", es)

    def load_w(nm, w):
        wd = io[nm].rearrange("(c p) n -> p c n", p=128)
        for gq in range(2):
            P.dma_multi('gpsimd', [(w[:, 4 * gq:4 * gq + 4, :], wd[:, 4 * gq:4 * gq + 4, :])], [(nm, c) for c in range(4 * gq, 4 * gq + 4)], (nm, gq), max_dma_last_dim=4096)
    kk_ = lambda nm: [(nm, c) for c in range(8)]
    gM = P.sb([128, 8], F32, "gM", es)
    load_fm(P, k, io['norm_mem'].rearrange("(c p) -> c p", p=128), 8, gM[:, :], 'gM', 'gM', es)
    xt = [P.sb([128, D], F32, "xtB%d" % i, es) for i in range(2)]
    hb = k.junk
    st = P.sb([128, 64], F32, "stB", es)
    hT = P.sb([128, 8, 512], BF16, "hTB", es)
    qT = P.sb([128, 8, 512], BF16, "qTB", es)
    PT = P.sb([128, 8, 512], BF16, "PTB", es)
    oT = P.sb([128, 8, 512], BF16, "oTB", es)
    pe = P.sb([128, 4, 256], F32, "peB", es)
    pn = P.sb([128, 4, 256], BF16, "pnB", es)
    KT, Vb = [None], [None]
    sm = P.sb([128, 32], F32, "smB", es)
    ob = P.sb([128, D], BF16, "obB", es)
    x2 = P.sb([128, D], F32, "x2B", es)
    cnt = {'x': 0, 'st': 0}

    def load_norm(tok0, ntok, gain, gkey, src, tb=4):
        nsub = (ntok + 127) // 128
        for j in range(nsub):
            n = min(128, ntok - j * 128)
            i = cnt['x'] % 2
            cnt['x'] += 1
            P.dma('sync', xt[i][0:n, :], src[tok0 + j * 128: tok0 + j * 128 + n, :], writes=[('xtB', i)], sem=('xtB', i))
            col = (cnt['st'] % 8) * 4
            cnt['st'] += 1
            norm_T(P, k, xt[i][0:n, :], n, [('xtB', i)], gain, gkey, hT[:, :, j * 128: j * 128 + n], ('hTB', j), st, col, hb, 'junk', tb)
        return [('hTB', j) for j in range(nsub)]

    def softmax_rows(S_ap, n, skeys, pn_out=None, pn_key='pnB'):
        pn_ = pn if pn_out is None else pn_out
        P.V(lambda e: e.tensor_reduce(out=sm[0:n, 0:4], in_=S_ap, axis=AX.X, op=ALU.max), reads=skeys, writes=['smB'])
        P.V(lambda e: e.tensor_scalar(out=sm[0:n, 4:8], in0=sm[0:n, 0:4], scalar1=-1.0, scalar2=None, op0=ALU.mult), reads=['smB'], writes=['smB'])
        for h in range(4):
            P.A(lambda e, h=h: e.activation(out=pe[0:n, h, :], in_=S_ap[:, h, :], func=AF.Exp, bias=sm[0:n, 4 + h:5 + h], accum_out=sm[0:n, 8 + h:9 + h]),
                reads=skeys + ['smB'], writes=[('peB', h), ('smB', h)])
        smk = [('smB', h) for h in range(4)]
        P.V(lambda e: e.reciprocal(out=sm[0:n, 12:16], in_=sm[0:n, 8:12]), reads=smk, writes=['smB2'])
        P.V(lambda e: e.tensor_tensor(out=pn_[0:n, :, :], in0=pe[0:n, :, :], in1=bcast(sm[0:n, 12:16], 2, 256), op=ALU.mult),
            reads=[('peB', h) for h in range(4)] + ['smB2'], writes=[pn_key])

    def qproj(ntok, hTk):
        for c in range(8):
            b = c % 2
            pb = bank(k, b, ntok)

            def mm(e, c=c, pb=pb):
                for kc in range(8):
                    ins = e.matmul(pb, lhsT=wq[:, kc, c * 128:(c + 1) * 128], rhs=hT[:, kc, 0:ntok], start=(kc == 0), stop=(kc == 7))
                return ins
            P.PE(mm, reads=kk_('w_mq') + hTk, writes=[('ps', b)])
            P.A(lambda e, c=c, pb=pb: e.activation(out=qT[:, c, 0:ntok], in_=pb, func=AF.Identity, scale=1.0 / 16.0), reads=[('ps', b)], writes=[('qTB', c)])
        return [('qTB', c) for c in range(8)]

    def oproj(tok0, ntok, src, dst, oTk):
        nsub = (ntok + 127) // 128
        for j in range(nsub):
            n = min(128, ntok - j * 128)
            i = cnt['x'] % 2
            cnt['x'] += 1
            P.dma('sync', xt[i][0:n, :], src[tok0 + j * 128: tok0 + j * 128 + n, :], writes=[('xtB', i)], sem=('xtB', i))
            for hf in range(2):
                b = 2 + hf
                pb = bank(k, b)[0:n, :]

                def mm(e, pb=pb, hf=hf, n=n, j=j):
                    for c in range(8):
                        ins = e.matmul(pb, lhsT=oT[:, c, j * 128: j * 128 + n], rhs=wo[:, c, hf * 512:(hf + 1) * 512], start=(c == 0), stop=(c == 7))
                    return ins
                P.PE(mm, reads=oTk + kk_('w_mo'), writes=[('ps', b)])
                P.V(lambda e, pb=pb, hf=hf, n=n, i=i: e.tensor_tensor(out=x2[0:n, hf * 512:(hf + 1) * 512], in0=pb, in1=xt[i][0:n, hf * 512:(hf + 1) * 512], op=ALU.add),
                    reads=[('ps', b), ('xtB', i)], writes=[('x2B', hf)])
            P.dma('sync', dst[tok0 + j * 128: tok0 + j * 128 + n, :], x2[0:n, :], reads=[('x2B', 0), ('x2B', 1)], sem='x2B')

    BSTEP = 99
    if do_prompt and BSTEP >= 1:
        esp = ExitStack()
        esp.__enter__()
        wk = P.sb([128, 8, D], BF16, "wmk", esp)
        wv = P.sb([128, 8, D], BF16, "wmv", esp)
        gKV = P.sb([128, 8], F32, "gKV", esp)
        KT[0] = P.sb([128, 8, 256], BF16, "KTB0", esp)
        Vb[0] = P.sb([128, 2, D], BF16, "VbB0", esp)
        kvf = P.sb([128, D], F32, "kvf", esp)
        load_w('w_mk', wk)
        load_w('w_mv', wv)
        load_w('w_mq', wq)
        load_w('w_mo', wo)
        load_fm(P, k, io['norm_memkv'].rearrange("(c p) -> c p", p=128), 8, gKV[:, :], 'gKV', 'gKV', esp)
        if True:
            hmk = load_norm(0, 256, gKV[:, :], 'gKV', io['mem_prompt'])
            for nm, w, dst in (('w_mk', wk, io['mem_k_prompt']), ('w_mv', wv, io['mem_v_prompt'])):
                if BSTEP < 2:
                    break
                for mt in range(2):
                    for hf in range(2):
                        b = hf
                        pb = bank(k, b)

                        def mm(e, pb=pb, hf=hf, mt=mt, w=w):
                            for kc in range(8):
                                ins = e.matmul(pb, lhsT=hT[:, kc, mt * 128:(mt + 1) * 128], rhs=w[:, kc, hf * 512:(hf + 1) * 512], start=(kc == 0), stop=(kc == 7))
                            return ins
                        P.PE(mm, reads=kk_(nm) + hmk, writes=[('ps', b)])
                        P.A(lambda e, pb=pb, hf=hf: e.activation(out=kvf[:, hf * 512:(hf + 1) * 512], in_=pb, func=AF.Identity), reads=[('ps', b)], writes=[('kvf', hf)])
                        if nm == 'w_mv':
                            P.V(lambda e, pb=pb, hf=hf, mt=mt: e.tensor_copy(out=Vb[0][:, mt, hf * 512:(hf + 1) * 512], in_=pb), reads=[('ps', b)], writes=[('VbB', 0, mt, hf)])
                    P.dma('sync', dst[mt * 128:(mt + 1) * 128, :], kvf[:, :], reads=[('kvf', 0), ('kvf', 1)], sem='kvf')
            for c in range(8 if BSTEP >= 3 else 0):
                b = c % 2
                pb = bank(k, b, 256)

                def mm(e, c=c, pb=pb):
                    for kc in range(8):
                        ins = e.matmul(pb, lhsT=wk[:, kc, c * 128:(c + 1) * 128], rhs=hT[:, kc, 0:256], start=(kc == 0), stop=(kc == 7))
                    return ins
                P.PE(mm, reads=kk_('w_mk') + hmk, writes=[('ps', b)])
                P.V(lambda e, c=c, pb=pb: e.tensor_copy(out=KT[0][:, c, :], in_=pb), reads=[('ps', b)], writes=[('KTB', 0, c)])
            KTk = [('KTB', 0, c) for c in range(8)]
            Vk = [('VbB', 0, mt, hf) for mt in range(2) for hf in range(2)]
            pn2 = P.sb([128, 4, 256], BF16, "pnB2", esp)
            qT1 = P.sb([128, 8, 512], BF16, "qTB1", esp)
            PT1 = P.sb([128, 8, 512], BF16, "PTB1", esp)
            qTs, PTs = [qT, qT1], [PT, PT1]
            NSUP = T // 512

            def g_qproj(S):
                hTk = [('hTB', j) for j in range(4)]
                q_ = qTs[S % 2]
                for c in range(8):
                    b_ = c % 2
                    pb = bank(k, b_, 512)
                    MM(P, [(pb, wq[:, kc, c * 128:(c + 1) * 128], hT[:, kc, 0:512], kc == 0, kc == 7) for kc in range(8)], kk_('w_mq') + hTk, [('ps', b_)])
                    ACTF(P, q_[:, c, :], pb, AF.Identity, [('ps', b_)], [('qTB', S % 2, c)], scale=1.0 / 16.0)
                    yield

            def g_soft(S):
                q_, pt_ = qTs[S % 2], PTs[S % 2]
                qk = [('qTB', S % 2, c) for c in range(8)]
                pend = None

                def ptrans(j, pnj, pk):
                    tb = k.psb[:, 1024 * 2: 1024 * 3].rearrange("p (c t) -> p c t", c=8)
                    TR(P, [(tb[:, 2 * h + mc, :], pnj[:, h, mc * 128:(mc + 1) * 128], k.identb) for h in range(4) for mc in range(2)], [pk, 'identb'], [('ps', 2)])
                    CP(P, 'vector', pt_[:, :, j * 128:(j + 1) * 128], tb, [('ps', 2)], [('PTB', S % 2, j)])
                for j in range(4):
                    Sps = k.ps[:, 512 * 4: 512 * 6].rearrange("p (h m) -> p h m", h=4)
                    for h in range(4):
                        MM(P, [(Sps[:, h, :], q_[:, 2 * h + dc, j * 128:(j + 1) * 128], KT[0][:, 2 * h + dc, :], dc == 0, dc == 1) for dc in range(2)],
                           qk + KTk, [('ps', 4 + h // 2)])
                    yield
                    pnj = pn if j % 2 == 0 else pn2
                    pk = 'pnB' if j % 2 == 0 else 'pnB2'
                    softmax_rows(Sps, 128, [('ps', 4), ('ps', 5)], pn_out=pnj, pn_key=pk)
                    yield
                    if pend is not None:
                        ptrans(*pend)
                        yield
                    pend = (j, pnj, pk)
                ptrans(*pend)
                yield

            def g_out(S):
                pt_ = PTs[S % 2]
                PTk = [('PTB', S % 2, j) for j in range(4)]
                for c in range(8):
                    h, dc = c // 2, c % 2
                    pb = bank(k, 3)
                    MM(P, [(pb, Vb[0][:, mc, h * 256 + dc * 128: h * 256 + dc * 128 + 128], pt_[:, 2 * h + mc, :], mc == 0, mc == 1) for mc in range(2)], PTk + Vk, [('ps', 3)])
                    CP(P, 'scalar', oT[:, c, :], pb, [('ps', 3)], [('oTB', c)])
                    if c % 2 == 1:
                        yield
                oTk = [('oTB', c) for c in range(8)]
                for j in range(4):
                    i = cnt['x'] % 2
                    cnt['x'] += 1
                    P.dma('sync', xt[i][:, :], x_src[S * 512 + j * 128: S * 512 + (j + 1) * 128, :], writes=[('xtB', i)], sem=('xtB', i))
                    for hf in range(2):
                        pb = bank(k, 6 + hf)
                        MM(P, [(pb, oT[:, c, j * 128:(j + 1) * 128], wo[:, c, hf * 512:(hf + 1) * 512], c == 0, c == 7) for c in range(8)], oTk + kk_('w_mo'), [('ps', 6 + hf)])
                        TT(P, 'vector', x2[:, hf * 512:(hf + 1) * 512], pb, xt[i][:, hf * 512:(hf + 1) * 512], ALU.add, [('ps', 6 + hf), ('xtB', i)], [('x2B', hf)])
                    P.dma('scalar', x_dst[S * 512 + j * 128: S * 512 + (j + 1) * 128, :], x2[:, :], reads=[('x2B', 0), ('x2B', 1)], sem='x2B')
                    yield

            def run_all(gs):
                gs = [g for g in gs if g is not None]
                while gs:
                    for g in list(gs):
                        try:
                            next(g)
                        except StopIteration:
                            gs.remove(g)
            load_norm(0, 512, gM[:, :], 'gM', x_src, tb=2)
            run_all([g_qproj(0)])
            for S in range(NSUP):
                nxt = None
                if S + 1 < NSUP:
                    load_norm((S + 1) * 512, 512, gM[:, :], 'gM', x_src, tb=2)
                    nxt = g_qproj(S + 1)
                run_all([g_soft(S), g_out(S - 1) if S > 0 else None, nxt])
            run_all([g_out(NSUP - 1)])
        P.barrier()
        esp.__exit__(None, None, None)
    else:
        load_w('w_mq', wq)
        load_w('w_mo', wo)
    if do_sample:
        with ExitStack() as ess:
            NG = 4
            Kb4 = [P.sb([128, NG, 2, D], BF16, "Kb4_%d" % i, ess) for i in range(2)]
            Vb4 = [P.sb([128, NG, 2, D], BF16, "Vb4_%d" % i, ess) for i in range(2)]
            KT4 = P.sb([128, NG, 8, 256], BF16, "KT4", ess)
            hTk = load_norm(T, NS, gM[:, :], 'gM', x_src)
            qTk = qproj(NS, hTk)
            ck = io['cache_mem_k'].rearrange("b (mt p) h d -> b p mt (h d)", p=128)
            cv = io['cache_mem_v'].rearrange("b (mt p) h d -> b p mt (h d)", p=128)
            MEMSET(P, 'vector', k.ps[:, 512 * 2: 512 * 6], 0.0, [('ps', 2), ('ps', 3), ('ps', 4), ('ps', 5)])
            CP(P, 'vector', pn[:, :, :], k.ps[:, 512 * 4: 512 * 6].rearrange("p (h m) -> p h m", h=4), [('ps', 4), ('ps', 5)], ['pnB'])
            CP(P, 'vector', ob[:, :], k.ps[:, 512 * 2: 512 * 4], [('ps', 2), ('ps', 3)], ['obB'])

            def loads(g):
                r = g % 2
                for q in range(NG):
                    bq = NG * g + q
                    for mt in range(2):
                        P.dma('gpsimd', Kb4[r][:, q, mt, :], ck[bq, :, mt, :], writes=[('Kb4', r, q, mt)], sem=('Kb4', r, q, mt), max_dma_last_dim=4096)
                        P.dma('gpsimd', Vb4[r][:, q, mt, :], cv[bq, :, mt, :], writes=[('Vb4', r, q, mt)], sem=('Vb4', r, q, mt), max_dma_last_dim=4096)
            loads(0)
            NR_ = 32 * (NG - 1) + 4
            for g in range(16 // NG):
                r = g % 2
                if g + 1 < 16 // NG:
                    loads(g + 1)
                for q in range(NG):
                    for half in range(2):
                        tb = k.psb[:, 1024 * (6 + half): 1024 * (7 + half)].rearrange("p (c m) -> p c m", c=4)
                        TR(P, [(tb[:, cc, mt * 128:(mt + 1) * 128], Kb4[r][:, q, mt, (half * 4 + cc) * 128:(half * 4 + cc + 1) * 128], k.identb) for cc in range(4) for mt in range(2)],
                           [('Kb4', r, q, 0), ('Kb4', r, q, 1), 'identb'], [('ps', 6 + half)])
                        CP(P, 'vector' if half == 0 else 'scalar', KT4[:, q, half * 4:(half + 1) * 4, :], tb, [('ps', 6 + half)], [('KT4', q, half)])
                for q in range(NG):
                    bq = NG * g + q
                    Sq = k.ps[32 * q:32 * q + 4, 512 * 4: 512 * 6].rearrange("p (h m) -> p h m", h=4)
                    for h in range(4):
                        MM(P, [(Sq[:, h, :], qT[:, 2 * h + dc, bq * 4:(bq + 1) * 4], KT4[:, q, 2 * h + dc, :], dc == 0, dc == 1, (0, 32 * q)) for dc in range(2)],
                           qTk + [('KT4', q, 0), ('KT4', q, 1)], [('ps', 4 + h // 2)])
                Sps = k.ps[0:NR_, 512 * 4: 512 * 6].rearrange("p (h m) -> p h m", h=4)
                softmax_rows(Sps, NR_, [('ps', 4), ('ps', 5)])
                tb = k.psb[:, 1024 * 0: 1024 * 1].rearrange("p (c t) -> p c t", c=8)
                TR(P, [(tb[:, 2 * h + mc, 0:NR_], pn[0:NR_, h, mc * 128:(mc + 1) * 128], k.identb[0:NR_, 0:NR_]) for h in range(4) for mc in range(2)], ['pnB', 'identb'], [('ps', 0)])
                CP(P, 'vector', PT[:, :, 0:NR_], tb[:, :, 0:NR_], [('ps', 0)], [('PTB', 0)])
                for q in range(NG):
                    oq = k.ps[32 * q:32 * q + 4, 512 * 2: 512 * 4].rearrange("p (h d) -> p h d", h=4)
                    for h in range(4):
                        MM(P, [(oq[:, h, :], PT[:, 2 * h + mc, 32 * q:32 * q + 4], Vb4[r][:, q, mc, h * 256:(h + 1) * 256], mc == 0, mc == 1, (0, 32 * q)) for mc in range(2)],
                           [('PTB', 0), ('Vb4', r, q, 0), ('Vb4', r, q, 1)], [('ps', 2 + h // 2)])
                CP(P, 'scalar', ob[0:NR_, :], k.ps[0:NR_, 512 * 2: 512 * 4], [('ps', 2), ('ps', 3)], ['obB'])
                tb2 = k.psb[:, 1024 * 1: 1024 * 2].rearrange("p (c t) -> p c t", c=8)
                TR(P, [(tb2[:, c, 0:NR_], ob[0:NR_, c * 128:(c + 1) * 128], k.identb[0:NR_, 0:NR_]) for c in range(8)], ['obB', 'identb'], [('ps', 1)])
                CP(P, 'vector', oT[:, :, 4 * NG * g:4 * NG * (g + 1)].rearrange("p c (q t) -> p c q t", t=4),
                   tb2[:, :, :].rearrange("p c (q u) -> p c q u", u=32)[:, :, 0:NG, 0:4], [('ps', 1)], [('oTB', 'c')])
            oproj(T, NS, x_src, x_dst, [('oTB', 'c')])


XBC0 = 1024
DT0 = 2560
VP0 = 2576
DIN = 3600
ST_A = 256


def phase_A(P, k, es, io, x_src, x_dst, do_prompt=True, do_sample=True):
    win = P.sb([128, 8, DIN], BF16, "win", es)
    wdt3 = P.sb([128, 8, 96], BF16, "wdt3", es)
    wout = P.sb([128, 16, D], BF16, "wout", es)
    wpool = P.sb([128, 4, 2, 256], BF16, "wpool", es)
    cb2 = P.sb([128, 2048], BF16, "cstb2", es)
    P.dma('gpsimd', cb2[:], io['consts'][:, CST_SMALL + 256:CST_SMALL + 256 + 2048], writes=['cstb'], sem='cstb2', max_dma_last_dim=4096)
    k.e2b = cb2[:, 0:2048]
    win_d = io['w_in'].rearrange("(c p) n -> p c n", p=128)
    MEMSET(P, 'vector', wdt3[:], 0.0, ['wdt3'])
    for nm_, c0, c1 in (('xbc', XBC0, XBC0 + 768), ('xbc2', XBC0 + 768, DT0 + 16), ('z', 0, 1024), ('vp', VP0, DIN)):
        P.dma_multi('gpsimd', [(win[:, :, c0:c1], win_d[:, :, c0:c1])], [('win', nm_)], ('win', nm_), max_dma_last_dim=4096)
    for r in range(3):
        P.dma('gpsimd', wdt3[:, :, 32 * r:32 * r + 16], win_d[:, :, DT0:DT0 + 16], reads=['wdt3'], writes=[('wdt3', r)], sem=('wdt3', r))
    wout_d = io['w_out'].rearrange("(c p) n -> p c n", p=128)
    for gq in range(2):
        P.dma_multi('gpsimd', [(wout[:, 8 * gq:8 * gq + 8, :], wout_d[:, 8 * gq:8 * gq + 8, :])], [('wout', c) for c in range(8 * gq, 8 * gq + 8)], ('wout', gq), max_dma_last_dim=4096)
    wp_d = io['w_pool'].rearrange("g (cc p) d -> p g cc d", p=128)
    P.dma_multi('gpsimd', [(wpool[:, 0:2, :, :], wp_d[:, 0:2, :, :]), (wpool[:, 2:4, :, :], wp_d[:, 2:4, :, :])], [('wpool', g) for g in range(4)], 'wpool')
    wink = [('win', nm_) for nm_ in ('xbc', 'xbc2', 'z', 'vp')]
    wdtk = [('wdt3', r) for r in range(3)]
    woutk = [('wout', c) for c in range(16)]
    wpk = [('wpool', g) for g in range(4)]

    gA = P.sb([128, 8], F32, "gA", es)
    gY = P.sb([128, 8], F32, "gY", es)
    psc = P.sb([128, 8], F32, "psc", es)
    load_fm(P, k, io['norm_mix'].rearrange("(c p) -> c p", p=128), 8, gA[:, :], 'gA', 'gA', es)
    load_fm(P, k, io['ssm_norm'].rearrange("(c p) -> c p", p=128), 8, gY[:, :], 'gY', 'gY', es)
    load_fm(P, k, io['pool_scale'].rearrange("(c p) -> c p", p=128), 8, psc[:, :], 'psc', 'psc', es)
    cw = P.sb([128, 4, 12], F32, "cwA", es)
    cb = P.sb([128, 12], F32, "cbA", es)
    cwd = io['ssm_conv_w'].rearrange("k (c p) -> k c p", p=128)
    for kk in range(4):
        load_fm(P, k, cwd[kk], 12, cw[:, kk, :], ('cwA', kk), 'cwA%d' % kk, es)
    load_fm(P, k, io['ssm_conv_b'].rearrange("(c p) -> c p", p=128), 12, cb[:, :], 'cbA', 'cbA', es)
    cwk = [('cwA', kk) for kk in range(4)] + ['cbA']
    TS(P, 'vector', cw[:], cw[:], 0.5, None, ALU.mult, None, cwk, cwk[:4])
    TS(P, 'vector', cb[:], cb[:], 0.5, None, ALU.mult, None, ['cbA'], ['cbA'])
    hp = P.sb([128, 4], F32, "hpA", es)
    MEMSET(P, 'vector', hp[:], 0.0, ['hpA'])
    for r in range(3):
        P.dma('sync', hp[32 * r:32 * r + 16, 0:1], io['ssm_dt_bias'].rearrange("(h o) -> h o", o=1), reads=['hpA'], writes=[('hpA', r)], sem=('hpA', r))
        P.dma('sync', hp[32 * r:32 * r + 16, 1:2], io['ssm_a_log'].rearrange("(h o) -> h o", o=1), reads=['hpA'], writes=[('hpA', r)], sem=('hpA', r))
    hpk = [('hpA', r) for r in range(3)]
    ACTF(P, hp[0:96, 2:3], hp[0:96, 1:2], AF.Exp, hpk, ['hpA2'])
    TS(P, 'vector', hp[0:96, 2:3], hp[0:96, 2:3], -1.0, None, ALU.mult, None, ['hpA2'], ['hpA2'])
    hb16 = P.sb([128, 48], F32, "hb16", es)
    P.dma('sync', hb16[:, 0:16], io['ssm_d'].partition_broadcast(128), writes=['hb16d'], sem='hb16d')
    P.dma('sync', hb16[:, 16:32], io['ssm_a_log'].partition_broadcast(128), writes=['hb16a'], sem='hb16a')
    ACTF(P, hb16[:, 32:48], hb16[:, 16:32], AF.Exp, ['hb16a'], ['hb16A'])
    TS(P, 'vector', hb16[:, 32:48], hb16[:, 32:48], -1.0, None, ALU.mult, None, ['hb16A'], ['hb16A'])
    Dbc = hb16[:, 0:16]
    Abc = hb16[:, 32:48]
    onesf = k.cs('ones')

    NSUB = ST_A // 128
    xt = [P.sb([128, D], F32, "xtA%d" % i, es) for i in range(2)]
    hb = k.junk
    st = P.sb([128, 64], F32, "stA", es)
    hT = P.sb([128, 8, ST_A], BF16, "hTA", es)
    xcT2 = [P.sb([128, 12, ST_A], BF16, "xcT0", es), None]
    acc = [P.sb([128, ST_A], F32, "accA%d" % i, es) for i in range(2)]
    th = [P.sb([128, ST_A], F32, "thA0", es)] * 2
    sz2 = [P.sb([128, NSUB, D], BF16, "szA0", es), None]
    pl = P.sb([128, 4, ST_A], BF16, "plA", es)
    pmT2 = [P.sb([128, 8, ST_A], BF16, "pmT0", es), None]
    ynT = P.sb([128, 8, ST_A], BF16, "ynT", es)
    dt3 = P.sb([128, ST_A], F32, "dt3A", es)
    a3 = P.sb([128, ST_A], F32, "a3A", es)
    d1 = a3
    acs3 = P.sb([128, ST_A], F32, "acs3A", es)
    stk2 = [P.sb([128, ST_A], F32, "stkA0", es), None]
    hl2 = [P.sb([128, ST_A], BF16, "hlA0", es), None]
    ptmp = P.sb([128, 2, 320], F32, "ptmpA", es)
    zt = P.sb([128, 512], F32, "ztA", es)
    tk = P.sb([128, 128], F32, "tkA", es)
    sml = P.sb([128, 64], F32, "smlA", es)
    xtok = P.sb([128, D], BF16, "xtokA", es)
    xdt = P.sb([128, D], BF16, "xdtA", es)
    xdtE = P.sb([128, D], BF16, "xdtEA", es)
    Btok = P.sb([128, 256], BF16, "BtokA", es)
    dcy = [P.sb([128, 4, 128], F32, "dcyA%d" % i, es) for i in range(2)]
    MT = P.sb([128, 16, 128], BF16, "MTA", es)
    y1 = P.sb([128, D], F32, "y1A", es)
    y2 = P.sb([128, D], F32, "y2A", es)
    yn = P.sb([128, D], BF16, "ynA", es)
    hst = P.sb([128, D], F32, "hstA", es)
    hbf = P.sb([128, D], BF16, "hbfA", es)
    cnt = {'x': 0, 'st': 0, 'p': 0, 'a': 0, 'd': 0}
    MEMSET(P, 'vector', y1[:], 0.0, [('y1A', 0), ('y1A', 1)])
    MEMSET(P, 'vector', ptmp[:], 0.0, [('ptmpA', 0), ('ptmpA', 1)])

    def v3(ap, nseq, a, b):
        return ap.rearrange("p (s l) -> p s l", s=nseq)[:, :, a:b]

    def run_group(tok0, nseq, L, extx, extv, first, negb, rsp, h0_bf_key, gname, sample_fn=None, par=0):
        xcT, sz, pmT, stk, hl = xcT2[par], sz2[par], pmT2[par], stk2[par], hl2[par]
        KP = 'p%d' % par
        ntok = nseq * L
        nsub = (ntok + 127) // 128
        exk = [('extx' + gname, c) for c in range(12)]
        evk = [('extv' + gname, c) for c in range(8)]
        for j in range(nsub):
            n = min(128, ntok - j * 128)
            i = cnt['x'] % 2
            cnt['x'] += 1
            P.dma('sync', xt[i][0:n, :], x_src[tok0 + j * 128: tok0 + j * 128 + n, :], writes=[('xtA', i)], sem=('xtA', i))
            col = (cnt['st'] % 8) * 4
            cnt['st'] += 1
            norm_T(P, k, xt[i][0:n, :], n, [('xtA', i)], gA[:, :], 'gA', hT[:, :, j * 128: j * 128 + n], ('hTA', j), st, col, hb, 'junk', 2)
        hTk = [('hTA', j) for j in range(nsub)]

        def proj_fm(col0, width, lhs_w, wkeys):
            b = cnt['p'] % 2
            cnt['p'] += 1
            pb = k.ps[0:width, 512 * b: 512 * b + ntok]
            MM(P, [(pb, lhs_w[:, kc, col0:col0 + width], hT[:, kc, 0:ntok], kc == 0, kc == 7) for kc in range(8)], wkeys + hTk, [('ps', b)])
            return pb, ('ps', b)

        xck = [('xcT' + KP, c) for c in range(12)]

        def sec_xbc():
            halo, fixx = extx
            fk = ['fixx0' + gname, 'fixx1' + gname, 'fixx2' + gname, 'fixt' + gname]
            wv = [bcast(cw[:, kk, :], 2, nseq) for kk in range(4)]
            hh = [halo[:, :, :, r_] for r_ in range(3)]
            tmpf = fixx[:, :, :, 3]
            TT(P, 'gpsimd', fixx[:, :, :, 0], hh[0], wv[0], ALU.mult, exk + cwk, [fk[0]])
            TT(P, 'gpsimd', tmpf, hh[1], wv[1], ALU.mult, exk + cwk, [fk[3]])
            TT(P, 'gpsimd', fixx[:, :, :, 0], fixx[:, :, :, 0], tmpf, ALU.add, [fk[0], fk[3]], [fk[0]])
            TT(P, 'gpsimd', tmpf, hh[2], wv[2], ALU.mult, exk + cwk + [fk[0]], [fk[3]])
            TT(P, 'gpsimd', fixx[:, :, :, 0], fixx[:, :, :, 0], tmpf, ALU.add, [fk[0], fk[3]], [fk[0]])
            TT(P, 'gpsimd', fixx[:, :, :, 1], hh[1], wv[0], ALU.mult, exk + cwk, [fk[1]])
            TT(P, 'gpsimd', tmpf, hh[2], wv[1], ALU.mult, exk + cwk + [fk[0]], [fk[3]])
            TT(P, 'gpsimd', fixx[:, :, :, 1], fixx[:, :, :, 1], tmpf, ALU.add, [fk[1], fk[3]], [fk[1]])
            TT(P, 'gpsimd', fixx[:, :, :, 2], hh[2], wv[0], ALU.mult, exk + cwk + [fk[1]], [fk[2]])
            fks = fk[:3]

            def xbc_stage1(c):
                pb, pk = proj_fm(XBC0 + c * 128, 128, win, [('win', 'xbc' if c < 6 else 'xbc2')])
                r = c % 2
                a_ap = v3(acc[r][:, 0:ntok], nseq, 0, L)
                p3 = v3(pb, nseq, 0, L)
                ACTF(P, a_ap, p3, AF.Identity, [pk] + cwk, [('accA', r)], bias=cb[:, c:c + 1], scale=cw[:, 3, c:c + 1])
                for sh in (1, 2, 3):
                    STT(P, a_ap[:, :, sh:L], p3[:, :, 0:L - sh], cw[:, 3 - sh, c:c + 1], a_ap[:, :, sh:L], ALU.mult, ALU.add, [pk, ('accA', r)] + cwk, [('accA', r)])
                TT(P, 'vector', a_ap[:, :, 0:3], a_ap[:, :, 0:3], fixx[:, c, :, 0:3], ALU.add, [('accA', r)] + fks, [('accA', r)])
                CP(P, 'vector', halo[:, c, :, :], p3[:, :, L - 3:L], [pk] + fk, [exk[c]])

            def xbc_stage2(c):
                r = c % 2
                a_ap = v3(acc[r][:, 0:ntok], nseq, 0, L)
                t_ap = v3(th[c % 2][:, 0:ntok], nseq, 0, L)
                ACTF(P, t_ap, a_ap, AF.Tanh, [('accA', r)], [('thA', 0)])
                STT(P, v3(xcT[:, c, 0:ntok], nseq, 0, L), t_ap, 1.0, a_ap, ALU.add, ALU.mult, [('thA', 0), ('accA', r)], [('xcT' + KP, c)])
            for c in range(13):
                if c < 12:
                    xbc_stage1(c)
                if c >= 1:
                    xbc_stage2(c - 1)
                yield
            xck = [('xcT' + KP, c) for c in range(12)]

            pb, pk = proj_fm(0, 96, wdt3, wdtk)
            ACTF(P, d1[0:96, 0:ntok], pb, AF.Exp, [pk] + hpk, ['a3A'], bias=hp[0:96, 0:1])
            ACTF(P, dt3[0:96, 0:ntok], d1[0:96, 0:ntok], AF.Ln, ['a3A'], ['dt3A'], bias=1.0)
            TS(P, 'vector', a3[0:96, 0:ntok], dt3[0:96, 0:ntok], hp[0:96, 2:3], None, ALU.mult, None, ['dt3A', 'hpA2'], ['a3A'])
            P.V(lambda e: e.tensor_tensor_scan(out=acs3[0:96, 0:ntok], data0=rsp[0:96, 0:ntok], data1=a3[0:96, 0:ntok], initial=0.0, op0=ALU.mult, op1=ALU.add),
                reads=['a3A', 'cst'], writes=['acs3A'])
            CP(P, 'vector', stk[0:96, 0:ntok], dt3[0:96, 0:ntok], ['dt3A'], ['stkA' + KP])
            CP(P, 'vector', stk[32:48, 0:ntok], acs3[32:48, 0:ntok], ['acs3A', 'stkA' + KP], ['stkA' + KP])
            nch = ntok // min(L, 128)
            Lc = min(L, 128)
            a3v = acs3[64:80, 0:ntok].rearrange("p (s l) -> p s l", l=Lc)
            TT(P, 'vector', stk[64:80, 0:ntok].rearrange("p (s l) -> p s l", l=Lc), a3v, bcast(a3v[:, :, Lc - 1], 2, Lc), ALU.subtract, ['acs3A', 'stkA' + KP], ['stkA' + KP])
            CP(P, 'vector', hl[0:96, 0:ntok], acs3[0:96, 0:ntok], ['acs3A'], ['hlA' + KP])
            TT(P, 'vector', hl[32:48, 0:ntok], acs3[32:48, 0:ntok], hl[32:48, 0:ntok], ALU.subtract, ['acs3A', 'hlA' + KP], ['hlA' + KP])

            yield
            yield

        def sec_z():
            for j in range(nsub):
                n = min(128, ntok - j * 128)
                for hf in range(2):
                    b = cnt['p'] % 2
                    cnt['p'] += 1
                    pb = k.ps[0:n, 512 * b: 512 * b + 512]
                    MM(P, [(pb, hT[:, kc, j * 128: j * 128 + n], win[:, kc, hf * 512:(hf + 1) * 512], kc == 0, kc == 7) for kc in range(8)], [('win', 'z')] + hTk, [('ps', b)])
                    r = cnt['a'] % 2
                    cnt['a'] += 1
                    ACTF(P, zt[0:n, :], pb, AF.Tanh, [('ps', b)], ['ztA'], scale=0.5)
                    STT(P, sz[0:n, j, hf * 512:(hf + 1) * 512], zt[0:n, :], 1.0, pb, ALU.add, ALU.mult, ['ztA', ('ps', b)], [('szA' + KP, j, hf)])
                    yield

            yield

        def sec_pool_a():
            if nseq == 1 and not first:
                CP(P, 'vector', extv[:, :, 0, 0:15], extv[:, :, 0, L:L + 15], evk, evk)
            for c in range(8):
                pb, pk = proj_fm(VP0 + c * 128, 128, win, [('win', 'vp')])
                CP(P, 'scalar', extv[:, c, :, 15:L + 15], v3(pb, nseq, 0, L), [pk], [evk[c]])
                if c % 2 == 1:
                    yield
            yield

        def sec_pool():
            def wpool_chunk(co):
                g, dh = co // 2, co % 2
                b = cnt['p'] % 2
                cnt['p'] += 1
                pb = k.ps[:, 512 * b: 512 * b + ntok]
                MM(P, [(pb, wpool[:, g, cc, dh * 128:(dh + 1) * 128], pl[:, (2 * g + cc) % 4, 0:ntok], cc == 0, cc == 1) for cc in range(2)],
                   wpk + [('plA', (2 * g) % 4), ('plA', (2 * g + 1) % 4)], [('ps', b)])
                ACTF(P, pmT[:, co, 0:ntok], pb, AF.Identity, [('ps', b), 'psc'], [('pmT' + KP, co)], scale=psc[:, co:co + 1])

            for c in range(8):
                gi = c // 2
                w = 2 << gi
                cur = extv[:, c, :, :]
                tot = L + 15
                step = 1
                bufs = [ptmp[:, 0, 0:nseq * tot].rearrange("p (s l) -> p s l", s=nseq), ptmp[:, 1, 0:nseq * tot].rearrange("p (s l) -> p s l", s=nseq)]
                bkeys = [('ptmpA', 0), ('ptmpA', 1)]
                bi = 0
                ckeys = [evk[c]]
                while step < w:
                    o = bufs[bi]
                    TT(P, 'gpsimd', o[:, :, step:tot], cur[:, :, step:tot], cur[:, :, 0:tot - step], ALU.add, ckeys, [bkeys[bi]])
                    cur = o
                    ckeys = [bkeys[bi]]
                    bi ^= 1
                    step *= 2
                o = bufs[bi]
                TS(P, 'gpsimd', o[:, :, 15:tot], cur[:, :, 15:tot], 1.0 / w, 0.0, ALU.mult, ALU.add, ckeys, [bkeys[bi]])
                if first and nseq == 1:
                    TT(P, 'gpsimd', o[:, :, 15:31], cur[:, :, 15:31], k.cs('csc')[:, gi * 16:(gi + 1) * 16].unsqueeze(1), ALU.mult, ckeys + ['cst'], [bkeys[bi]])
                TT(P, 'gpsimd', v3(pl[:, c % 4, 0:ntok], nseq, 0, L), o[:, :, 15:tot], extv[:, c, :, 15:tot], ALU.subtract, [bkeys[bi], evk[c]], [('plA', c % 4)])
                yield
                if c % 2 == 1:
                    for co in (c - 1, c):
                        wpool_chunk(co)
                    yield
            yield

        yield from sec_xbc()
        yield from sec_z()
        yield 'POOLA'
        yield from sec_pool_a()
        pmk = [('pmT' + KP, c) for c in range(8)]
        gpb = sec_pool()

        def adv(nsteps=1):
            for _ in range(nsteps):
                try:
                    next(gpb)
                except StopIteration:
                    return

        yield 'SPLIT'
        for j in range(nsub):
            n = min(128, ntok - j * 128)
            js = slice(j * 128, j * 128 + n)
            xb = k.psb[0:n, 1024 * 2: 1024 * 2 + 1024].rearrange("p (c t) -> p c t", c=8)
            TR(P, [(xb[:, c, :], xcT[:, c, js], k.identb) for c in range(8)], xck + ['identb'], [('ps', 2)])
            sp = k.ps[0:n, 512 * 3: 512 * 3 + 96]
            TR(P, [(sp, stk[0:96, js], k.identf[0:96, 0:96])], ['stkA' + KP, 'cst'], [('ps', 3)])
            CP(P, 'vector', tk[0:n, 0:96], sp, [('ps', 3)], ['tkA'])
            bb = k.psb[0:n, 1024 * 3 + 256: 1024 * 3 + 512].rearrange("p (c t) -> p c t", c=2)
            TR(P, [(bb[:, g, :], xcT[:, 8 + g, js], k.identb) for g in range(2)], xck + ['identb'], [('ps', 3)])
            CP(P, 'vector', Btok[0:n, :].rearrange("p (c t) -> p c t", c=2), bb, [('ps', 3)], ['BtokA'])
            TS(P, 'vector', sml[0:n, 0:16], tk[0:n, 32:48], -1.0, None, ALU.mult, None, ['tkA'], ['nacs'])
            ACTF(P, sml[0:n, 16:32], tk[0:n, 32:48], AF.Exp, ['tkA'], ['eacs'])
            ACTF(P, sml[0:n, 32:48], tk[0:n, 64:80], AF.Exp, ['tkA'], ['dend'], scale=-1.0)
            TT(P, 'vector', tk[0:n, 96:112], tk[0:n, 0:16], Abc[0:n, :], ALU.mult, ['tkA', 'hb16A'], ['atok'])
            CP(P, 'scalar', xtok[0:n, :].rearrange("p (c t) -> p c t", c=8), xb, [('ps', 2)], ['xtokA'])
            TT(P, 'vector', xdt[0:n, :].rearrange("p (h q) -> p h q", h=16), k.psb[0:n, 1024 * 2: 1024 * 2 + 1024].rearrange("p (h q) -> p h q", h=16),
               bcast(tk[0:n, 0:16], 2, 64), ALU.mult, [('ps', 2), 'tkA'], ['xdtA'])
            TT(P, 'vector', xdtE[0:n, :].rearrange("p (h q) -> p h q", h=16), xdt[0:n, :].rearrange("p (h q) -> p h q", h=16),
               bcast(sml[0:n, 32:48], 2, 64), ALU.mult, ['xdtA', 'dend'], ['xdtEA'])
            adv(2)
            yield
            cbp = k.ps[0:n, 512 * 3: 512 * 3 + 2 * n].rearrange("p (g l) -> p g l", g=2)
            MM(P, [(cbp[:, g, :], xcT[:, 8 + g, js], xcT[:, 10 + g, js], True, True) for g in range(2)], xck, [('ps', 3)])
            for q in range(4):
                b = 4 + q % 2
                dp = k.ps[0:n, 512 * b: 512 * b + 4 * n].rearrange("p (h l) -> p h l", h=4)
                items = []
                for hh in range(4):
                    h = 4 * q + hh
                    items.append((dp[:, hh, :], k.e2b[0:64, h * 128: h * 128 + n], hl[0:64, js], True, False))
                    items.append((dp[:, hh, :], k.identb[0:n, 0:n], negb[0:n, 0:n], False, True))
                MM(P, items, ['hlA' + KP, 'cstb', 'identb'], [('ps', b)])
                r = cnt['d'] % 2
                cnt['d'] += 1
                for hh in range(4):
                    h = 4 * q + hh
                    ACTF(P, dcy[r][0:n, hh, 0:n], dp[:, hh, :], AF.Exp, [('ps', b), 'nacs'], [('dcyA', r, hh)], bias=sml[0:n, h:h + 1])
                g = q // 2
                TT(P, 'vector', MT[0:n, 4 * q:4 * q + 4, 0:n], dcy[r][0:n, :, 0:n], bcast(cbp[:, g, :], 1, 4), ALU.mult,
                   [('dcyA', r, hh) for hh in range(4)] + [('ps', 3)], [('MTA', q)])
                adv(2)
                yield
            MTk = [('MTA', q) for q in range(4)]
            yd = k.ps[0:n, 512 * 6: 512 * 8].rearrange("p (h q) -> p h q", h=16)
            for half in range(2):
                MM(P, [(yd[:, h, :], MT[0:n, h, 0:n], xdt[0:n, h * 64:(h + 1) * 64], True, True) for h in range(8 * half, 8 * half + 8)],
                   MTk + ['xdtA'], [('ps', 6 + half)])
            adv(2)
            yield
            yo = k.ps[0:n, 512 * 4: 512 * 6]
            if sample_fn is not None:
                sample_fn(MTk, xck, xcT, hl, 'hlA' + KP)
                h0_bf_key = 'sample'
                TT(P, 'vector', y1[0:n, :].rearrange("p (h q) -> p h q", h=16), yo.rearrange("p (h q) -> p h q", h=16), bcast(sml[0:n, 16:32], 2, 64), ALU.mult,
                   [('ps', 4), ('ps', 5), 'eacs'], [('y1A', 0), ('y1A', 1)])
            elif h0_bf_key is not None:
                for g in range(2):
                    MM(P, [(yo[:, g * 512:(g + 1) * 512], xcT[:, 10 + g, js], hbf[:, g * 512:(g + 1) * 512], True, True)], xck + [h0_bf_key], [('ps', 4 + g)])
                TT(P, 'vector', y1[0:n, :].rearrange("p (h q) -> p h q", h=16), yo.rearrange("p (h q) -> p h q", h=16), bcast(sml[0:n, 16:32], 2, 64), ALU.mult,
                   [('ps', 4), ('ps', 5), 'eacs'], [('y1A', 0), ('y1A', 1)])
            TT(P, 'gpsimd', y2[0:n, :].rearrange("p (h q) -> p h q", h=16), xtok[0:n, :].rearrange("p (h q) -> p h q", h=16), bcast(Dbc[0:n, :], 2, 64), ALU.mult,
               ['xtokA', 'hb16d'], [('y2A', 0), ('y2A', 1)])
            if h0_bf_key is not None:
                TT(P, 'vector', y1[0:n, :], y1[0:n, :], y2[0:n, :], ALU.add, [('y1A', 0), ('y1A', 1), ('y2A', 0), ('y2A', 1)], [('y1A', 0), ('y1A', 1)])
                ysrc, ysk = y1, [('y1A', 0), ('y1A', 1)]
            else:
                ysrc, ysk = y2, [('y2A', 0), ('y2A', 1)]
            TT(P, 'vector', y1[0:n, :], k.ps[0:n, 512 * 6: 512 * 8], ysrc[0:n, :], ALU.add, [('ps', 6), ('ps', 7)] + ysk, [('y1A', 0), ('y1A', 1)])
            TT(P, 'vector', y1[0:n, :], y1[0:n, :], sz[0:n, j, :], ALU.mult, [('y1A', 0), ('y1A', 1), ('szA' + KP, j, 0), ('szA' + KP, j, 1)], [('y1A', 0), ('y1A', 1)])
            adv(2)
            yield
            for g in range(2):
                col = (cnt['st'] % 8) * 4
                cnt['st'] += 1
                r_ap, ks = rstd_op(P, k, y1[0:n, g * 512:(g + 1) * 512], n, [('y1A', 0), ('y1A', 1)], st, col, scale=0.25 / 512, extra=0.5)
                TS(P, 'vector', yn[0:n, g * 512:(g + 1) * 512], y1[0:n, g * 512:(g + 1) * 512], r_ap, None, ALU.mult, None, [('y1A', 0), ('y1A', 1), ks], [('ynA', g)])
            yb = k.psb[:, 1024 * 2: 1024 * 2 + 1024].rearrange("p (c t) -> p c t", c=8)
            TR(P, [(yb[:, c, 0:n], yn[0:n, c * 128:(c + 1) * 128], k.identb[0:n, 0:n]) for c in range(8)], [('ynA', 0), ('ynA', 1), 'identb'], [('ps', 2)])
            TT(P, 'vector', ynT[:, :, js], yb[:, :, 0:n], bcast(gY[:, :], 2, n), ALU.mult, [('ps', 2), 'gY'], [('ynT', j)])
            adv(2)
            yield
            if nseq == 1:
                sp2 = k.ps[:, 512 * 6: 512 * 8]
                for g in range(2):
                    MM(P, [(sp2[:, g * 512:(g + 1) * 512], Btok[0:n, g * 128:(g + 1) * 128], xdtE[0:n, g * 512:(g + 1) * 512], True, True)], ['BtokA', 'xdtEA'], [('ps', 6 + g)])
                cdp = k.ps[:, 512 * 3 + 256: 512 * 3 + 272]
                MM(P, [(cdp, onesf[0:n, :], tk[0:n, 96:112], True, True)], ['cst', 'atok'], [('ps', 3)])
                ACTF(P, tk[:, 112:128], cdp, AF.Exp, [('ps', 3)], ['cdA'])
                if h0_bf_key is not None:
                    TT(P, 'vector', y2[:, :].rearrange("p (h q) -> p h q", h=16), hst[:, :].rearrange("p (h q) -> p h q", h=16), bcast(tk[:, 112:128], 2, 64), ALU.mult,
                       ['hstA', 'cdA'], [('y2A', 0), ('y2A', 1)])
                    TT(P, 'vector', hst[:, :], sp2, y2[:, :], ALU.add, [('ps', 6), ('ps', 7), ('y2A', 0), ('y2A', 1)], ['hstA'])
                else:
                    CP(P, 'vector', hst[:, :], sp2, [('ps', 6), ('ps', 7)], ['hstA'])
                CP(P, 'vector', hbf[:, :], hst[:, :], ['hstA'], ['hbfA'])
                h0_bf_key = 'hbfA'
            adv(2)
            yield
        ynk = [('ynT', j) for j in range(nsub)]
        adv(100)
        for j in range(nsub):
            n = min(128, ntok - j * 128)
            js = slice(j * 128, j * 128 + n)
            i = cnt['x'] % 2
            cnt['x'] += 1
            P.dma('sync', xt[i][0:n, :], x_src[tok0 + j * 128: tok0 + j * 128 + n, :], writes=[('xtA', i)], sem=('xtA', i))
            for hf in range(2):
                b = cnt['p'] % 2
                cnt['p'] += 1
                pb = k.ps[0:n, 512 * b: 512 * b + 512]
                items = [(pb, ynT[:, c, js], wout[:, c, hf * 512:(hf + 1) * 512], c == 0, False) for c in range(8)]
                items += [(pb, pmT[:, c, js], wout[:, 8 + c, hf * 512:(hf + 1) * 512], False, c == 7) for c in range(8)]
                MM(P, items, ynk + pmk + woutk, [('ps', b)])
                TT(P, 'vector', xt[i][0:n, hf * 512:(hf + 1) * 512], pb, xt[i][0:n, hf * 512:(hf + 1) * 512], ALU.add, [('ps', b), ('xtA', i)], [('xtA', i)])
            P.dma('sync', x_dst[tok0 + j * 128: tok0 + j * 128 + n, :], xt[i][0:n, :], reads=[('xtA', i)], sem=('x1o', i))
            adv(2)
            yield

    def rows_out(src_fm, nrows_per, nchunks, skeys, dst_rows, tag):
        R = nrows_per
        for q in range(0, nchunks, 4):
            m = min(4, nchunks - q)
            pb = k.ps[0:R, 512 * 3: 512 * 3 + m * 128]
            for u in range(m):
                sap = src_fm(q + u)
                if len(sap.shape) > 2:
                    CP(P, 'vector', tk[:, 0:R].rearrange("p (a b) -> p a b", a=sap.shape[1]), sap, skeys, ['tkA'])
                    sap, sk2 = tk[:, 0:R], ['tkA']
                else:
                    sk2 = skeys
                TR(P, [(pb[:, u * 128:(u + 1) * 128], sap, k.identf)], sk2 + ['cst'], [('ps', 3)])
            CP(P, 'vector', y2[0:R, 0:m * 128], pb, [('ps', 3)], [('y2A', 0), ('y2A', 1)])
            P.dma('sync', dst_rows[:, q * 128:(q + m) * 128], y2[0:R, 0:m * 128], reads=[('y2A', 0), ('y2A', 1)], sem='rows' + tag)

    if do_prompt:
        with ExitStack() as esp:
            xcT2[1] = P.sb([128, 12, ST_A], BF16, "xcT1", esp)
            sz2[1] = P.sb([128, NSUB, D], BF16, "szA1", esp)
            pmT2[1] = P.sb([128, 8, ST_A], BF16, "pmT1", esp)
            stk2[1] = P.sb([128, ST_A], F32, "stkA1", esp)
            hl2[1] = P.sb([128, ST_A], BF16, "hlA1", esp)
            haloP = P.sb([128, 12, 1, 3], F32, "haloxP", esp)
            fixP = P.sb([128, 12, 1, 4], F32, "fixxP", esp)
            extx = (haloP, fixP)
            extv = P.sb([128, 8, 1, ST_A + 15], F32, "extvP", esp)
            MEMSET(P, 'vector', haloP[:], 0.0, [('extxP', c) for c in range(12)])
            MEMSET(P, 'vector', extv[:], 0.0, [('extvP', c) for c in range(8)])
            NSUP = T // ST_A
            RB, RF = 1, 1
            gens = [run_group(S * ST_A, 1, ST_A, extx, extv, S == 0, k.negcb, k.cs('rsp'), (None if S == 0 else 'hbfA'), 'P', par=S % 2) for S in range(NSUP)]

            hold = {}

            def step(g, front, back_alive=False):
                if front and hold.get(id(g)) and back_alive:
                    return True
                hold.pop(id(g), None)
                try:
                    v = next(g)
                except StopIteration:
                    return False
                if front and v == 'POOLA' and back_alive:
                    hold[id(g)] = True
                return not (front and v == 'SPLIT')
            while step(gens[0], True):
                pass
            for S in range(NSUP):
                gb = gens[S]
                gf = gens[S + 1] if S + 1 < NSUP else None
                ab, af = True, gf is not None
                while ab or af:
                    for _ in range(RB):
                        if ab:
                            ab = step(gb, False)
                    for _ in range(RF):
                        if af:
                            af = step(gf, True, ab)
            rows_out(lambda c: haloP[:, c, 0, :], 3, 12, [('extxP', c) for c in range(12)], io['conv_prompt'], 'cp')
            rows_out(lambda c: extv[:, c, 0, ST_A:ST_A + 15], 15, 8, [('extvP', c) for c in range(8)], io['pool_prompt'], 'pp')
            for half in range(2):
                pb = k.ps[:, 512 * (4 + half): 512 * (5 + half)]
                TR(P, [(pb[:, u * 128:(u + 1) * 128], hst[:, (4 * half + u) * 128:(4 * half + u + 1) * 128], k.identf) for u in range(4)], ['hstA', 'cst'], [('ps', 4 + half)])
                CP(P, 'vector', y1[:, half * 512:(half + 1) * 512], pb, [('ps', 4 + half)], [('y1A', half)])
            P.dma('sync', io['ssm_prompt'].rearrange("(c p) n -> p c n", p=128), y1[:, :].rearrange("p (c n) -> p c n", c=8), reads=[('y1A', 0), ('y1A', 1)], sem='ssmP')
            P.barrier()

    if do_sample:
        with ExitStack() as ess:
            haloS = P.sb([128, 12, 16, 3], F32, "haloxS", ess)
            fixS = P.sb([128, 12, 16, 4], F32, "fixxS", ess)
            extx = (haloS, fixS)
            extv = P.sb([128, 8, 16, 19], F32, "extvS", ess)
            h0a = P.sb([128, 8, 128], F32, "h0S", ess)
            cdT = P.sb([128, 8, 16], F32, "cdTS", ess)
            cb3 = P.sb([128, 1024], BF16, "cstb3", ess)
            P.dma('gpsimd', cb3[:], io['consts'][:, CST_SMALL + 256 + 2048:CST_SMALL + 256 + 3072], writes=['cstb3'], sem='cstb3', max_dma_last_dim=4096)
            k.expdb = cb3[:, :]
            exk = [('extxS', c) for c in range(12)]
            evk = [('extvS', c) for c in range(8)]
            y1k = [('y1A', 0), ('y1A', 1)]
            y2k = [('y2A', 0), ('y2A', 1)]
            scv = io['state_ssm_conv'].rearrange("b r c -> (b r) c")
            P.dma('sync', y1[0:48, :], scv[:, 0:1024], writes=y1k, sem='stS1')
            P.dma('sync', y2[0:48, 0:512], scv[:, 1024:1536], writes=y2k, sem='stS2')
            for c in range(12):
                src = y1[0:48, c * 128:(c + 1) * 128] if c < 8 else y2[0:48, (c - 8) * 128:(c - 7) * 128]
                bq_ = 3 - c % 2
                pb = k.ps[:, 512 * bq_: 512 * bq_ + 48]
                TR(P, [(pb, src, k.identf[0:48, 0:48])], y1k + y2k + ['cst'], [('ps', bq_)])
                CP(P, 'vector' if c % 2 == 0 else 'scalar', haloS[:, c, :, :], pb.rearrange("p (b r) -> p b r", r=3), [('ps', bq_)], [exk[c]])
            spv = io['state_pool'].rearrange("b r c -> (b r) c")
            for half in range(2):
                P.dma('sync', y1[0:120, :], spv[half * 120:(half + 1) * 120, :], reads=y1k, writes=y1k, sem=('stS3', half))
                for c in range(8):
                    bq_ = 3 - c % 2
                    pb = k.ps[:, 512 * bq_: 512 * bq_ + 120]
                    TR(P, [(pb, y1[0:120, c * 128:(c + 1) * 128], k.identf[0:120, 0:120])], y1k + ['cst'], [('ps', bq_)])
                    CP(P, 'vector' if c % 2 == 0 else 'scalar', extv[:, c, 8 * half:8 * half + 8, 0:15], pb.rearrange("p (b r) -> p b r", r=15), [('ps', bq_)], [evk[c]])
            ssd = io['state_ssm'].rearrange("b (c q) n -> b q c n", q=128)
            sso = io['ssm_sample'].rearrange("b (c q) n -> b q c n", q=128)

            def sample_fn(MTk, xck, xcT, hl, hlk):
                n = NS
                MEMSET(P, 'vector', MT[:], 0.0, MTk)
                ctm_diag = bass.AP(MT.tensor if hasattr(MT, 'tensor') else MT, 0, [[2048, 128], [1024, 2], [68, 16], [1, 4]])
                CP(P, 'vector', ctm_diag, xcT[:, 10:12, 0:64].rearrange("p g (b t) -> p g b t", t=4), xck + MTk, MTk)
                CTm = MT[:].rearrange("p h l -> p (h l)").rearrange("p (g b t) -> p g b t", g=2, b=16)
                cdp = k.ps[:, 512 * 3: 512 * 3 + 128].rearrange("p (c b) -> p c b", c=8)
                hl_last = hl[0:64, 0:64].rearrange("p (b t) -> p b t", t=4)[:, :, 3]
                MM(P, [(cdp[:, jc, :], k.expdb[0:64, jc * 128:(jc + 1) * 128], hl_last, True, True) for jc in range(8)], [hlk, 'cstb3'], [('ps', 3)])
                ACTF(P, cdT[:, :, :], cdp, AF.Exp, [('ps', 3)], ['cdTS'])
                for b in range(16):
                    h0 = h0a[:] if b % 2 == 0 else hst[:, :].rearrange("p (c n) -> p c n", c=8)
                    h0k = 'h0S' if b % 2 == 0 else 'hstA'
                    hn = (y1 if b % 2 == 0 else y2)[:, :].rearrange("p (c n) -> p c n", c=8)
                    hnk = y1k if b % 2 == 0 else y2k
                    if b == 0:
                        P.dma('sync', h0, ssd[0], writes=[h0k], sem=('h0S', 0))
                    if b + 1 < 16:
                        h0n = h0a[:] if (b + 1) % 2 == 0 else hst[:, :].rearrange("p (c n) -> p c n", c=8)
                        P.dma('sync', h0n, ssd[b + 1], writes=['h0S' if (b + 1) % 2 == 0 else 'hstA'], sem=('h0S', (b + 1) % 2))
                    tp = k.ps[:, 512 * 2: 512 * 4]
                    for half in range(2):
                        TR(P, [(tp[:, (4 * half + u) * 128:(4 * half + u + 1) * 128], h0[:, 4 * half + u, :], k.identf) for u in range(4)], [h0k, 'cst'], [('ps', 2 + half)])
                    CP(P, 'scalar', hbf[:, 0:512], tp[:, 0:512], [('ps', 2)], [('hbfA', 0)])
                    CP(P, 'vector', hbf[:, 512:1024], tp[:, 512:1024], [('ps', 3)], [('hbfA', 1)])
                    for g in range(2):
                        MM(P, [(k.ps[0:n, 512 * (4 + g): 512 * (5 + g)], CTm[:, g, b, :], hbf[:, g * 512:(g + 1) * 512], b == 0, b == 15)], MTk + [('hbfA', g)], [('ps', 4 + g)])
                    ACTF(P, xdt[0:n, :], xdtE[0:n, :], AF.Identity, ['xdtEA', 'cst'], ['xdtA'], scale=k.cs('blk')[0:n, b:b + 1])
                    sp = k.ps[:, 0:1024].rearrange("p (c n) -> p c n", c=8)
                    for half in range(2):
                        MM(P, [(sp[:, jc, :], xdt[0:n, jc * 128:(jc + 1) * 128], Btok[0:n, (jc // 4) * 128:(jc // 4 + 1) * 128], True, True) for jc in range(4 * half, 4 * half + 4)],
                           ['xdtA', 'BtokA'], [('ps', half)])
                    TT(P, 'gpsimd', hn, h0, bcast(cdT[:, :, b], 2, 128), ALU.mult, [h0k, 'cdTS'], hnk)
                    TT(P, 'vector', hn.rearrange("p c n -> p (c n)"), hn.rearrange("p c n -> p (c n)"), k.ps[:, 0:1024], ALU.add, hnk + [('ps', 0), ('ps', 1)], hnk)
                    P.dma('sync', sso[b], hn, reads=hnk, sem=('hnS', b % 2))

            for _ in run_group(T, 16, 4, extx, extv, False, k.negsb, k.cs('rss'), None, 'S', sample_fn=sample_fn, par=0):
                pass
            cso = io['conv_sample'].rearrange("b r c -> (b r) c")
            rows_out(lambda c: haloS[:, c, :, :], 48, 12, exk, cso, 'cs')
            pso = io['pool_sample'].rearrange("b r c -> (b r) c")
            for half in range(2):
                rows_out(lambda c, half=half: extv[:, c, 8 * half:8 * half + 8, 4:19], 120, 8, evk, pso[half * 120:(half + 1) * 120, :], 'ps%d' % half)
            P.barrier()


N_CORES = 8
_IN_SPECS = [
    ('x_src', [T + NS, D]), ('consts', [128, CST_W]), ('mem_prompt', [256, D]),
    ('state_ssm', [16, 1024, 128]), ('state_ssm_conv', [16, 3, 1536]), ('state_pool', [16, 15, 1024]),
    ('state_ffn_conv', [16, 2, 2 * DFF]), ('cache_mem_k', [16, 256, 4, 256]), ('cache_mem_v', [16, 256, 4, 256]),
    ('norm_mix', [D]), ('w_in', [D, DIN]), ('ssm_conv_w', [4, 1536]), ('ssm_conv_b', [1536]),
    ('ssm_dt_bias', [16]), ('ssm_a_log', [16]), ('ssm_d', [16]), ('ssm_norm', [D]),
    ('w_pool', [4, 256, 256]), ('pool_scale', [D]), ('w_out', [2 * D, D]), ('norm_mem', [D]), ('norm_memkv', [D]),
    ('w_mq', [D, D]), ('w_mk', [D, D]), ('w_mv', [D, D]), ('w_mo', [D, D]), ('norm_ffn', [D]),
    ('w_up', [D, 2 * DFF]), ('ffn_conv_w', [3, 2 * DFF]), ('ffn_conv_b', [2 * DFF]), ('w_down', [DFF, D]), ('final_norm', [D]),
]
_OUT_SPECS = [
    ('y_prompt', [T, D]), ('y_sample', [NS, D]), ('ssm_prompt', [1024, 128]), ('ssm_sample', [16, 1024, 128]),
    ('conv_prompt', [3, 1536]), ('conv_sample', [16, 3, 1536]), ('pool_prompt', [15, 1024]), ('pool_sample', [16, 15, 1024]),
    ('ffn_prompt', [2, 2 * DFF]), ('ffn_sample', [16, 2, 2 * DFF]), ('mem_k_prompt', [256, D]), ('mem_v_prompt', [256, D]),
]


def build_program():
    nc = bass.Bass("TRN2", target_bir_lowering=False)
    io = {}
    for name, shape in _IN_SPECS:
        io[name] = nc.dram_tensor(name, list(shape), F32, kind="ExternalInput").ap()
    for name, shape in _OUT_SPECS:
        io[name] = nc.dram_tensor(name, list(shape), F32, kind="ExternalOutput").ap()
    x1 = nc.dram_tensor("x1_scratch", [T + NS, D], F32, kind="Internal").ap()
    x2 = nc.dram_tensor("x2_scratch", [T + NS, D], F32, kind="Internal").ap()
    with ExitStack() as es:
        P = Prog(nc, es)
        k = K()
        setup_common(P, k, io['consts'])
        with ExitStack() as es2:
            phase_A(P, k, es2, io, io['x_src'], x1)
            P.end_phase()
        with ExitStack() as es2:
            phase_B(P, k, es2, io, x1, x2)
            P.end_phase()
        with ExitStack() as es2:
            phase_C(P, k, es2, io, x2)
            P.end_phase()
        P.emit()
    return nc


_PROG = {}


def kernel(**inputs):
    f = lambda a: np.ascontiguousarray(np.asarray(a, dtype=np.float32))
    if 'nc' not in _PROG:
        _PROG['nc'] = build_program()
    nc = _PROG['nc']
    consts = make_consts()
    xp, xs = f(inputs['x_prompt']), f(inputs['x_sample'])
    shared = {'consts': consts}
    for name in ['norm_mix', 'w_in', 'ssm_conv_w', 'ssm_conv_b', 'ssm_dt_bias', 'ssm_a_log', 'ssm_d', 'ssm_norm', 'w_pool',
                 'pool_scale', 'w_out', 'norm_mem', 'norm_memkv', 'w_mq', 'w_mk', 'w_mv', 'w_mo', 'norm_ffn', 'w_up',
                 'ffn_conv_w', 'ffn_conv_b', 'w_down']:
        shared[name] = f(inputs[name])[0]
    shared['final_norm'] = f(inputs['final_norm'])
    st_ssm, st_conv = f(inputs['state_ssm'])[0], f(inputs['state_ssm_conv'])[0]
    st_pool, st_ffn = f(inputs['state_pool'])[0], f(inputs['state_ffn_conv'])[0]
    ck, cv, mem = f(inputs['cache_mem_k'])[0], f(inputs['cache_mem_v'])[0], f(inputs['mem_prompt'])
    in_maps = []
    for i in range(N_CORES):
        sl = slice(16 * i, 16 * i + 16)
        m = dict(shared)
        m['x_src'] = np.concatenate([xp[i], xs[sl].reshape(NS, D)], axis=0)
        m['mem_prompt'] = mem[i]
        m['state_ssm'] = st_ssm[sl].reshape(16, 1024, 128)
        m['state_ssm_conv'] = st_conv[sl]
        m['state_pool'] = st_pool[sl]
        m['state_ffn_conv'] = st_ffn[sl]
        m['cache_mem_k'] = ck[sl]
        m['cache_mem_v'] = cv[sl]
        in_maps.append(m)
    res = run_bass_kernel_spmd(nc, in_maps, core_ids=list(range(N_CORES)))
    R = res.results
    g = lambda name: np.stack([np.asarray(R[i][name], dtype=np.float32) for i in range(N_CORES)], axis=0)
    y_prompt = g('y_prompt')
    y_sample = g('y_sample').reshape(128, 4, D)
    ssm_p = g('ssm_prompt').reshape(1, 8, 16, 64, 128)
    ssm_s = g('ssm_sample').reshape(1, 128, 16, 64, 128)
    conv_p = g('conv_prompt').reshape(1, 8, 3, 1536)
    conv_s = g('conv_sample').reshape(1, 128, 3, 1536)
    pool_p = g('pool_prompt').reshape(1, 8, 15, 1024)
    pool_s = g('pool_sample').reshape(1, 128, 15, 1024)
    ffn_p = g('ffn_prompt').reshape(1, 8, 2, 2 * DFF)
    ffn_s = g('ffn_sample').reshape(1, 128, 2, 2 * DFF)
    mk_p = g('mem_k_prompt').reshape(1, 8, 256, 4, 256)
    mv_p = g('mem_v_prompt').reshape(1, 8, 256, 4, 256)
    return (y_prompt, y_sample, ssm_p, ssm_s, conv_p, conv_s, pool_p, pool_s, ffn_p, ffn_s, mk_p, mv_p)
```
